# Optimizing a Trainium2 kernel written in Bass

```python
import jax
import jax.numpy as jnp
from jax import lax
import numpy as np


D_MODEL = 1024
BATCH = 4
SEQ = 4096
DEPTH = 4

GRID_W = 64
CTX_LEN = 256
EPS = 1e-6

CONV_W = D_MODEL
GLA_HEADS = 4
GLA_DK = D_MODEL // 2 // GLA_HEADS
GLA_DV = D_MODEL // GLA_HEADS
GLA_RANK = 16
GLA_NORMALIZER = 16.0
GLA_CHUNK = 64
MLA_HEADS = 8
MLA_NOPE = 128
MLA_ROPE = 64
MLA_DV = 128
MLA_QK = MLA_NOPE + MLA_ROPE
MLA_Q_RANK = 384
MLA_KV_RANK = 128
ROPE_BASE = 10000.0
Q_BLOCK = 128

IN_LAYOUT = (
    ('a_v', CONV_W), ('a_b', CONV_W), ('a_c', CONV_W), ('a_z', CONV_W),
    ('b_q', GLA_HEADS * GLA_DK), ('b_k', GLA_HEADS * GLA_DK), ('b_v', GLA_HEADS * GLA_DV),
    ('b_z', GLA_HEADS * GLA_DV), ('b_af', GLA_RANK), ('b_ab', GLA_RANK),
    ('c_q', MLA_Q_RANK), ('c_kv', MLA_KV_RANK), ('c_kr', MLA_ROPE), ('c_z', MLA_HEADS * MLA_DV),
    ('g_a', D_MODEL), ('g_b', D_MODEL), ('g_c', D_MODEL),
)
IN_DIM = sum(w for _, w in IN_LAYOUT)

kernel_name = 'hybrid_conv_gla_mla_prefix_trunk'


def _rmsnorm(x, g):
    xf = x.astype(jnp.float32)
    y = xf * lax.rsqrt(jnp.mean(xf * xf, axis=-1, keepdims=True) + EPS)
    return (y * g.astype(jnp.float32)).astype(x.dtype)


def _split_in(u):
    out = {}
    start = 0
    for name, width in IN_LAYOUT:
        out[name] = u[..., start:start + width]
        start += width
    return out


def _axial_rope_tables(n_tokens):
    rows = n_tokens // GRID_W
    row = jnp.repeat(jnp.arange(rows, dtype=jnp.float32), GRID_W)
    col = jnp.tile(jnp.arange(GRID_W, dtype=jnp.float32), rows)
    n_freq = MLA_ROPE // 4
    freqs = ROPE_BASE ** (-jnp.arange(n_freq, dtype=jnp.float32) / n_freq)
    ang_r = row[:, None] * freqs[None, :]
    ang_c = col[:, None] * freqs[None, :]
    ang = jnp.concatenate([ang_r, ang_r, ang_c, ang_c], axis=-1)
    return jnp.cos(ang), jnp.sin(ang)


def _apply_axial_rope(x, cos, sin):
    half = MLA_ROPE // 2
    quarter = MLA_ROPE // 4
    xf = x.astype(jnp.float32)

    def rot(blk):
        return jnp.concatenate([-blk[..., quarter:], blk[..., :quarter]], axis=-1)

    rotated = jnp.concatenate([rot(xf[..., :half]), rot(xf[..., half:])], axis=-1)
    return (xf * cos[:, None, :] + rotated * sin[:, None, :]).astype(x.dtype)


def _short_conv(u, w):
    up = jnp.pad(u, ((0, 0), (1, 1), (0, 0)))
    return up[:, :-2] * w[0] + up[:, 1:-1] * w[1] + up[:, 2:] * w[2]


def _short_conv_mixer(u, conv_w):
    y = u['a_b'] * _short_conv(u['a_c'] * u['a_v'], conv_w)
    return y * jax.nn.silu(u['a_z'])


def _gla_log_decay(h_low, w_up, b):
    z = (h_low @ w_up + b).astype(jnp.float32)
    return jax.nn.log_sigmoid(z) / GLA_NORMALIZER


def _gla_query(u):
    bsz, t = u['b_q'].shape[:2]
    return u['b_q'].reshape(bsz, t, GLA_HEADS, GLA_DK) * (GLA_DK ** -0.5)


def _gla_prepare(u, p):
    bsz, t = u['b_k'].shape[:2]
    k = u['b_k'].reshape(bsz, t, GLA_HEADS, GLA_DK)
    v = u['b_v'].reshape(bsz, t, GLA_HEADS, GLA_DV)
    la_f = _gla_log_decay(u['b_af'], p['gla_wa_up_f'], p['gla_ba_f']).reshape(bsz, t, GLA_HEADS, GLA_DK)
    la_b = _gla_log_decay(u['b_ab'], p['gla_wa_up_b'], p['gla_ba_b']).reshape(bsz, t, GLA_HEADS, GLA_DK)
    return k, v, la_f, la_b


def _gla_chunked(q, k, v, loga, s0):
    bsz, t, h, dk = q.shape
    dv = v.shape[-1]
    n = t // GLA_CHUNK
    shp = (bsz, n, GLA_CHUNK, h)
    qc = q.astype(jnp.float32).reshape(shp + (dk,))
    kc = k.astype(jnp.float32).reshape(shp + (dk,))
    vc = v.astype(jnp.float32).reshape(shp + (dv,))
    b = jnp.cumsum(loga.reshape(shp + (dk,)), axis=2)
    b_last = b[:, :, -1:]
    q_dec = qc * jnp.exp(b)
    k_inv = kc * jnp.exp(-b)
    k_end = kc * jnp.exp(b_last - b)
    mask = jnp.tril(jnp.ones((GLA_CHUNK, GLA_CHUNK), dtype=bool))
    att = jnp.where(mask, jnp.einsum('bnihd,bnjhd->bnhij', q_dec, k_inv), 0.0)
    o_intra = jnp.einsum('bnhij,bnjhe->bnihe', att, vc)
    u = jnp.einsum('bnjhd,bnjhe->nbhde', k_end, vc)
    g = jnp.exp(b_last[:, :, 0]).swapaxes(0, 1)

    def step(s, inp):
        g_n, u_n = inp
        return g_n[..., None] * s + u_n, s

    _, s_start = lax.scan(step, s0, (g, u))
    o_inter = jnp.einsum('bnihd,nbhde->bnihe', q_dec, s_start)
    return (o_intra + o_inter).reshape(bsz, t, h, dv).astype(v.dtype)


def _gla_final_state(k, v, loga):
    cum = jnp.cumsum(loga, axis=1)
    w = jnp.exp(cum[:, -1:] - cum)
    return jnp.einsum('bthd,bthe->bhde', k.astype(jnp.float32) * w, v.astype(jnp.float32))


def _gla_bidir(q, k, v, la_f, la_b, s_f, s_b):
    o_f = _gla_chunked(q, k, v, la_f, s_f)
    o_b = jnp.flip(_gla_chunked(jnp.flip(q, 1), jnp.flip(k, 1), jnp.flip(v, 1), jnp.flip(la_b, 1), s_b), 1)
    return o_f + o_b


def _gla_output(o, z, g):
    bsz, t = o.shape[:2]
    o = _rmsnorm(o.astype(z.dtype), g.reshape(GLA_HEADS, GLA_DV))
    return o.reshape(bsz, t, GLA_HEADS * GLA_DV) * jax.nn.silu(z)


def _mla_q(c_q, p, cos, sin):
    bsz, t = c_q.shape[:2]
    q = (_rmsnorm(c_q, p['mla_q_norm_g']) @ p['mla_wq_up']).reshape(bsz, t, MLA_HEADS, MLA_QK)
    q = _rmsnorm(q, p['mla_qn_g'])
    if cos is not None:
        q = jnp.concatenate([q[..., :MLA_NOPE], _apply_axial_rope(q[..., MLA_NOPE:], cos, sin)], axis=-1)
    return q


def _mla_kv(c_kv, c_kr, p, cos, sin):
    bsz, t = c_kv.shape[:2]
    kv = (_rmsnorm(c_kv, p['mla_kv_norm_g']) @ p['mla_wkv_up']).reshape(bsz, t, MLA_HEADS, MLA_NOPE + MLA_DV)
    k_nope, v = kv[..., :MLA_NOPE], kv[..., MLA_NOPE:]
    k_rope = jnp.broadcast_to(c_kr[:, :, None, :], (bsz, t, MLA_HEADS, MLA_ROPE))
    k = _rmsnorm(jnp.concatenate([k_nope, k_rope], axis=-1), p['mla_kn_g'])
    if cos is not None:
        k = jnp.concatenate([k[..., :MLA_NOPE], _apply_axial_rope(k[..., MLA_NOPE:], cos, sin)], axis=-1)
    return k, v


def _attend(q, k, v):
    s = jnp.einsum('bqhd,bkhd->bhqk', q, k).astype(jnp.float32) * (MLA_QK ** -0.5)
    pr = jax.nn.softmax(s, axis=-1).astype(v.dtype)
    return jnp.einsum('bhqk,bkhd->bqhd', pr, v)


def _attend_blocked(q, k, v):
    bsz, t, h, d = q.shape
    nb = t // Q_BLOCK
    qb = q.reshape(bsz, nb, Q_BLOCK, h, d).swapaxes(0, 1)
    ob = lax.map(lambda qi: _attend(qi, k, v), qb)
    return ob.swapaxes(0, 1).reshape(bsz, t, h, v.shape[-1])


def _merge_branches(y_a, y_b, y_c, u, p):
    m = (jax.nn.sigmoid(u['g_a']) * (y_a @ p['w_br_a'])
         + jax.nn.sigmoid(u['g_b']) * (y_b @ p['w_br_b'])
         + jax.nn.sigmoid(u['g_c']) * (y_c @ p['w_br_c']))
    return m @ p['w_out']


def _layer(x, ctx, c, c_ctx, p, cos, sin, update_ctx):
    mod_x = jax.nn.silu(c) @ p['w_mod'] + p['b_mod']
    mod_c = jax.nn.silu(c_ctx) @ p['w_mod'] + p['b_mod']
    shift_x, scale_x, gate_x = jnp.split(mod_x[:, None, :], 3, axis=-1)
    shift_c, scale_c, gate_c = jnp.split(mod_c, 3, axis=-1)
    hx = _rmsnorm(x, p['norm_g']) * (1.0 + scale_x) + shift_x
    hc = _rmsnorm(ctx, p['norm_g']) * (1.0 + scale_c) + shift_c
    ux = _split_in(hx @ p['w_in'])
    uc = _split_in(hc @ p['w_in'])

    kc_g, vc_g, laf_c, lab_c = _gla_prepare(uc, p)
    s_f = _gla_final_state(kc_g, vc_g, laf_c)
    s_b = _gla_final_state(jnp.flip(kc_g, 1), jnp.flip(vc_g, 1), jnp.flip(lab_c, 1))
    k_ctx, v_ctx = _mla_kv(uc['c_kv'], uc['c_kr'], p, None, None)

    bsz, t = x.shape[:2]
    y_a = _short_conv_mixer(ux, p['conv_w'])
    kx, vx, laf_x, lab_x = _gla_prepare(ux, p)
    y_b = _gla_output(_gla_bidir(_gla_query(ux), kx, vx, laf_x, lab_x, s_f, s_b), ux['b_z'], p['gla_norm_g'])
    q_lat = _mla_q(ux['c_q'], p, cos, sin)
    k_lat, v_lat = _mla_kv(ux['c_kv'], ux['c_kr'], p, cos, sin)
    o_c = _attend_blocked(q_lat, jnp.concatenate([k_lat, k_ctx], axis=1), jnp.concatenate([v_lat, v_ctx], axis=1))
    y_c = o_c.reshape(bsz, t, MLA_HEADS * MLA_DV) * jax.nn.silu(ux['c_z'])
    x_new = x + gate_x * _merge_branches(y_a, y_b, y_c, ux, p)

    if update_ctx:
        lc = ctx.shape[1]
        zeros = jnp.zeros_like(s_f)
        yc_a = _short_conv_mixer(uc, p['conv_w'])
        yc_b = _gla_output(_gla_bidir(_gla_query(uc), kc_g, vc_g, laf_c, lab_c, zeros, zeros), uc['b_z'], p['gla_norm_g'])
        q_ctx = _mla_q(uc['c_q'], p, None, None)
        yc_c = _attend(q_ctx, k_ctx, v_ctx).reshape(ctx.shape[0], lc, MLA_HEADS * MLA_DV) * jax.nn.silu(uc['c_z'])
        ctx = ctx + gate_c * _merge_branches(yc_a, yc_b, yc_c, uc, p)
    return x_new, ctx


def setup_inputs(seed: int = 0) -> dict:
    key = jax.random.key(seed)
    ks = jax.random.split(key, 24)
    f32 = jnp.float32

    def nrm(k, shape, s):
        return jax.random.normal(k, shape, f32) * s

    def gain(k, shape):
        return 1.0 + 0.02 * jax.random.normal(k, shape, f32)

    L, D = DEPTH, D_MODEL
    return {
        'x': nrm(ks[0], (BATCH, SEQ, D), 1.0),
        'c': nrm(ks[1], (BATCH, D), 1.0),
        'ctx': nrm(ks[2], (BATCH, CTX_LEN, D), 1.0),
        'c_ctx': nrm(ks[3], (D,), 1.0),
        'w_mod': nrm(ks[4], (L, D, 3 * D), 0.5 * D ** -0.5),
        'b_mod': nrm(ks[5], (L, 3 * D), 0.02),
        'norm_g': gain(ks[6], (L, D)),
        'w_in': nrm(ks[7], (L, D, IN_DIM), D ** -0.5),
        'conv_w': nrm(ks[8], (L, 3, CONV_W), 3 ** -0.5),
        'gla_wa_up_f': nrm(ks[9], (L, GLA_RANK, GLA_HEADS * GLA_DK), GLA_RANK ** -0.5),
        'gla_ba_f': nrm(ks[10], (L, GLA_HEADS * GLA_DK), 0.1),
        'gla_wa_up_b': nrm(ks[11], (L, GLA_RANK, GLA_HEADS * GLA_DK), GLA_RANK ** -0.5),
        'gla_ba_b': nrm(ks[12], (L, GLA_HEADS * GLA_DK), 0.1),
        'gla_norm_g': gain(ks[13], (L, GLA_HEADS * GLA_DV)),
        'mla_q_norm_g': gain(ks[14], (L, MLA_Q_RANK)),
        'mla_kv_norm_g': gain(ks[15], (L, MLA_KV_RANK)),
        'mla_wq_up': nrm(ks[16], (L, MLA_Q_RANK, MLA_HEADS * MLA_QK), MLA_Q_RANK ** -0.5),
        'mla_wkv_up': nrm(ks[17], (L, MLA_KV_RANK, MLA_HEADS * (MLA_NOPE + MLA_DV)), MLA_KV_RANK ** -0.5),
        'mla_qn_g': gain(ks[18], (L, MLA_QK)),
        'mla_kn_g': gain(ks[19], (L, MLA_QK)),
        'w_br_a': nrm(ks[20], (L, CONV_W, D), CONV_W ** -0.5),
        'w_br_b': nrm(ks[21], (L, GLA_HEADS * GLA_DV, D), (GLA_HEADS * GLA_DV) ** -0.5),
        'w_br_c': nrm(ks[22], (L, MLA_HEADS * MLA_DV, D), (MLA_HEADS * MLA_DV) ** -0.5),
        'w_out': nrm(ks[23], (L, D, D), D ** -0.5),
    }


def reference(x, c, ctx, c_ctx, w_mod, b_mod, norm_g, w_in, conv_w, gla_wa_up_f, gla_ba_f, gla_wa_up_b,
              gla_ba_b, gla_norm_g, mla_q_norm_g, mla_kv_norm_g, mla_wq_up, mla_wkv_up, mla_qn_g, mla_kn_g,
              w_br_a, w_br_b, w_br_c, w_out):
    cos, sin = _axial_rope_tables(x.shape[1])
    for l in range(DEPTH):
        p = {
            'w_mod': w_mod[l], 'b_mod': b_mod[l], 'norm_g': norm_g[l], 'w_in': w_in[l], 'conv_w': conv_w[l],
            'gla_wa_up_f': gla_wa_up_f[l], 'gla_ba_f': gla_ba_f[l], 'gla_wa_up_b': gla_wa_up_b[l],
            'gla_ba_b': gla_ba_b[l], 'gla_norm_g': gla_norm_g[l], 'mla_q_norm_g': mla_q_norm_g[l],
            'mla_kv_norm_g': mla_kv_norm_g[l], 'mla_wq_up': mla_wq_up[l], 'mla_wkv_up': mla_wkv_up[l],
            'mla_qn_g': mla_qn_g[l], 'mla_kn_g': mla_kn_g[l], 'w_br_a': w_br_a[l], 'w_br_b': w_br_b[l],
            'w_br_c': w_br_c[l], 'w_out': w_out[l],
        }
        x, ctx = _layer(x, ctx, c, c_ctx, p, cos, sin, l < DEPTH - 1)
    return x
```

```python
import contextlib
import numpy as np
import concourse.bass as bass
import concourse.mybir as mybir
from concourse.bass_utils import run_bass_kernel_spmd

F32 = mybir.dt.float32
BF16 = mybir.dt.bfloat16
ALU = mybir.AluOpType
AF = mybir.ActivationFunctionType
AX = mybir.AxisListType

D = 1024
DEPTH = 4
NCTX = 2
NXT = 32
NT = NCTX + NXT
T = NT * 128
EPS = 1e-6
IN_DIM = 11872
O_AV, O_AB, O_AC, O_AZ = 0, 1024, 2048, 3072
O_BQ, O_BK, O_BV, O_BZ, O_AF, O_ABW = 4096, 4608, 5120, 6144, 7168, 7184
O_CQ, O_CKV, O_CKR, O_CZ = 7200, 7584, 7712, 7776
O_GA, O_GB, O_GC = 8800, 9824, 10848

SEM_PERIOD = 30000
NDMA_SLOTS = 8


class Buf:
    __slots__ = ("name", "excl", "last_w", "readers")

    def __init__(self, name, excl=False):
        self.name = name
        self.excl = excl
        self.last_w = None
        self.readers = []


class Op:
    __slots__ = ("eng", "fn", "reads", "writes", "dma", "deps", "signal", "tick", "idx")


class Prog:
    ENGS = ("pe", "act", "dve", "pool", "sp")

    def __init__(self, nc):
        self.nc = nc
        self.ops = []

    def add(self, eng, fn, reads=(), writes=(), dma=False):
        op = Op()
        op.eng, op.fn, op.dma = eng, fn, dma
        op.reads, op.writes = list(reads), list(writes)
        op.deps, op.signal, op.tick = [], False, None
        op.idx = len(self.ops)
        deps = {}
        for b in op.reads:
            if not b.excl and b.last_w is not None:
                deps[b.last_w.idx] = b.last_w
        wr = list(op.writes) + [b for b in op.reads if b.excl]
        for b in wr:
            if b.last_w is not None:
                deps[b.last_w.idx] = b.last_w
            for r in b.readers:
                deps[r.idx] = r
        for b in op.reads:
            if not b.excl:
                if not op.dma:
                    b.readers = [r for r in b.readers if r.dma or r.eng != op.eng]
                b.readers.append(op)
        for b in wr:
            b.last_w = op
            b.readers = []
        for d in deps.values():
            if d is op:
                continue
            if d.eng == "pe" and op.eng == "pe":
                continue
            op.deps.append(d)
            d.signal = True
        self.ops.append(op)
        return op

    def pe(self, fn, r=(), w=()):
        return self.add("pe", fn, r, w)

    def act(self, fn, r=(), w=()):
        return self.add("act", fn, r, w)

    def dve(self, fn, r=(), w=()):
        return self.add("dve", fn, r, w)

    def pool(self, fn, r=(), w=()):
        return self.add("pool", fn, r, w)

    def ld(self, fn, r=(), w=()):
        return self.add("sp", fn, r, w, dma=True)

    def wld(self, fn, r=(), w=()):
        return self.add("pool", fn, r, w, dma=True)

    def st(self, fn, r=(), w=()):
        return self.add("act", fn, r, w, dma=True)

    def emit(self, final_ops=()):
        nc = self.nc
        for o in final_ops:
            o.signal = True
        n_ticks = {e: 0 for e in self.ENGS}
        for op in self.ops:
            if op.signal and not op.dma:
                n_ticks[op.eng] += 1
        stack = contextlib.ExitStack()
        eng_sems, dma_sems = {}, {}
        for e in self.ENGS:
            n = max(1, (n_ticks[e] + SEM_PERIOD - 1) // SEM_PERIOD)
            eng_sems[e] = [stack.enter_context(nc.semaphore(f"s_{e}_{i}")) for i in range(n)]
            if any(op.dma and op.eng == e for op in self.ops):
                dma_sems[e] = [stack.enter_context(nc.semaphore(f"d_{e}_{i}")) for i in range(NDMA_SLOTS)]
        cnt = {e: 0 for e in self.ENGS}
        dcnt = {e: 0 for e in self.ENGS}
        for op in self.ops:
            if op.dma:
                k = dcnt[op.eng]
                dcnt[op.eng] += 1
                slot = k % NDMA_SLOTS
                op.tick = (dma_sems[op.eng][slot], 16 * (k // NDMA_SLOTS + 1), ("d", op.eng, slot))
            elif op.signal:
                k = cnt[op.eng]
                cnt[op.eng] += 1
                si = k // SEM_PERIOD
                op.tick = (eng_sems[op.eng][si], k % SEM_PERIOD + 1, ("e", op.eng, si))
        by_eng = {e: [op for op in self.ops if op.eng == e] for e in self.ENGS}
        final = list(final_ops)

        def run_engine(ename, eng):
            waited = {}
            for op in by_eng[ename]:
                needs = {}
                for d in op.deps:
                    sem, val, key = d.tick
                    if waited.get(key, 0) >= val:
                        continue
                    if key not in needs or needs[key][1] < val:
                        needs[key] = (sem, val)
                if op.dma:
                    sem, val, key = op.tick
                    if val > 16 and waited.get(key, 0) < val - 16:
                        if key not in needs or needs[key][1] < val - 16:
                            needs[key] = (sem, val - 16)
                for key, (sem, val) in needs.items():
                    eng.wait_ge(sem, val)
                    waited[key] = val
                ins = op.fn(eng)
                if op.dma:
                    ins.then_inc(op.tick[0], 16)
                elif op.signal:
                    ins.then_inc(op.tick[0], 1)
            if ename == "sp":
                for o in final:
                    sem, val, key = o.tick
                    eng.wait_ge(sem, val)

        with stack:
            with nc.Block() as block:
                @block.tensor
                def _(e):
                    run_engine("pe", e)

                @block.scalar
                def _(e):
                    run_engine("act", e)

                @block.vector
                def _(e):
                    run_engine("dve", e)

                @block.gpsimd
                def _(e):
                    run_engine("pool", e)

                @block.sync
                def _(e):
                    run_engine("sp", e)


class Tl:
    __slots__ = ("t", "b")

    def __init__(self, t, b):
        self.t, self.b = t, b


class Arena:
    def __init__(self, nc, stack, nbytes):
        self.a = stack.enter_context(nc.sbuf_tensor("arena", [128, nbytes // 2], BF16))
        self.nbytes = nbytes
        self.top = 0
        self.live = []
        self.dead = []

    def alloc(self, name, shape, dt, nb=0):
        esz = 4 if dt == F32 else 2
        n = 1
        for d_ in shape[1:]:
            n *= d_
        nbytes = (n * esz + 63) // 64 * 64
        off = self.top
        self.top += nbytes
        assert self.top <= self.nbytes, f"SBUF arena overflow at {name}: {self.top}"
        v = self.a[0:shape[0], off // 2:off // 2 + n * esz // 2]
        if dt == F32:
            v = v.bitcast(F32)
        if len(shape) > 2:
            names = "abcdef"[:len(shape) - 1]
            kw = {names[i]: shape[i + 1] for i in range(len(shape) - 2)}
            v = v.rearrange(f"p ({' '.join(names)}) -> p {' '.join(names)}", **kw)
        inherit = {}
        for (o0, o1, ops) in self.dead:
            if o0 < off + nbytes and off < o1:
                for op in ops:
                    inherit[op.idx] = op
        bufs = [Buf(f"{name}{i}") for i in range(nb)] if nb else [Buf(name)]
        for b in bufs:
            b.readers = list(inherit.values())
        self.live.append((off, off + nbytes, bufs))
        return Tl(v, bufs if nb else bufs[0])

    def release(self, mark):
        keep = []
        for (o0, o1, bufs) in self.live:
            if o0 >= mark:
                ops = {}
                for b in bufs:
                    if b.last_w is not None:
                        ops[b.last_w.idx] = b.last_w
                    for r in b.readers:
                        ops[r.idx] = r
                self.dead.append((o0, o1, list(ops.values())))
            else:
                keep.append((o0, o1, bufs))
        self.live = keep
        self.top = mark


class Scope:
    def __init__(self, ar):
        self.ar = ar

    def __enter__(self):
        self.mark = self.ar.top
        return self

    def __exit__(self, *a):
        self.ar.release(self.mark)
        return False

    def sb(self, name, shape, dt, nb=0):
        return self.ar.alloc(name, shape, dt, nb)


def build_nc(L=DEPTH, dbg=False):
    LW = L
    nc = bass.Bass("TRN2", target_bir_lowering=False)
    P = Prog(nc)

    def din(name, shape, dt=F32):
        return nc.dram_tensor(name, shape, dt, kind="ExternalInput").ap()

    xin = din("xin", [NT, 128, D])
    cvec = din("cvec", [128, 16])
    w_mod = din("w_mod", [LW, D, 3 * D])
    b_mod = din("b_mod", [LW, 3 * D])
    norm_g = din("norm_g", [LW, D])
    w_in = din("w_in", [LW, D, IN_DIM])
    w_krp = din("w_krp", [LW, D, 64])
    conv_wT = din("conv_wT", [LW, 128, 24])
    wa_f = din("wa_f", [LW, 17, 512])
    wa_b = din("wa_b", [LW, 17, 512])
    gla_g = din("gla_norm_g", [LW, D])
    qn_g = din("mla_q_norm_g", [LW, 384])
    kvn_g = din("mla_kv_norm_g", [LW, 128])
    wq_aug = din("wq_aug", [LW, 384, 2048])
    wkv = din("wkv", [LW, 128, 2048])
    gvec = din("gvec", [LW, 128, 8])
    w_br = [din(f"w_br_{s}", [LW, D, D]) for s in "abc"]
    w_out = din("w_out", [LW, D, D])
    ident_d = din("ident", [128, 128])
    tri_d = din("tri", [4, 128, 128])
    mask_d = din("mask", [2, 128, 128])
    cos_d = din("cosT", [64, T])
    sin_d = din("sinST", [64, T])
    out = nc.dram_tensor("out", [NXT, 128, D], F32, kind="ExternalOutput").ap()

    skind = "ExternalOutput" if dbg else "Internal"
    xs = nc.dram_tensor("xs", [NT, 128, D], F32, kind=skind).ap()
    hT_d = nc.dram_tensor("hT_d", [NT, 128, D], BF16, kind=skind).ap()
    of_d = nc.dram_tensor("of_d", [NT, 128, D], BF16, kind=skind).ap()
    ybT_d = nc.dram_tensor("ybT_d", [NT, 128, D], BF16, kind=skind).ap()
    ocT_d = nc.dram_tensor("ocT_d", [9, 128, 8 * 512], BF16, kind=skind).ap()
    xs_b = [Buf(f"xs{i}") for i in range(NT)]
    hT_db = [Buf(f"hTd{i}") for i in range(NT)]
    of_db = [Buf(f"ofd{i}") for i in range(NT)]
    ybT_db = [Buf(f"ybTd{i}") for i in range(NT)]
    ocT_db = [Buf(f"ocTd{i}") for i in range(9)]
    out_ops = []

    gstack = contextlib.ExitStack()
    with gstack:
        AR = Arena(nc, gstack, 200 * 1024)
        top = Scope(AR)
        top.__enter__()
        banks = []
        for i in range(8):
            t = gstack.enter_context(nc.psum_tensor(f"bank{i}", [128, 512], F32))
            banks.append(Tl(t, Buf(f"bank{i}", excl=True)))
        bank_ctr = [0]
        reserved = set()

        def pb():
            while True:
                k = bank_ctr[0] % 8
                bank_ctr[0] += 1
                if k not in reserved:
                    return banks[k]

        def bf(bank):
            return bank.t[:].bitcast(BF16)

        ident = top.sb("ident", [128, 128], BF16)
        ones = top.sb("ones", [128, 128], BF16)
        zeros = top.sb("zeros", [128, 128], F32)
        tri = top.sb("tri", [128, 4, 128], F32)
        mask = top.sb("mask", [128, 2, 128], F32)
        cosT = top.sb("cosT", [64, T], BF16)
        sinT = top.sb("sinT", [64, T], BF16)
        cv = top.sb("cv", [128, 16], F32)
        screp = top.sb("screp", [128, 16, 128], BF16)
        rk_s = top.sb("rk_s", [128, NT, 8], F32)
        P.wld(lambda e: e.dma_start(out=ident.t[:], in_=ident_d[:, :]), [], [ident.b])
        P.ld(lambda e: e.dma_start(out=tri.t[:], in_=tri_d.rearrange("a p n -> p a n")), [], [tri.b])
        P.ld(lambda e: e.dma_start(out=mask.t[:], in_=mask_d.rearrange("a p n -> p a n")), [], [mask.b])
        P.wld(lambda e: e.dma_start(out=cosT.t[:], in_=cos_d[:, :]), [], [cosT.b])
        P.wld(lambda e: e.dma_start(out=sinT.t[:], in_=sin_d[:, :]), [], [sinT.b])
        P.ld(lambda e: e.dma_start(out=cv.t[:], in_=cvec[:, :]), [], [cv.b])
        P.pool(lambda e: e.memset(ones.t[:], 1.0), [], [ones.b])
        P.pool(lambda e: e.memset(zeros.t[:], 0.0), [], [zeros.b])
        P.act(lambda e: e.activation(out=cv.t[:], in_=cv.t[:], func=AF.Silu), [cv.b], [cv.b])
        for c in range(16):
            P.act(lambda e, c=c: e.activation(out=screp.t[:, c, :], in_=zeros.t[:], func=AF.Identity, bias=cv.t[:, c:c + 1]),
                  [cv.b, zeros.b], [screp.b])

        def emit_layer(l):
            last = (l == DEPTH - 1)
            x_src = xin if l == 0 else xs
            with Scope(AR) as ly:
                G = ly.sb("G", [128, 2, D], F32)
                gv = ly.sb("gv", [128, 8], F32)
                P.ld(lambda e: e.dma_start(out=gv.t[:], in_=gvec[l]), [], [gv.b])
                s14 = Scope(AR)
                s14.__enter__()
                cqnT = s14.sb("cqnT", [128, 3, T], BF16)
                ckvnT = s14.sb("ckvnT", [128, T], BF16)
                krT = s14.sb("krT", [64, T], BF16)

                with Scope(AR) as s1:
                    AB = s1.sb("AB", [128, 4, D], F32)
                    grep = s1.sb("grep", [128, D], F32)
                    bmrep = s1.sb("bmrep", [128, 3 * D], F32)
                    gq_rep = s1.sb("gq_rep", [128, 512], F32)
                    P.ld(lambda e: e.dma_start(out=grep.t[:], in_=norm_g[l, :].partition_broadcast(128)), [], [grep.b])
                    P.ld(lambda e: e.dma_start(out=bmrep.t[:], in_=b_mod[l, :].partition_broadcast(128)), [], [bmrep.b])
                    P.ld(lambda e: e.dma_start(out=gq_rep.t[:, 0:384], in_=qn_g[l, :].partition_broadcast(128)), [], [gq_rep.b])
                    P.ld(lambda e: e.dma_start(out=gq_rep.t[:, 384:512], in_=kvn_g[l, :].partition_broadcast(128)), [], [gq_rep.b])
                    wm = s1.sb("wm", [128, 2, 8, 512], BF16, nb=2)
                    for n in range(6):
                        P.wld(lambda e, n=n: e.dma_start(out=wm.t[:, n % 2], in_=w_mod[l, :, n * 512:(n + 1) * 512].rearrange("(c p) n -> p c n", p=128)),
                              [], [wm.b[n % 2]])
                        for who in range(2):
                            bk = pb()
                            for k in range(8):
                                P.pe(lambda e, k=k, bk=bk, n=n, who=who: e.matmul(bk.t[:], lhsT=screp.t[:, (8 if who == 0 else 0) + k, :],
                                                                                  rhs=wm.t[:, n % 2, k, :], start=(k == 0), stop=(k == 7)),
                                     [screp.b, wm.b[n % 2]], [bk.b])
                            part, half = n // 2, n % 2
                            cs = slice(half * 512, (half + 1) * 512)
                            bms = bmrep.t[:, n * 512:(n + 1) * 512]
                            if part == 0:
                                P.dve(lambda e, bk=bk, who=who, cs=cs, bms=bms: e.tensor_tensor(out=AB.t[:, 2 * who + 1, cs], in0=bk.t[:], in1=bms, op=ALU.add),
                                      [bk.b, bmrep.b], [AB.b])
                            elif part == 1:
                                P.dve(lambda e, bk=bk, who=who, cs=cs, bms=bms: e.tensor_tensor(out=AB.t[:, 2 * who, cs], in0=bk.t[:], in1=bms, op=ALU.add),
                                      [bk.b, bmrep.b], [AB.b])
                                P.dve(lambda e, who=who, cs=cs: e.scalar_tensor_tensor(out=AB.t[:, 2 * who, cs], in0=AB.t[:, 2 * who, cs], scalar=1.0,
                                                                                       in1=grep.t[:, cs], op0=ALU.add, op1=ALU.mult),
                                      [AB.b, grep.b], [AB.b])
                            else:
                                P.dve(lambda e, bk=bk, who=who, cs=cs, bms=bms: e.tensor_tensor(out=G.t[:, who, cs], in0=bk.t[:], in1=bms, op=ALU.add),
                                      [bk.b, bmrep.b], [G.b])

                    wlat = s1.sb("wlat", [128, 8, 576], BF16)
                    wkrp = s1.sb("wkrp", [128, 8, 64], BF16)
                    wkvs = s1.sb("wkvs", [128, 2048], BF16)
                    P.wld(lambda e: e.dma_start(out=wlat.t[:], in_=w_in[l, :, O_CQ:O_CQ + 576].rearrange("(c p) n -> p c n", p=128)), [], [wlat.b])
                    P.wld(lambda e: e.dma_start(out=wkrp.t[:], in_=w_krp[l].rearrange("(c p) n -> p c n", p=128)), [], [wkrp.b])
                    P.wld(lambda e: e.dma_start(out=wkvs.t[:], in_=wkv[l]), [], [wkvs.b])

                    xt = s1.sb("xt", [128, 2, D], F32, nb=2)
                    junk = s1.sb("junk", [128, D], BF16)
                    st4 = s1.sb("st4", [128, 2, 8], F32, nb=2)
                    tnrm = s1.sb("tnrm", [128, 2, D], F32, nb=2)
                    hb = s1.sb("hb", [128, 2, D], BF16, nb=2)
                    hTt = s1.sb("hTt", [128, 2, 8, 128], BF16, nb=2)
                    cq = s1.sb("cq", [128, 2, 512], BF16, nb=2)
                    rt = s1.sb("rt", [64, 2, 2, 128], F32, nb=2)
                    sqk = s1.sb("sqk", [128, 2, D], F32, nb=2)
                    ssk = s1.sb("ssk", [128, 2, 8], F32, nb=2)
                    for i in range(NT):
                        p2 = i % 2
                        who = 0 if i < NCTX else 1
                        xb, sb_, tb, hbb, hTb, cqb, rtb, sqb, skb = (xt.b[p2], st4.b[p2], tnrm.b[p2], hb.b[p2], hTt.b[p2], cq.b[p2], rt.b[p2],
                                                                      sqk.b[p2], ssk.b[p2])
                        P.ld(lambda e, i=i, p2=p2: e.dma_start(out=xt.t[:, p2], in_=x_src[i]), [xs_b[i]] if l > 0 else [], [xb])
                        P.act(lambda e, p2=p2: e.activation(out=junk.t[:], in_=xt.t[:, p2], func=AF.Square, scale=1.0 / 32, accum_out=st4.t[:, p2, 0:1]),
                              [xb], [junk.b, sb_])
                        P.act(lambda e, p2=p2: e.activation(out=st4.t[:, p2, 1:2], in_=st4.t[:, p2, 0:1], func=AF.Sqrt, bias=EPS, scale=1.0), [sb_], [sb_])
                        P.dve(lambda e, p2=p2: e.reciprocal(out=st4.t[:, p2, 2:3], in_=st4.t[:, p2, 1:2]), [sb_], [sb_])
                        P.dve(lambda e, p2=p2, who=who: e.scalar_tensor_tensor(out=tnrm.t[:, p2], in0=xt.t[:, p2], scalar=st4.t[:, p2, 2:3],
                                                                                in1=AB.t[:, 2 * who], op0=ALU.mult, op1=ALU.mult),
                              [xb, sb_, AB.b], [tb])
                        P.pool(lambda e, p2=p2, who=who: e.tensor_tensor(out=hb.t[:, p2], in0=tnrm.t[:, p2], in1=AB.t[:, 2 * who + 1], op=ALU.add),
                               [tb, AB.b], [hbb])
                        bk = pb()
                        for c in range(8):
                            P.pe(lambda e, c=c, bk=bk, p2=p2: e.transpose(out=bf(bk)[:, c * 128:(c + 1) * 128], in_=hb.t[:, p2, c * 128:(c + 1) * 128],
                                                                           identity=ident.t[:]), [hbb, ident.b], [bk.b])
                        P.act(lambda e, bk=bk, p2=p2: e.activation(out=hTt.t[:, p2].rearrange("p c n -> p (c n)"), in_=bf(bk), func=AF.Copy),
                              [bk.b], [hTb])
                        P.st(lambda e, i=i, p2=p2: e.dma_start(out=hT_d[i], in_=hTt.t[:, p2].rearrange("p c n -> p (c n)")), [hTb], [hT_db[i]])
                        b1, b2 = pb(), pb()
                        for k in range(8):
                            P.pe(lambda e, k=k, b1=b1, p2=p2: e.matmul(b1.t[:], lhsT=hTt.t[:, p2, k, :], rhs=wlat.t[:, k, 0:512], start=(k == 0), stop=(k == 7)),
                                 [hTb, wlat.b], [b1.b])
                        for k in range(8):
                            P.pe(lambda e, k=k, b2=b2, p2=p2: e.matmul(b2.t[:, 0:64], lhsT=hTt.t[:, p2, k, :], rhs=wlat.t[:, k, 512:576], start=(k == 0), stop=(k == 7)),
                                 [hTb, wlat.b], [b2.b])
                        P.act(lambda e, b1=b1, p2=p2: e.activation(out=junk.t[:, 0:384], in_=b1.t[:, 0:384], func=AF.Square, scale=float(384 ** -0.5),
                                                                    accum_out=st4.t[:, p2, 3:4]), [b1.b], [junk.b, sb_])
                        P.act(lambda e, b1=b1, p2=p2: e.activation(out=junk.t[:, 384:512], in_=b1.t[:, 384:512], func=AF.Square, scale=float(128 ** -0.5),
                                                                    accum_out=st4.t[:, p2, 4:5]), [b1.b], [junk.b, sb_])
                        P.act(lambda e, b2=b2, p2=p2: e.activation(out=junk.t[:, 512:576], in_=b2.t[:, 0:64], func=AF.Square,
                                                                    accum_out=st4.t[:, p2, 7:8]), [b2.b], [junk.b, sb_])
                        P.act(lambda e, p2=p2: e.activation(out=st4.t[:, p2, 5:7], in_=st4.t[:, p2, 3:5], func=AF.Sqrt, bias=EPS, scale=1.0), [sb_], [sb_])
                        P.dve(lambda e, p2=p2: e.reciprocal(out=st4.t[:, p2, 5:7], in_=st4.t[:, p2, 5:7]), [sb_], [sb_])
                        P.dve(lambda e, b1=b1, p2=p2: e.scalar_tensor_tensor(out=cq.t[:, p2, 0:384], in0=b1.t[:, 0:384], scalar=st4.t[:, p2, 5:6],
                                                                              in1=gq_rep.t[:, 0:384], op0=ALU.mult, op1=ALU.mult),
                              [b1.b, sb_, gq_rep.b], [cqb])
                        P.dve(lambda e, b1=b1, p2=p2: e.scalar_tensor_tensor(out=cq.t[:, p2, 384:512], in0=b1.t[:, 384:512], scalar=st4.t[:, p2, 6:7],
                                                                              in1=gq_rep.t[:, 384:512], op0=ALU.mult, op1=ALU.mult),
                              [b1.b, sb_, gq_rep.b], [cqb])
                        b3 = pb()
                        for c in range(4):
                            P.pe(lambda e, c=c, b3=b3, p2=p2: e.transpose(out=bf(b3)[:, c * 128:(c + 1) * 128], in_=cq.t[:, p2, c * 128:(c + 1) * 128],
                                                                           identity=ident.t[:]), [cqb, ident.b], [b3.b])
                        tsl = slice(i * 128, (i + 1) * 128)
                        P.act(lambda e, b3=b3, tsl=tsl: e.activation(out=cqnT.t[:, :, tsl], in_=bf(b3)[:, 0:384].rearrange("p (c n) -> p c n", c=3), func=AF.Copy),
                              [b3.b], [cqnT.b])
                        P.act(lambda e, b3=b3, tsl=tsl: e.activation(out=ckvnT.t[:, tsl], in_=bf(b3)[:, 384:512], func=AF.Copy), [b3.b], [ckvnT.b])
                        b4 = pb()
                        for k in range(8):
                            P.pe(lambda e, k=k, b4=b4, p2=p2: e.matmul(b4.t[0:64, 0:128], lhsT=wlat.t[:, k, 512:576], rhs=hTt.t[:, p2, k, :], start=(k == 0), stop=(k == 7)),
                                 [hTb, wlat.b], [b4.b])
                        for k in range(8):
                            P.pe(lambda e, k=k, b4=b4, p2=p2: e.matmul(b4.t[0:64, 128:256], lhsT=wkrp.t[:, k, :], rhs=hTt.t[:, p2, k, :], start=(k == 0), stop=(k == 7)),
                                 [hTb, wkrp.b], [b4.b])
                        P.dve(lambda e, b4=b4, p2=p2, tsl=tsl: e.scalar_tensor_tensor(out=rt.t[:, p2, 0], in0=b4.t[0:64, 0:128], scalar=gv.t[0:64, 4:5],
                                                                                       in1=cosT.t[:, tsl], op0=ALU.mult, op1=ALU.mult),
                              [b4.b, gv.b, cosT.b], [rtb])
                        P.dve(lambda e, b4=b4, p2=p2, tsl=tsl: e.scalar_tensor_tensor(out=rt.t[:, p2, 1], in0=b4.t[0:64, 128:256], scalar=gv.t[0:64, 5:6],
                                                                                       in1=sinT.t[:, tsl], op0=ALU.mult, op1=ALU.mult),
                              [b4.b, gv.b, sinT.b], [rtb])
                        P.pool(lambda e, p2=p2, tsl=tsl: e.tensor_tensor(out=krT.t[:, tsl], in0=rt.t[:, p2, 0], in1=rt.t[:, p2, 1], op=ALU.add), [rtb], [krT.b])
                        b5, b6 = pb(), pb()
                        for hh, bb in ((0, b5), (1, b6)):
                            P.pe(lambda e, hh=hh, bb=bb, tsl=tsl: e.matmul(bb.t[:], lhsT=ckvnT.t[:, tsl],
                                                                            rhs=wkvs.t[:].rearrange("p (h x) -> p h x", h=8)[:, hh * 4:(hh + 1) * 4, 0:128],
                                                                            start=True, stop=True), [ckvnT.b, wkvs.b], [bb.b])
                            P.act(lambda e, hh=hh, bb=bb, p2=p2: e.activation(out=sqk.t[:, p2, hh * 512:(hh + 1) * 512], in_=bb.t[:], func=AF.Square),
                                  [bb.b], [sqb])
                        P.dve(lambda e, p2=p2: e.tensor_reduce(out=ssk.t[:, p2], in_=sqk.t[:, p2].rearrange("p (h x) -> p h x", h=8), axis=AX.X, op=ALU.add),
                              [sqb], [skb])
                        P.dve(lambda e, p2=p2: e.tensor_scalar(out=ssk.t[:, p2], in0=ssk.t[:, p2], scalar1=st4.t[:, p2, 7:8], scalar2=1.0 / 192,
                                                               op0=ALU.add, op1=ALU.mult), [skb, sb_], [skb])
                        P.act(lambda e, p2=p2: e.activation(out=ssk.t[:, p2], in_=ssk.t[:, p2], func=AF.Sqrt, bias=EPS, scale=1.0), [skb], [skb])
                        P.dve(lambda e, p2=p2: e.reciprocal(out=ssk.t[:, p2], in_=ssk.t[:, p2]), [skb], [skb])
                        P.dve(lambda e, p2=p2, i=i: e.tensor_scalar(out=rk_s.t[:, i, :], in0=ssk.t[:, p2], scalar1=float(192 ** -0.5), scalar2=None, op0=ALU.mult),
                              [skb], [rk_s.b])

                if dbg == 1:
                    d_cq = nc.dram_tensor("d_cq", [128, 3 * T], BF16, kind="ExternalOutput").ap()
                    d_ckv = nc.dram_tensor("d_ckv", [128, T], BF16, kind="ExternalOutput").ap()
                    d_kr = nc.dram_tensor("d_kr", [64, T], BF16, kind="ExternalOutput").ap()
                    d_rk = nc.dram_tensor("d_rk", [128, NT * 8], F32, kind="ExternalOutput").ap()
                    d_G = nc.dram_tensor("d_G", [128, 2 * D], F32, kind="ExternalOutput").ap()
                    out_ops.append(P.ld(lambda e: e.dma_start(out=d_cq[:, :], in_=cqnT.t[:].rearrange("p c n -> p (c n)")), [cqnT.b], []))
                    out_ops.append(P.ld(lambda e: e.dma_start(out=d_ckv[:, :], in_=ckvnT.t[:]), [ckvnT.b], []))
                    out_ops.append(P.ld(lambda e: e.dma_start(out=d_kr[:, :], in_=krT.t[:]), [krT.b], []))
                    out_ops.append(P.ld(lambda e: e.dma_start(out=d_rk[:, :], in_=rk_s.t[:].rearrange("p a b -> p (a b)")), [rk_s.b], []))
                    out_ops.append(P.ld(lambda e: e.dma_start(out=d_G[:, :], in_=G.t[:].rearrange("p a b -> p (a b)")), [G.b], []))
                    out_ops.append(P.ld(lambda e: e.dma_start(out=out[1], in_=xin[3]), hT_db, []))
                    s14.__exit__(None, None, None)
                    return True

                with Scope(AR) as s3:
                    wgT = s3.sb("wgT", [128, 8, 2560], BF16)
                    wgq = s3.sb("wgq", [128, 8, 512], BF16)
                    waf = s3.sb("waf", [128, 8, 32], BF16)
                    waa = s3.sb("waa", [17, 2, 512], BF16)
                    gg_rep = s3.sb("gg_rep", [128, D], F32)
                    P.wld(lambda e: e.dma_start(out=wgT.t[:], in_=w_in[l, :, O_BK:O_BK + 2560].rearrange("(c p) n -> p c n", p=128)), [], [wgT.b])
                    P.wld(lambda e: e.dma_start(out=wgq.t[:], in_=w_in[l, :, O_BQ:O_BQ + 512].rearrange("(c p) n -> p c n", p=128)), [], [wgq.b])
                    P.wld(lambda e: e.dma_start(out=waf.t[:], in_=w_in[l, :, O_AF:O_AF + 32].rearrange("(c p) n -> p c n", p=128)), [], [waf.b])
                    P.wld(lambda e: e.dma_start(out=waa.t[:, 0, :], in_=wa_f[l]), [], [waa.b])
                    P.wld(lambda e: e.dma_start(out=waa.t[:, 1, :], in_=wa_b[l]), [], [waa.b])
                    P.ld(lambda e: e.dma_start(out=gg_rep.t[:], in_=gla_g[l, :].partition_broadcast(128)), [], [gg_rep.b])
                    S = s3.sb("S", [128, 4, 256], F32)
                    Sbf = s3.sb("Sbf", [128, 4, 256], BF16)
                    baf = s3.sb("baf", [17, 128], BF16)
                    P.pool(lambda e: e.memset(baf.t[:], 1.0), [], [baf.b])
                    hTg = s3.sb("hTg", [128, 2, 8, 128], BF16, nb=2)
                    lsp = s3.sb("lsp", [128, 2, 512], F32, nb=2)
                    ec = s3.sb("ec", [128, 2, 512], F32, nb=2)
                    kend = s3.sb("kend", [128, 2, 512], BF16, nb=2)
                    vv = s3.sb("vv", [128, 2, 1024], BF16, nb=2)
                    ebT = s3.sb("ebT", [128, 2, 4, 128], F32, nb=2)
                    enbT = s3.sb("enbT", [128, 2, 4, 128], F32, nb=2)
                    qd = s3.sb("qd", [128, 2, 4, 128], BF16, nb=2)
                    ki = s3.sb("ki", [128, 2, 4, 128], BF16, nb=2)
                    AT = s3.sb("AT", [128, 2, 4, 128], BF16, nb=2)
                    ofs = s3.sb("ofs", [128, 2, 1024], BF16, nb=2)
                    osum = s3.sb("osum", [128, 1024], F32)
                    junk3 = s3.sb("junk3", [128, 256], BF16)
                    stt = s3.sb("stt", [128, 2, 8], F32, nb=2)
                    sz = s3.sb("sz", [128, 1024], F32)
                    ybt = s3.sb("ybt", [128, 1024], BF16)
                    ybTt = s3.sb("ybTt", [128, 2, 1024], BF16, nb=2)
                    orders = (list(range(NT)), [1, 0] + list(range(NT - 1, 1, -1)))
                    for ps_ in (0, 1):
                        gcol = 127 if ps_ == 0 else 0
                        P.pool(lambda e: e.memset(S.t[:], 0.0), [], [S.b])
                        P.pool(lambda e: e.memset(Sbf.t[:], 0.0), [], [Sbf.b])
                        for n, i in enumerate(orders[ps_]):
                            p2 = n % 2
                            need_out = not (last and i < NCTX)
                            hB = hTg.b[p2]
                            P.ld(lambda e, i=i, p2=p2: e.dma_start(out=hTg.t[:, p2].rearrange("p c n -> p (c n)"), in_=hT_d[i]), [hT_db[i]], [hB])
                            bk = pb()
                            for k in range(8):
                                P.pe(lambda e, k=k, bk=bk, p2=p2, ps_=ps_: e.matmul(bk.t[0:16, 0:128], lhsT=waf.t[:, k, ps_ * 16:(ps_ + 1) * 16], rhs=hTg.t[:, p2, k, :],
                                                                                   start=(k == 0), stop=(k == 7)), [hB, waf.b], [bk.b])
                            P.act(lambda e, bk=bk: e.activation(out=baf.t[0:16, :], in_=bk.t[0:16, 0:128], func=AF.Copy), [bk.b], [baf.b])
                            bz = pb()
                            P.pe(lambda e, bz=bz, ps_=ps_: e.matmul(bz.t[:], lhsT=baf.t[:, :], rhs=waa.t[:, ps_, :], start=True, stop=True), [baf.b, waa.b], [bz.b])
                            P.act(lambda e, bz=bz, p2=p2: e.activation(out=lsp.t[:, p2], in_=bz.t[:], func=AF.Exp, scale=-1.0), [bz.b], [lsp.b[p2]])
                            P.act(lambda e, p2=p2: e.activation(out=lsp.t[:, p2], in_=lsp.t[:, p2], func=AF.Ln, bias=1.0, scale=1.0), [lsp.b[p2]], [lsp.b[p2]])
                            bc = pb()
                            P.pe(lambda e, bc=bc, p2=p2, ps_=ps_: e.matmul(bc.t[:], lhsT=tri.t[:, 2 * ps_ + 1, :], rhs=lsp.t[:, p2, :], start=True, stop=True),
                                 [tri.b, lsp.b[p2]], [bc.b])
                            bb = pb()
                            for h in range(4):
                                P.pe(lambda e, bb=bb, h=h, p2=p2, ps_=ps_: e.matmul(bb.t[:, h * 128:(h + 1) * 128], lhsT=lsp.t[:, p2, h * 128:(h + 1) * 128],
                                                                                   rhs=tri.t[:, 2 * ps_, :], start=True, stop=True), [tri.b, lsp.b[p2]], [bb.b])
                            P.act(lambda e, bc=bc, p2=p2: e.activation(out=ec.t[:, p2], in_=bc.t[:], func=AF.Exp), [bc.b], [ec.b[p2]])
                            P.act(lambda e, bb=bb, p2=p2: e.activation(out=ebT.t[:, p2].rearrange("p h n -> p (h n)"), in_=bb.t[:], func=AF.Exp), [bb.b], [ebT.b[p2]])
                            P.act(lambda e, bb=bb, p2=p2: e.activation(out=enbT.t[:, p2].rearrange("p h n -> p (h n)"), in_=bb.t[:], func=AF.Exp, scale=-1.0),
                                  [bb.b], [enbT.b[p2]])
                            bkk = pb()
                            for k in range(8):
                                P.pe(lambda e, k=k, bkk=bkk, p2=p2: e.matmul(bkk.t[:], lhsT=hTg.t[:, p2, k, :], rhs=wgT.t[:, k, 0:512], start=(k == 0), stop=(k == 7)),
                                     [hB, wgT.b], [bkk.b])
                            P.dve(lambda e, bkk=bkk, p2=p2: e.tensor_tensor(out=kend.t[:, p2], in0=bkk.t[:], in1=ec.t[:, p2], op=ALU.mult), [bkk.b, ec.b[p2]], [kend.b[p2]])
                            for half in range(2):
                                bv = pb()
                                for k in range(8):
                                    P.pe(lambda e, k=k, bv=bv, p2=p2, half=half: e.matmul(bv.t[:], lhsT=hTg.t[:, p2, k, :], rhs=wgT.t[:, k, 512 + half * 512:1024 + half * 512],
                                                                                         start=(k == 0), stop=(k == 7)), [hB, wgT.b], [bv.b])
                                P.act(lambda e, bv=bv, p2=p2, half=half: e.activation(out=vv.t[:, p2, half * 512:(half + 1) * 512], in_=bv.t[:], func=AF.Copy),
                                      [bv.b], [vv.b[p2]])
                            bq = pb()
                            for h in range(4):
                                for k in range(8):
                                    P.pe(lambda e, k=k, h=h, bq=bq, p2=p2: e.matmul(bq.t[:, h * 128:(h + 1) * 128], lhsT=wgq.t[:, k, h * 128:(h + 1) * 128], rhs=hTg.t[:, p2, k, :],
                                                                                   start=(k == 0), stop=(k == 7)), [hB, wgq.b], [bq.b])
                            P.dve(lambda e, bq=bq, p2=p2: e.scalar_tensor_tensor(out=qd.t[:, p2].rearrange("p h n -> p (h n)"), in0=bq.t[:], scalar=float(128 ** -0.5),
                                                                                  in1=ebT.t[:, p2].rearrange("p h n -> p (h n)"), op0=ALU.mult, op1=ALU.mult),
                                  [bq.b, ebT.b[p2]], [qd.b[p2]])
                            bkT = pb()
                            for h in range(4):
                                for k in range(8):
                                    P.pe(lambda e, k=k, h=h, bkT=bkT, p2=p2: e.matmul(bkT.t[:, h * 128:(h + 1) * 128], lhsT=wgT.t[:, k, h * 128:(h + 1) * 128], rhs=hTg.t[:, p2, k, :],
                                                                                     start=(k == 0), stop=(k == 7)), [hB, wgT.b], [bkT.b])
                            P.dve(lambda e, bkT=bkT, p2=p2: e.tensor_tensor(out=ki.t[:, p2].rearrange("p h n -> p (h n)"), in0=bkT.t[:],
                                                                             in1=enbT.t[:, p2].rearrange("p h n -> p (h n)"), op=ALU.mult), [bkT.b, enbT.b[p2]], [ki.b[p2]])
                            if need_out:
                                ba = pb()
                                for h in range(4):
                                    P.pe(lambda e, h=h, ba=ba, p2=p2: e.matmul(ba.t[:, h * 128:(h + 1) * 128], lhsT=ki.t[:, p2, h, :], rhs=qd.t[:, p2, h, :], start=True, stop=True),
                                         [ki.b[p2], qd.b[p2]], [ba.b])
                                P.dve(lambda e, ba=ba, p2=p2, ps_=ps_: e.tensor_tensor(out=AT.t[:, p2], in0=ba.t[:].rearrange("p (h n) -> p h n", h=4),
                                                                                      in1=mask.t[:, ps_, :].unsqueeze(1).to_broadcast([128, 4, 128]), op=ALU.mult),
                                      [ba.b, mask.b], [AT.b[p2]])
                                bo = (pb(), pb())
                                for h in range(4):
                                    b_ = bo[h // 2]
                                    cs = (h % 2) * 256
                                    P.pe(lambda e, h=h, b_=b_, cs=cs, p2=p2: e.matmul(b_.t[:, cs:cs + 256], lhsT=AT.t[:, p2, h, :], rhs=vv.t[:, p2, h * 256:(h + 1) * 256],
                                                                                     start=True, stop=False), [AT.b[p2], vv.b[p2]], [b_.b])
                                    P.pe(lambda e, h=h, b_=b_, cs=cs, p2=p2: e.matmul(b_.t[:, cs:cs + 256], lhsT=qd.t[:, p2, h, :], rhs=Sbf.t[:, h, :],
                                                                                     start=False, stop=True), [qd.b[p2], Sbf.b], [b_.b])
                                if ps_ == 0:
                                    for half in range(2):
                                        P.act(lambda e, half=half, p2=p2, b_=bo[half]: e.activation(out=ofs.t[:, p2, half * 512:(half + 1) * 512], in_=b_.t[:], func=AF.Copy),
                                              [bo[half].b], [ofs.b[p2]])
                                    P.st(lambda e, i=i, p2=p2: e.dma_start(out=of_d[i], in_=ofs.t[:, p2]), [ofs.b[p2]], [of_db[i]])
                                else:
                                    P.ld(lambda e, i=i, p2=p2: e.dma_start(out=ofs.t[:, p2], in_=of_d[i]), [of_db[i]], [ofs.b[p2]])
                                    for half in range(2):
                                        P.dve(lambda e, half=half, p2=p2, b_=bo[half]: e.tensor_tensor(out=osum.t[:, half * 512:(half + 1) * 512], in0=b_.t[:],
                                                                                                         in1=ofs.t[:, p2, half * 512:(half + 1) * 512], op=ALU.add),
                                              [bo[half].b, ofs.b[p2]], [osum.b])
                                    for h in range(4):
                                        P.act(lambda e, h=h, p2=p2: e.activation(out=junk3.t[:], in_=osum.t[:, h * 256:(h + 1) * 256], func=AF.Square, scale=1.0 / 16,
                                                                                 accum_out=stt.t[:, p2, h:h + 1]), [osum.b], [junk3.b, stt.b[p2]])
                                    P.act(lambda e, p2=p2: e.activation(out=stt.t[:, p2, 4:8], in_=stt.t[:, p2, 0:4], func=AF.Sqrt, bias=EPS, scale=1.0), [stt.b[p2]], [stt.b[p2]])
                                    P.dve(lambda e, p2=p2: e.reciprocal(out=stt.t[:, p2, 4:8], in_=stt.t[:, p2, 4:8]), [stt.b[p2]], [stt.b[p2]])
                                    for half in range(2):
                                        bzz = pb()
                                        for k in range(8):
                                            P.pe(lambda e, k=k, bzz=bzz, p2=p2, half=half: e.matmul(bzz.t[:], lhsT=hTg.t[:, p2, k, :], rhs=wgT.t[:, k, 1536 + half * 512:2048 + half * 512],
                                                                                                   start=(k == 0), stop=(k == 7)), [hB, wgT.b], [bzz.b])
                                        P.act(lambda e, bzz=bzz, half=half: e.activation(out=sz.t[:, half * 512:(half + 1) * 512], in_=bzz.t[:], func=AF.Silu), [bzz.b], [sz.b])
                                    P.pool(lambda e: e.tensor_tensor(out=sz.t[:], in0=sz.t[:], in1=gg_rep.t[:], op=ALU.mult), [sz.b, gg_rep.b], [sz.b])
                                    for h in range(4):
                                        P.dve(lambda e, h=h, p2=p2: e.scalar_tensor_tensor(out=ybt.t[:, h * 256:(h + 1) * 256], in0=osum.t[:, h * 256:(h + 1) * 256],
                                                                                           scalar=stt.t[:, p2, 4 + h:5 + h], in1=sz.t[:, h * 256:(h + 1) * 256],
                                                                                           op0=ALU.mult, op1=ALU.mult), [osum.b, stt.b[p2], sz.b], [ybt.b])
                                    bt = pb()
                                    for c in range(8):
                                        P.pe(lambda e, c=c, bt=bt: e.transpose(out=bf(bt)[:, c * 128:(c + 1) * 128], in_=ybt.t[:, c * 128:(c + 1) * 128], identity=ident.t[:]),
                                             [ybt.b, ident.b], [bt.b])
                                    P.act(lambda e, bt=bt, p2=p2: e.activation(out=ybTt.t[:, p2], in_=bf(bt), func=AF.Copy), [bt.b], [ybTt.b[p2]])
                                    P.st(lambda e, i=i, p2=p2: e.dma_start(out=ybT_d[i], in_=ybTt.t[:, p2]), [ybTt.b[p2]], [ybT_db[i]])
                            bu = (pb(), pb())
                            for h in range(4):
                                b_ = bu[h // 2]
                                cs = (h % 2) * 256
                                P.pe(lambda e, h=h, b_=b_, cs=cs, p2=p2: e.matmul(b_.t[:, cs:cs + 256], lhsT=kend.t[:, p2, h * 128:(h + 1) * 128], rhs=vv.t[:, p2, h * 256:(h + 1) * 256],
                                                                                 start=True, stop=True), [kend.b[p2], vv.b[p2]], [b_.b])
                            for h in range(4):
                                b_ = bu[h // 2]
                                cs = (h % 2) * 256
                                P.dve(lambda e, h=h, b_=b_, cs=cs, p2=p2, gcol=gcol: e.scalar_tensor_tensor(out=S.t[:, h, :], in0=S.t[:, h, :], scalar=ebT.t[:, p2, h, gcol:gcol + 1],
                                                                                                           in1=b_.t[:, cs:cs + 256], op0=ALU.mult, op1=ALU.add),
                                      [S.b, ebT.b[p2], b_.b], [S.b])
                            P.pool(lambda e: e.tensor_copy(out=Sbf.t[:], in_=S.t[:]), [S.b], [Sbf.b])
                if dbg == 3:
                    out_ops.append(P.ld(lambda e: e.dma_start(out=out[1], in_=xin[3]), hT_db + of_db + ybT_db, []))
                    s14.__exit__(None, None, None)
                    return True

                with Scope(AR) as s4:
                    wkvs4 = s4.sb("wkvs4", [128, 2048], BF16)
                    wqs = s4.sb("wqs", [128, 3, 2048], BF16)
                    P.wld(lambda e: e.dma_start(out=wkvs4.t[:], in_=wkv[l]), [], [wkvs4.b])
                    P.wld(lambda e: e.dma_start(out=wqs.t[:], in_=wq_aug[l].rearrange("(c p) n -> p c n", p=128)), [], [wqs.b])
                    KhT = s4.sb("KhT", [128, T], BF16)
                    Vh = s4.sb("Vh", [128, NT, 128], BF16)
                    QnT = s4.sb("QnT", [128, 2, 512], BF16, nb=2)
                    QrT = s4.sb("QrT", [64, 2, 512], BF16, nb=2)
                    sq1 = s4.sb("sq1", [128, 512], BF16)
                    sq2 = s4.sb("sq2", [64, 512], BF16)
                    rq = s4.sb("rq", [128, 2, 512], F32, nb=2)
                    t12 = s4.sb("t12", [64, 2, 512], F32)
                    PT = s4.sb("PT", [128, 4, 512], BF16, nb=4)
                    rs = s4.sb("rs", [128, 512], F32)
                    oc = s4.sb("oc", [128, 2, 512], BF16, nb=2)
                    acc_banks = [(banks[0], banks[1]), (banks[2], banks[3])]
                    reserved.update({0, 1, 2, 3})
                    qctr = 0
                    qtiles = ([] if last else [(0, 0, 256, [0, 1])]) + [(1 + q, 256 + q * 512, 512, list(range(NT))) for q in range(8)]
                    for h in range(8):
                        for kt in range(9):
                            n = 512 if kt < 8 else 256
                            cols = slice(kt * 512, kt * 512 + n)
                            bk = pb()
                            P.pe(lambda e, bk=bk, n=n, cols=cols, h=h: e.matmul(bk.t[:, 0:n], lhsT=wkvs4.t[:, h * 256:h * 256 + 128], rhs=ckvnT.t[:, cols], start=True, stop=True),
                                 [wkvs4.b, ckvnT.b], [bk.b])
                            P.act(lambda e, bk=bk, n=n, cols=cols: e.activation(out=KhT.t[:, cols], in_=bk.t[:, 0:n], func=AF.Identity, scale=gv.t[:, 1:2]), [bk.b, gv.b], [KhT.b])
                        for g in range(9):
                            tl_ = list(range(4 * g, min(4 * g + 4, NT)))
                            bk = pb()
                            for jj, j in enumerate(tl_):
                                P.pe(lambda e, bk=bk, jj=jj, j=j, h=h: e.matmul(bk.t[:, jj * 128:(jj + 1) * 128], lhsT=ckvnT.t[:, j * 128:(j + 1) * 128],
                                                                               rhs=wkvs4.t[:, h * 256 + 128:h * 256 + 256], start=True, stop=True), [wkvs4.b, ckvnT.b], [bk.b])
                            nn = len(tl_)
                            P.act(lambda e, bk=bk, g=g, nn=nn: e.activation(out=Vh.t[:, 4 * g:4 * g + nn, :], in_=bk.t[:, 0:nn * 128].rearrange("p (j n) -> p j n", j=nn), func=AF.Copy),
                                  [bk.b], [Vh.b])
                        for (tg, t0, n, keys) in qtiles:
                            qp = qctr % 2
                            qctr += 1
                            bO, bS = acc_banks[qp]
                            qs = slice(t0, t0 + n)
                            b1, b2, b3 = pb(), pb(), pb()
                            for (bq_, c0, m_) in ((b1, 0, 128), (b2, 128, 64), (b3, 192, 64)):
                                for c in range(3):
                                    P.pe(lambda e, bq_=bq_, c0=c0, m_=m_, c=c, n=n, qs=qs, h=h: e.matmul(bq_.t[0:m_, 0:n], lhsT=wqs.t[:, c, h * 256 + c0:h * 256 + c0 + m_], rhs=cqnT.t[:, c, qs],
                                                                                                        start=(c == 0), stop=(c == 2)), [wqs.b, cqnT.b], [bq_.b])
                            P.act(lambda e, b1=b1, n=n: e.activation(out=sq1.t[:, 0:n], in_=b1.t[:, 0:n], func=AF.Square), [b1.b], [sq1.b])
                            P.act(lambda e, b2=b2, n=n: e.activation(out=sq2.t[:, 0:n], in_=b2.t[0:64, 0:n], func=AF.Square), [b2.b], [sq2.b])
                            b4 = pb()
                            P.pe(lambda e, b4=b4, n=n: e.matmul(b4.t[:, 0:n], lhsT=ones.t[:, :], rhs=sq1.t[:, 0:n], start=True, stop=False), [ones.b, sq1.b], [b4.b])
                            P.pe(lambda e, b4=b4, n=n: e.matmul(b4.t[:, 0:n], lhsT=ones.t[0:64, :], rhs=sq2.t[:, 0:n], start=False, stop=True), [ones.b, sq2.b], [b4.b])
                            P.act(lambda e, b4=b4, n=n, qp=qp: e.activation(out=rq.t[:, qp, 0:n], in_=b4.t[:, 0:n], func=AF.Sqrt, bias=EPS, scale=1.0 / 192), [b4.b], [rq.b[qp]])
                            P.dve(lambda e, n=n, qp=qp: e.reciprocal(out=rq.t[:, qp, 0:n], in_=rq.t[:, qp, 0:n]), [rq.b[qp]], [rq.b[qp]])
                            P.dve(lambda e, b1=b1, n=n, qp=qp: e.scalar_tensor_tensor(out=QnT.t[:, qp, 0:n], in0=b1.t[:, 0:n], scalar=gv.t[:, 0:1], in1=rq.t[:, qp, 0:n],
                                                                                      op0=ALU.mult, op1=ALU.mult), [b1.b, gv.b, rq.b[qp]], [QnT.b[qp]])
                            P.dve(lambda e, b2=b2, n=n, qs=qs: e.scalar_tensor_tensor(out=t12.t[:, 0, 0:n], in0=b2.t[0:64, 0:n], scalar=gv.t[0:64, 2:3], in1=cosT.t[:, qs],
                                                                                      op0=ALU.mult, op1=ALU.mult), [b2.b, gv.b, cosT.b], [t12.b])
                            P.dve(lambda e, b3=b3, n=n, qs=qs: e.scalar_tensor_tensor(out=t12.t[:, 1, 0:n], in0=b3.t[0:64, 0:n], scalar=gv.t[0:64, 3:4], in1=sinT.t[:, qs],
                                                                                      op0=ALU.mult, op1=ALU.mult), [b3.b, gv.b, sinT.b], [t12.b])
                            P.pool(lambda e, n=n: e.tensor_tensor(out=t12.t[:, 0, 0:n], in0=t12.t[:, 0, 0:n], in1=t12.t[:, 1, 0:n], op=ALU.add), [t12.b], [t12.b])
                            P.pool(lambda e, n=n, qp=qp: e.tensor_tensor(out=QrT.t[:, qp, 0:n], in0=t12.t[:, 0, 0:n], in1=rq.t[0:64, qp, 0:n], op=ALU.mult),
                                   [t12.b, rq.b[qp]], [QrT.b[qp]])
                            nk = len(keys)
                            sbanks = {}
                            for jj in range(nk + 1):
                                if jj < nk:
                                    j = keys[jj]
                                    bs = pb()
                                    sbanks[jj] = bs
                                    ks = slice(j * 128, (j + 1) * 128)
                                    P.pe(lambda e, bs=bs, n=n, ks=ks, qp=qp: e.matmul(bs.t[:, 0:n], lhsT=KhT.t[:, ks], rhs=QnT.t[:, qp, 0:n], start=True, stop=False),
                                         [KhT.b, QnT.b[qp]], [bs.b])
                                    P.pe(lambda e, bs=bs, n=n, ks=ks, qp=qp: e.matmul(bs.t[:, 0:n], lhsT=krT.t[:, ks], rhs=QrT.t[:, qp, 0:n], start=False, stop=True),
                                         [krT.b, QrT.b[qp]], [bs.b])
                                    sl = jj % 4
                                    P.act(lambda e, bs=bs, n=n, sl=sl, j=j, h=h: e.activation(out=PT.t[:, sl, 0:n], in_=bs.t[:, 0:n], func=AF.Exp, scale=rk_s.t[:, j, h:h + 1]),
                                          [bs.b, rk_s.b], [PT.b[sl]])
                                if jj >= 1:
                                    j = keys[jj - 1]
                                    sl = (jj - 1) % 4
                                    P.pe(lambda e, bO=bO, n=n, sl=sl, j=j, jj=jj, nk=nk: e.matmul(bO.t[:, 0:n], lhsT=Vh.t[:, j, :], rhs=PT.t[:, sl, 0:n], start=(jj == 1), stop=(jj == nk)),
                                         [Vh.b, PT.b[sl]], [bO.b])
                                    P.pe(lambda e, bS=bS, n=n, sl=sl, jj=jj, nk=nk: e.matmul(bS.t[:, 0:n], lhsT=ones.t[:, :], rhs=PT.t[:, sl, 0:n], start=(jj == 1), stop=(jj == nk)),
                                         [ones.b, PT.b[sl]], [bS.b])
                            P.dve(lambda e, bS=bS, n=n: e.reciprocal(out=rs.t[:, 0:n], in_=bS.t[:, 0:n]), [bS.b], [rs.b])
                            P.dve(lambda e, bO=bO, n=n, qp=qp: e.tensor_tensor(out=oc.t[:, qp, 0:n], in0=bO.t[:, 0:n], in1=rs.t[:, 0:n], op=ALU.mult), [bO.b, rs.b], [oc.b[qp]])
                            P.st(lambda e, tg=tg, h=h, n=n, qp=qp: e.dma_start(out=ocT_d[tg][:, h * 512:h * 512 + n], in_=oc.t[:, qp, 0:n]), [oc.b[qp]], [ocT_db[tg]])
                    reserved.clear()
                s14.__exit__(None, None, None)
                if dbg == 4:
                    out_ops.append(P.ld(lambda e: e.dma_start(out=out[1], in_=xin[3]), hT_db + of_db + ybT_db + ocT_db, []))
                    return True

                with Scope(AR) as s5:
                    wout = s5.sb("wout", [128, 8, 1024], BF16)
                    cw = s5.sb("cw", [128, 24], F32)
                    P.wld(lambda e: e.dma_start(out=wout.t[:], in_=w_out[l].rearrange("(c p) n -> p c n", p=128)), [], [wout.b])
                    P.ld(lambda e: e.dma_start(out=cw.t[:], in_=conv_wT[l]), [], [cw.b])
                    hg = s5.sb("hg", [128, 8, 768], BF16)
                    ya = s5.sb("ya", [128, 8, 512], BF16)
                    yb5 = s5.sb("yb5", [128, 8, 512], BF16)
                    oc5 = s5.sb("oc5", [128, 8, 512], BF16)
                    yc = s5.sb("yc", [128, 8, 512], BF16)
                    mm_ = s5.sb("mm_", [128, 8, 512], BF16)
                    wa4 = s5.sb("wa4", [128, 2, 4, 8, 128], BF16, nb=2)
                    wcz = s5.sb("wcz", [128, 2, 8, 128], BF16, nb=2)
                    wbg = s5.sb("wbg", [128, 2, 6, 8, 128], BF16, nb=2)
                    pext = s5.sb("pext", [128, 514], F32)
                    av = s5.sb("av", [128, 512], F32)
                    avh = s5.sb("avh", [128, 2], F32)
                    szz = s5.sb("szz", [128, 512], F32)
                    c1 = s5.sb("c1", [128, 512], F32)
                    c2 = s5.sb("c2", [128, 512], F32)
                    sig = s5.sb("sig", [128, 2, 512], F32, nb=2)
                    acc = s5.sb("acc", [128, 512], F32)
                    tt = s5.sb("tt", [128, 512], F32)
                    xt5 = s5.sb("xt5", [128, 2, 1024], F32, nb=2)
                    xo = s5.sb("xo", [128, 2, 1024], F32, nb=2)
                    fctr = hctr = octr = xctr = 0
                    tgs = ([] if last else [(0, [0, 1])]) + [(1 + q, list(range(2 + 4 * q, 6 + 4 * q))) for q in range(8)]
                    a_offs = (O_AV, O_AB, O_AC, O_AZ)
                    g_offs = (O_GA, O_GB, O_GC)
                    for (tg, tl_) in tgs:
                        n = 128 * len(tl_)
                        who = 0 if tg == 0 else 1
                        for jj, i in enumerate(tl_):
                            P.ld(lambda e, i=i, jj=jj: e.dma_start(out=hg.t[:, :, 128 + jj * 128:256 + jj * 128], in_=hT_d[i].rearrange("p (c n) -> p c n", c=8)), [hT_db[i]], [hg.b])
                        has_l = tl_[0] not in (0, NCTX)
                        has_r = tl_[-1] not in (NCTX - 1, NT - 1)
                        if has_l:
                            P.ld(lambda e, i=tl_[0] - 1: e.dma_start(out=hg.t[:, :, 0:128], in_=hT_d[i].rearrange("p (c n) -> p c n", c=8)), [hT_db[tl_[0] - 1]], [hg.b])
                        if has_r:
                            P.ld(lambda e, i=tl_[-1] + 1, n=n: e.dma_start(out=hg.t[:, :, 128 + n:256 + n], in_=hT_d[i].rearrange("p (c n) -> p c n", c=8)), [hT_db[tl_[-1] + 1]], [hg.b])
                        for jj, i in enumerate(tl_):
                            P.ld(lambda e, i=i, jj=jj: e.dma_start(out=yb5.t[:, :, jj * 128:(jj + 1) * 128], in_=ybT_d[i].rearrange("p (c n) -> p c n", c=8)), [ybT_db[i]], [yb5.b])
                        P.ld(lambda e, tg=tg, n=n: e.dma_start(out=oc5.t[:, :, 0:n], in_=ocT_d[tg].rearrange("p (c n) -> p c n", c=8)[:, :, 0:n]), [ocT_db[tg]], [oc5.b])
                        hmain = lambda k, n=n: hg.t[:, k, 128:128 + n]
                        for f in range(8):
                            fp = fctr % 2
                            fctr += 1
                            for w_ in range(4):
                                P.wld(lambda e, w_=w_, f=f, fp=fp: e.dma_start(out=wa4.t[:, fp, w_], in_=w_in[l, :, a_offs[w_] + f * 128:a_offs[w_] + (f + 1) * 128].rearrange("(c p) n -> p c n", p=128)),
                                      [], [wa4.b[fp]])
                            bC, bV, bH = pb(), pb(), pb()
                            for (bk, w_) in ((bC, 2), (bV, 0)):
                                for k in range(8):
                                    P.pe(lambda e, bk=bk, w_=w_, k=k, n=n, fp=fp: e.matmul(bk.t[:, 0:n], lhsT=wa4.t[:, fp, w_, k, :], rhs=hg.t[:, k, 128:128 + n], start=(k == 0), stop=(k == 7)),
                                         [wa4.b[fp], hg.b], [bk.b])
                            for (c0, w_) in ((0, 2), (2, 0)):
                                for k in range(8):
                                    P.pe(lambda e, c0=c0, w_=w_, k=k, n=n, fp=fp, bH=bH: e.matmul(bH.t[:, c0:c0 + 2], lhsT=wa4.t[:, fp, w_, k, :], rhs=hg.t[:, k, 127:127 + n + 2:n + 1],
                                                                                                 start=(k == 0), stop=(k == 7)), [wa4.b[fp], hg.b], [bH.b])
                            P.act(lambda e, bV=bV, n=n: e.activation(out=av.t[:, 0:n], in_=bV.t[:, 0:n], func=AF.Copy), [bV.b], [av.b])
                            P.act(lambda e, bH=bH: e.activation(out=avh.t[:, 0:2], in_=bH.t[:, 2:4], func=AF.Copy), [bH.b], [avh.b])
                            P.dve(lambda e, bC=bC, n=n: e.tensor_tensor(out=pext.t[:, 1:1 + n], in0=bC.t[:, 0:n], in1=av.t[:, 0:n], op=ALU.mult), [bC.b, av.b], [pext.b])
                            P.dve(lambda e, bH=bH, n=n: e.tensor_tensor(out=pext.t[:, 0:n + 2:n + 1], in0=bH.t[:, 0:2], in1=avh.t[:, 0:2], op=ALU.mult), [bH.b, avh.b], [pext.b])
                            if not has_l:
                                P.pool(lambda e: e.memset(pext.t[:, 0:1], 0.0), [], [pext.b])
                            if not has_r:
                                P.pool(lambda e, n=n: e.memset(pext.t[:, n + 1:n + 2], 0.0), [], [pext.b])
                            bB, bZ = pb(), pb()
                            for (bk, w_) in ((bB, 1), (bZ, 3)):
                                for k in range(8):
                                    P.pe(lambda e, bk=bk, w_=w_, k=k, n=n, fp=fp: e.matmul(bk.t[:, 0:n], lhsT=wa4.t[:, fp, w_, k, :], rhs=hg.t[:, k, 128:128 + n], start=(k == 0), stop=(k == 7)),
                                         [wa4.b[fp], hg.b], [bk.b])
                            P.act(lambda e, bZ=bZ, n=n: e.activation(out=szz.t[:, 0:n], in_=bZ.t[:, 0:n], func=AF.Silu), [bZ.b], [szz.b])
                            P.dve(lambda e, n=n, f=f: e.tensor_scalar(out=c1.t[:, 0:n], in0=pext.t[:, 1:1 + n], scalar1=cw.t[:, f * 3 + 1:f * 3 + 2], scalar2=None, op0=ALU.mult),
                                  [pext.b, cw.b], [c1.b])
                            P.dve(lambda e, n=n, f=f: e.scalar_tensor_tensor(out=c1.t[:, 0:n], in0=pext.t[:, 0:n], scalar=cw.t[:, f * 3:f * 3 + 1], in1=c1.t[:, 0:n], op0=ALU.mult, op1=ALU.add),
                                  [pext.b, cw.b, c1.b], [c1.b])
                            P.dve(lambda e, n=n, f=f: e.scalar_tensor_tensor(out=c1.t[:, 0:n], in0=pext.t[:, 2:2 + n], scalar=cw.t[:, f * 3 + 2:f * 3 + 3], in1=c1.t[:, 0:n], op0=ALU.mult, op1=ALU.add),
                                  [pext.b, cw.b, c1.b], [c1.b])
                            P.dve(lambda e, bB=bB, n=n: e.tensor_tensor(out=c2.t[:, 0:n], in0=bB.t[:, 0:n], in1=c1.t[:, 0:n], op=ALU.mult), [bB.b, c1.b], [c2.b])
                            P.pool(lambda e, n=n, f=f: e.tensor_tensor(out=ya.t[:, f, 0:n], in0=c2.t[:, 0:n], in1=szz.t[:, 0:n], op=ALU.mult), [c2.b, szz.b], [ya.b])
                        for h in range(8):
                            hp = hctr % 2
                            hctr += 1
                            P.wld(lambda e, h=h, hp=hp: e.dma_start(out=wcz.t[:, hp], in_=w_in[l, :, O_CZ + h * 128:O_CZ + (h + 1) * 128].rearrange("(c p) n -> p c n", p=128)), [], [wcz.b[hp]])
                            bz = pb()
                            for k in range(8):
                                P.pe(lambda e, bz=bz, k=k, n=n, hp=hp: e.matmul(bz.t[:, 0:n], lhsT=wcz.t[:, hp, k, :], rhs=hg.t[:, k, 128:128 + n], start=(k == 0), stop=(k == 7)),
                                     [wcz.b[hp], hg.b], [bz.b])
                            P.act(lambda e, bz=bz, n=n: e.activation(out=szz.t[:, 0:n], in_=bz.t[:, 0:n], func=AF.Silu), [bz.b], [szz.b])
                            P.pool(lambda e, n=n, h=h: e.tensor_tensor(out=yc.t[:, h, 0:n], in0=oc5.t[:, h, 0:n], in1=szz.t[:, 0:n], op=ALU.mult), [oc5.b, szz.b], [yc.b])
                        for o in range(8):
                            op_ = octr % 2
                            octr += 1
                            for br in range(3):
                                P.wld(lambda e, br=br, o=o, op_=op_: e.dma_start(out=wbg.t[:, op_, br], in_=w_br[br][l, :, o * 128:(o + 1) * 128].rearrange("(c p) n -> p c n", p=128)), [], [wbg.b[op_]])
                                P.wld(lambda e, br=br, o=o, op_=op_: e.dma_start(out=wbg.t[:, op_, 3 + br], in_=w_in[l, :, g_offs[br] + o * 128:g_offs[br] + (o + 1) * 128].rearrange("(c p) n -> p c n", p=128)),
                                      [], [wbg.b[op_]])
                            for br in range(3):
                                src = (ya, yb5, yc)[br]
                                bg, bb_ = pb(), pb()
                                for k in range(8):
                                    P.pe(lambda e, bg=bg, k=k, n=n, op_=op_, br=br: e.matmul(bg.t[:, 0:n], lhsT=wbg.t[:, op_, 3 + br, k, :], rhs=hg.t[:, k, 128:128 + n], start=(k == 0), stop=(k == 7)),
                                         [wbg.b[op_], hg.b], [bg.b])
                                P.act(lambda e, bg=bg, n=n, br=br: e.activation(out=sig.t[:, br % 2, 0:n], in_=bg.t[:, 0:n], func=AF.Sigmoid), [bg.b], [sig.b[br % 2]])
                                for k in range(8):
                                    P.pe(lambda e, bb_=bb_, k=k, n=n, op_=op_, br=br, src=src: e.matmul(bb_.t[:, 0:n], lhsT=wbg.t[:, op_, br, k, :], rhs=src.t[:, k, 0:n], start=(k == 0), stop=(k == 7)),
                                         [wbg.b[op_], src.b], [bb_.b])
                                if br == 0:
                                    P.dve(lambda e, bb_=bb_, n=n, br=br: e.tensor_tensor(out=acc.t[:, 0:n], in0=bb_.t[:, 0:n], in1=sig.t[:, br % 2, 0:n], op=ALU.mult), [bb_.b, sig.b[br % 2]], [acc.b])
                                else:
                                    P.dve(lambda e, bb_=bb_, n=n, br=br: e.tensor_tensor(out=tt.t[:, 0:n], in0=bb_.t[:, 0:n], in1=sig.t[:, br % 2, 0:n], op=ALU.mult), [bb_.b, sig.b[br % 2]], [tt.b])
                                    if br == 1:
                                        P.pool(lambda e, n=n: e.tensor_tensor(out=acc.t[:, 0:n], in0=acc.t[:, 0:n], in1=tt.t[:, 0:n], op=ALU.add), [acc.b, tt.b], [acc.b])
                                    else:
                                        P.pool(lambda e, n=n, o=o: e.tensor_tensor(out=mm_.t[:, o, 0:n], in0=acc.t[:, 0:n], in1=tt.t[:, 0:n], op=ALU.add), [acc.b, tt.b], [mm_.b])
                        for jj, i in enumerate(tl_):
                            xp = xctr % 2
                            xctr += 1
                            P.ld(lambda e, i=i, xp=xp: e.dma_start(out=xt5.t[:, xp], in_=x_src[i]), [xs_b[i]] if l > 0 else [], [xt5.b[xp]])
                            for half in range(2):
                                bo = pb()
                                hs = slice(half * 512, (half + 1) * 512)
                                for o in range(8):
                                    P.pe(lambda e, bo=bo, o=o, jj=jj, hs=hs: e.matmul(bo.t[:], lhsT=mm_.t[:, o, jj * 128:(jj + 1) * 128], rhs=wout.t[:, o, hs], start=(o == 0), stop=(o == 7)),
                                         [mm_.b, wout.b], [bo.b])
                                P.dve(lambda e, bo=bo, hs=hs, xp=xp, who=who: e.tensor_tensor(out=xo.t[:, xp, hs], in0=bo.t[:], in1=G.t[:, who, hs], op=ALU.mult), [bo.b, G.b], [xo.b[xp]])
                                P.pool(lambda e, hs=hs, xp=xp: e.tensor_tensor(out=xo.t[:, xp, hs], in0=xo.t[:, xp, hs], in1=xt5.t[:, xp, hs], op=ALU.add), [xo.b[xp], xt5.b[xp]], [xo.b[xp]])
                            if l == L - 1 and i >= NCTX:
                                out_ops.append(P.st(lambda e, i=i, xp=xp: e.dma_start(out=out[i - NCTX], in_=xo.t[:, xp]), [xo.b[xp]], []))
                            else:
                                P.st(lambda e, i=i, xp=xp: e.dma_start(out=xs[i], in_=xo.t[:, xp]), [xo.b[xp]], [xs_b[i]])
            return False

        for l_ in range(L):
            if emit_layer(l_):
                break
        if dbg in (1, 3, 4):
            dummy = P.ld(lambda e: e.dma_start(out=out[0], in_=xin[2]), [], [])
            out_ops.append(dummy)
        P.emit(out_ops)
    return nc


_ROPE_PERM = np.concatenate([np.arange(16, 32), np.arange(0, 16), np.arange(48, 64), np.arange(32, 48)])
_ROPE_SIGN = np.concatenate([-np.ones(16), np.ones(16), -np.ones(16), np.ones(16)]).astype(np.float32)


def _rope_tables():
    rows = 4096 // 64
    row = np.repeat(np.arange(rows, dtype=np.float32), 64)
    col = np.tile(np.arange(64, dtype=np.float32), rows)
    n_freq = 16
    freqs = (np.float32(10000.0) ** (-np.arange(n_freq, dtype=np.float32) / n_freq)).astype(np.float32)
    ang_r = row[:, None] * freqs[None, :]
    ang_c = col[:, None] * freqs[None, :]
    ang = np.concatenate([ang_r, ang_r, ang_c, ang_c], axis=-1)
    cos = np.ones((T, 64), np.float32)
    sin = np.zeros((T, 64), np.float32)
    cos[NCTX * 128:] = np.cos(ang)
    sin[NCTX * 128:] = np.sin(ang) * _ROPE_SIGN[None, :]
    return np.ascontiguousarray(cos.T), np.ascontiguousarray(sin.T)


def _consts():
    j = np.arange(128)[:, None]
    i = np.arange(128)[None, :]
    s = np.float32(-1.0 / 16.0)
    tri = np.stack([(j <= i), (j > i), (j >= i), (j < i)]).astype(np.float32) * s
    mask = np.stack([(j <= i), (j >= i)]).astype(np.float32)
    cosT, sinT = _rope_tables()
    return dict(ident=np.eye(128, dtype=np.float32), tri=tri, mask=mask, cosT=cosT, sinST=sinT)


def prep_inputs(inp, LW=DEPTH):
    f = lambda a: np.ascontiguousarray(np.asarray(a, dtype=np.float32))
    w_in = f(inp["w_in"])
    shared = dict(
        w_mod=f(inp["w_mod"]), b_mod=f(inp["b_mod"]), norm_g=f(inp["norm_g"]), w_in=w_in,
        w_krp=np.ascontiguousarray(w_in[:, :, O_CKR:O_CKR + 64][:, :, _ROPE_PERM]),
        conv_wT=np.ascontiguousarray(f(inp["conv_w"]).reshape(DEPTH, 3, 8, 128).transpose(0, 3, 2, 1).reshape(DEPTH, 128, 24)),
        wa_f=np.ascontiguousarray(np.concatenate([f(inp["gla_wa_up_f"]), f(inp["gla_ba_f"])[:, None, :]], axis=1)),
        wa_b=np.ascontiguousarray(np.concatenate([f(inp["gla_wa_up_b"]), f(inp["gla_ba_b"])[:, None, :]], axis=1)),
        gla_norm_g=f(inp["gla_norm_g"]), mla_q_norm_g=f(inp["mla_q_norm_g"]), mla_kv_norm_g=f(inp["mla_kv_norm_g"]),
        wkv=f(inp["mla_wkv_up"]),
        w_br_a=f(inp["w_br_a"]), w_br_b=f(inp["w_br_b"]), w_br_c=f(inp["w_br_c"]), w_out=f(inp["w_out"]),
    )
    wq = f(inp["mla_wq_up"]).reshape(DEPTH, 384, 8, 192)
    shared["wq_aug"] = np.ascontiguousarray(np.concatenate([wq, wq[..., 128:192][..., _ROPE_PERM]], axis=-1).reshape(DEPTH, 384, 2048))
    qg, kg = f(inp["mla_qn_g"]), f(inp["mla_kn_g"])
    gv = np.zeros((DEPTH, 128, 8), np.float32)
    gv[:, :, 0] = qg[:, 0:128]
    gv[:, :, 1] = kg[:, 0:128]
    gv[:, 0:64, 2] = qg[:, 128:192]
    gv[:, 0:64, 3] = qg[:, 128:192][:, _ROPE_PERM]
    gv[:, 0:64, 4] = kg[:, 128:192]
    gv[:, 0:64, 5] = kg[:, 128:192][:, _ROPE_PERM]
    shared["gvec"] = gv
    shared = {k: np.ascontiguousarray(v[:LW]) for k, v in shared.items()}
    shared.update(_consts())
    x, ctx, c, c_ctx = f(inp["x"]), f(inp["ctx"]), f(inp["c"]), f(inp["c_ctx"])
    maps = []
    for core in range(8):
        b = core % 4
        m = dict(shared)
        m["xin"] = np.ascontiguousarray(np.concatenate([ctx[b], x[b]], axis=0).reshape(NT, 128, D))
        cv = np.zeros((128, 16), np.float32)
        cv[:, 0:8] = c[b].reshape(8, 128).T
        cv[:, 8:16] = c_ctx.reshape(8, 128).T
        m["cvec"] = cv
        maps.append(m)
    return maps


_NC_CACHE = {}


def kernel(**inputs):
    if "nc" not in _NC_CACHE:
        _NC_CACHE["nc"] = build_nc()
    nc = _NC_CACHE["nc"]
    maps = prep_inputs(inputs)
    res = run_bass_kernel_spmd(nc, maps, core_ids=list(range(8)))
    outs = [np.asarray(res.results[b]["out"], dtype=np.float32).reshape(NXT * 128, D) for b in range(4)]
    return np.stack(outs, axis=0)
```

```python
import contextlib
import numpy as np
import concourse.bass as bass
import concourse.mybir as mybir
from concourse.bass_utils import run_bass_kernel_spmd

F32 = mybir.dt.float32
BF16 = mybir.dt.bfloat16
ALU = mybir.AluOpType
AF = mybir.ActivationFunctionType
AX = mybir.AxisListType

D = 1024
DEPTH = 4
NCTX = 2
NXT = 32
NT = NCTX + NXT
T = NT * 128
EPS = 1e-6
IN_DIM = 11872
O_AV, O_AB, O_AC, O_AZ = 0, 1024, 2048, 3072
O_BQ, O_BK, O_BV, O_BZ, O_AF, O_ABW = 4096, 4608, 5120, 6144, 7168, 7184
O_CQ, O_CKV, O_CKR, O_CZ = 7200, 7584, 7712, 7776
O_GA, O_GB, O_GC = 8800, 9824, 10848

SEM_PERIOD = 30000
NDMA_SLOTS = 8


class Buf:
    __slots__ = ("name", "excl", "last_w", "readers")

    def __init__(self, name, excl=False):
        self.name = name
        self.excl = excl
        self.last_w = None
        self.readers = []


class Op:
    __slots__ = ("eng", "fn", "reads", "writes", "dma", "deps", "signal", "tick", "idx")


class Prog:
    ENGS = ("pe", "act", "dve", "pool", "sp")

    def __init__(self, nc):
        self.nc = nc
        self.ops = []

    def add(self, eng, fn, reads=(), writes=(), dma=False):
        op = Op()
        op.eng, op.fn, op.dma = eng, fn, dma
        op.reads, op.writes = list(reads), list(writes)
        op.deps, op.signal, op.tick = [], False, None
        op.idx = len(self.ops)
        deps = {}
        for b in op.reads:
            if not b.excl and b.last_w is not None:
                deps[b.last_w.idx] = b.last_w
        wr = list(op.writes) + [b for b in op.reads if b.excl]
        for b in wr:
            if b.last_w is not None:
                deps[b.last_w.idx] = b.last_w
            for r in b.readers:
                deps[r.idx] = r
        for b in op.reads:
            if not b.excl:
                if not op.dma:
                    b.readers = [r for r in b.readers if r.dma or r.eng != op.eng]
                b.readers.append(op)
        for b in wr:
            b.last_w = op
            b.readers = []
        for d in deps.values():
            if d is op:
                continue
            if d.eng == "pe" and op.eng == "pe":
                continue
            op.deps.append(d)
            d.signal = True
        self.ops.append(op)
        return op

    def pe(self, fn, r=(), w=()):
        return self.add("pe", fn, r, w)

    def act(self, fn, r=(), w=()):
        return self.add("act", fn, r, w)

    def dve(self, fn, r=(), w=()):
        return self.add("dve", fn, r, w)

    def pool(self, fn, r=(), w=()):
        return self.add("pool", fn, r, w)

    def ld(self, fn, r=(), w=()):
        return self.add("sp", fn, r, w, dma=True)

    def wld(self, fn, r=(), w=()):
        return self.add("pool", fn, r, w, dma=True)

    def st(self, fn, r=(), w=()):
        return self.add("act", fn, r, w, dma=True)

    def emit(self, final_ops=()):
        nc = self.nc
        for o in final_ops:
            o.signal = True
        n_ticks = {e: 0 for e in self.ENGS}
        for op in self.ops:
            if op.signal and not op.dma:
                n_ticks[op.eng] += 1
        stack = contextlib.ExitStack()
        eng_sems, dma_sems = {}, {}
        for e in self.ENGS:
            n = max(1, (n_ticks[e] + SEM_PERIOD - 1) // SEM_PERIOD)
            eng_sems[e] = [stack.enter_context(nc.semaphore(f"s_{e}_{i}")) for i in range(n)]
            if any(op.dma and op.eng == e for op in self.ops):
                dma_sems[e] = [stack.enter_context(nc.semaphore(f"d_{e}_{i}")) for i in range(NDMA_SLOTS)]
        cnt = {e: 0 for e in self.ENGS}
        dcnt = {e: 0 for e in self.ENGS}
        for op in self.ops:
            if op.dma:
                k = dcnt[op.eng]
                dcnt[op.eng] += 1
                slot = k % NDMA_SLOTS
                op.tick = (dma_sems[op.eng][slot], 16 * (k // NDMA_SLOTS + 1), ("d", op.eng, slot))
            elif op.signal:
                k = cnt[op.eng]
                cnt[op.eng] += 1
                si = k // SEM_PERIOD
                op.tick = (eng_sems[op.eng][si], k % SEM_PERIOD + 1, ("e", op.eng, si))
        by_eng = {e: [op for op in self.ops if op.eng == e] for e in self.ENGS}
        final = list(final_ops)

        def run_engine(ename, eng):
            waited = {}
            for op in by_eng[ename]:
                needs = {}
                for d in op.deps:
                    sem, val, key = d.tick
                    if waited.get(key, 0) >= val:
                        continue
                    if key not in needs or needs[key][1] < val:
                        needs[key] = (sem, val)
                if op.dma:
                    sem, val, key = op.tick
                    if val > 16 and waited.get(key, 0) < val - 16:
                        if key not in needs or needs[key][1] < val - 16:
                            needs[key] = (sem, val - 16)
                for key, (sem, val) in needs.items():
                    eng.wait_ge(sem, val)
                    waited[key] = val
                ins = op.fn(eng)
                if op.dma:
                    ins.then_inc(op.tick[0], 16)
                elif op.signal:
                    ins.then_inc(op.tick[0], 1)
            if ename == "sp":
                for o in final:
                    sem, val, key = o.tick
                    eng.wait_ge(sem, val)

        with stack:
            with nc.Block() as block:
                @block.tensor
                def _(e):
                    run_engine("pe", e)

                @block.scalar
                def _(e):
                    run_engine("act", e)

                @block.vector
                def _(e):
                    run_engine("dve", e)

                @block.gpsimd
                def _(e):
                    run_engine("pool", e)

                @block.sync
                def _(e):
                    run_engine("sp", e)


class Tl:
    __slots__ = ("t", "b")

    def __init__(self, t, b):
        self.t, self.b = t, b


class Arena:
    def __init__(self, nc, stack, nbytes):
        self.a = stack.enter_context(nc.sbuf_tensor("arena", [128, nbytes // 2], BF16))
        self.nbytes = nbytes
        self.top = 0
        self.live = []
        self.dead = []

    def alloc(self, name, shape, dt, nb=0):
        esz = 4 if dt == F32 else 2
        n = 1
        for d_ in shape[1:]:
            n *= d_
        nbytes = (n * esz + 63) // 64 * 64
        off = self.top
        self.top += nbytes
        assert self.top <= self.nbytes, f"SBUF arena overflow at {name}: {self.top}"
        v = self.a[0:shape[0], off // 2:off // 2 + n * esz // 2]
        if dt == F32:
            v = v.bitcast(F32)
        if len(shape) > 2:
            names = "abcdef"[:len(shape) - 1]
            kw = {names[i]: shape[i + 1] for i in range(len(shape) - 2)}
            v = v.rearrange(f"p ({' '.join(names)}) -> p {' '.join(names)}", **kw)
        inherit = {}
        for (o0, o1, ops) in self.dead:
            if o0 < off + nbytes and off < o1:
                for op in ops:
                    inherit[op.idx] = op
        bufs = [Buf(f"{name}{i}") for i in range(nb)] if nb else [Buf(name)]
        for b in bufs:
            b.readers = list(inherit.values())
        self.live.append((off, off + nbytes, bufs))
        return Tl(v, bufs if nb else bufs[0])

    def release(self, mark):
        keep = []
        for (o0, o1, bufs) in self.live:
            if o0 >= mark:
                ops = {}
                for b in bufs:
                    if b.last_w is not None:
                        ops[b.last_w.idx] = b.last_w
                    for r in b.readers:
                        ops[r.idx] = r
                self.dead.append((o0, o1, list(ops.values())))
            else:
                keep.append((o0, o1, bufs))
        self.live = keep
        self.top = mark


class Scope:
    def __init__(self, ar):
        self.ar = ar

    def __enter__(self):
        self.mark = self.ar.top
        return self

    def __exit__(self, *a):
        self.ar.release(self.mark)
        return False

    def sb(self, name, shape, dt, nb=0):
        return self.ar.alloc(name, shape, dt, nb)


def build_nc(L=DEPTH, dbg=False):
    LW = L
    nc = bass.Bass("TRN2", target_bir_lowering=False)
    P = Prog(nc)

    def din(name, shape, dt=F32):
        return nc.dram_tensor(name, shape, dt, kind="ExternalInput").ap()

    xin = din("xin", [NT, 128, D])
    cvec = din("cvec", [128, 16])
    w_mod = din("w_mod", [LW, D, 3 * D])
    b_mod = din("b_mod", [LW, 3 * D])
    norm_g = din("norm_g", [LW, D])
    w_in = din("w_in", [LW, D, IN_DIM])
    w_krp = din("w_krp", [LW, D, 64])
    conv_wT = din("conv_wT", [LW, 128, 24])
    wa_f = din("wa_f", [LW, 17, 512])
    wa_b = din("wa_b", [LW, 17, 512])
    gla_g = din("gla_norm_g", [LW, D])
    qn_g = din("mla_q_norm_g", [LW, 384])
    kvn_g = din("mla_kv_norm_g", [LW, 128])
    wq_aug = din("wq_aug", [LW, 384, 2048])
    wkv = din("wkv", [LW, 128, 2048])
    gvec = din("gvec", [LW, 128, 8])
    w_br = [din(f"w_br_{s}", [LW, D, D]) for s in "abc"]
    w_out = din("w_out", [LW, D, D])
    ident_d = din("ident", [128, 128])
    tri_d = din("tri", [4, 128, 128])
    mask_d = din("mask", [2, 128, 128])
    cos_d = din("cosT", [64, T])
    sin_d = din("sinST", [64, T])
    out = nc.dram_tensor("out", [NXT, 128, D], F32, kind="ExternalOutput").ap()

    skind = "ExternalOutput" if dbg else "Internal"
    xs = nc.dram_tensor("xs", [NT, 128, D], F32, kind=skind).ap()
    hT_d = nc.dram_tensor("hT_d", [NT, 128, D], BF16, kind=skind).ap()
    of_d = nc.dram_tensor("of_d", [NT, 128, D], BF16, kind=skind).ap()
    ybT_d = nc.dram_tensor("ybT_d", [NT, 128, D], BF16, kind=skind).ap()
    ocT_d = nc.dram_tensor("ocT_d", [9, 128, 8 * 512], BF16, kind=skind).ap()
    WA_d = nc.dram_tensor("WA_d", [2, 8, 128, 4, 8, 128], BF16, kind="Internal").ap()
    WC_d = nc.dram_tensor("WC_d", [2, 8, 128, 8, 128], BF16, kind="Internal").ap()
    WG_d = nc.dram_tensor("WG_d", [2, 8, 128, 6, 8, 128], BF16, kind="Internal").ap()
    wa_db = [[Buf(f"wad{p}_{i}") for i in range(32)] for p in range(2)]
    wc_db = [[Buf(f"wcd{p}_{i}") for i in range(8)] for p in range(2)]
    wg_db = [[Buf(f"wgd{p}_{i}") for i in range(48)] for p in range(2)]
    xs_b = [Buf(f"xs{i}") for i in range(NT)]
    hT_db = [Buf(f"hTd{i}") for i in range(NT)]
    of_db = [Buf(f"ofd{i}") for i in range(NT)]
    ybT_db = [Buf(f"ybTd{i}") for i in range(NT)]
    ocT_db = [Buf(f"ocTd{i}") for i in range(9)]
    out_ops = []

    gstack = contextlib.ExitStack()
    with gstack:
        AR = Arena(nc, gstack, 200 * 1024)
        top = Scope(AR)
        top.__enter__()
        banks = []
        for i in range(8):
            t = gstack.enter_context(nc.psum_tensor(f"bank{i}", [128, 512], F32))
            banks.append(Tl(t, Buf(f"bank{i}", excl=True)))
        bank_ctr = [0]
        reserved = set()

        def pb():
            while True:
                k = bank_ctr[0] % 8
                bank_ctr[0] += 1
                if k not in reserved:
                    return banks[k]

        def bf(bank):
            return bank.t[:].bitcast(BF16)

        ident = top.sb("ident", [128, 128], BF16)
        ones = top.sb("ones", [128, 128], BF16)
        zeros = top.sb("zeros", [128, 128], F32)
        ones32 = top.sb("ones32", [128, 128], F32)
        tri = top.sb("tri", [128, 4, 128], F32)
        mask = top.sb("mask", [128, 2, 128], F32)
        cosT = top.sb("cosT", [64, T], BF16)
        sinT = top.sb("sinT", [64, T], BF16)
        cv = top.sb("cv", [128, 16], F32)
        screp = top.sb("screp", [128, 16, 128], BF16)
        rk_s = top.sb("rk_s", [128, NT, 8], F32)
        P.wld(lambda e: e.dma_start(out=ident.t[:], in_=ident_d[:, :]), [], [ident.b])
        P.ld(lambda e: e.dma_start(out=tri.t[:], in_=tri_d.rearrange("a p n -> p a n")), [], [tri.b])
        P.ld(lambda e: e.dma_start(out=mask.t[:], in_=mask_d.rearrange("a p n -> p a n")), [], [mask.b])
        P.wld(lambda e: e.dma_start(out=cosT.t[:], in_=cos_d[:, :]), [], [cosT.b])
        P.wld(lambda e: e.dma_start(out=sinT.t[:], in_=sin_d[:, :]), [], [sinT.b])
        P.ld(lambda e: e.dma_start(out=cv.t[:], in_=cvec[:, :]), [], [cv.b])
        P.pool(lambda e: e.memset(ones.t[:], 1.0), [], [ones.b])
        P.pool(lambda e: e.memset(zeros.t[:], 0.0), [], [zeros.b])
        P.pool(lambda e: e.memset(ones32.t[:], 1.0), [], [ones32.b])
        P.act(lambda e: e.activation(out=cv.t[:], in_=cv.t[:], func=AF.Silu), [cv.b], [cv.b])
        for c in range(16):
            P.act(lambda e, c=c: e.activation(out=screp.t[:, c, :], in_=zeros.t[:], func=AF.Identity, bias=cv.t[:, c:c + 1]),
                  [cv.b, zeros.b], [screp.b])


        A_OFFS = (O_AV, O_AB, O_AC, O_AZ)
        G_OFFS = (O_GA, O_GB, O_GC)

        def precast_jobs(l):
            par = l % 2
            jobs = []
            for a in range(4):
                for k in range(8):
                    jobs.append((lambda e, a=a, k=k: e.dma_start(out=WA_d[par][:, :, a, k, :].rearrange("f p n -> p f n"),
                                                                 in_=w_in[l, k * 128:(k + 1) * 128, A_OFFS[a]:A_OFFS[a] + 1024].rearrange("p (f n) -> p f n", n=128)),
                                 wa_db[par][a * 8 + k]))
            for k in range(8):
                jobs.append((lambda e, k=k: e.dma_start(out=WC_d[par][:, :, k, :].rearrange("f p n -> p f n"),
                                                        in_=w_in[l, k * 128:(k + 1) * 128, O_CZ:O_CZ + 1024].rearrange("p (f n) -> p f n", n=128)),
                             wc_db[par][k]))
            for g in range(3):
                for k in range(8):
                    jobs.append((lambda e, g=g, k=k: e.dma_start(out=WG_d[par][:, :, 3 + g, k, :].rearrange("f p n -> p f n"),
                                                                 in_=w_in[l, k * 128:(k + 1) * 128, G_OFFS[g]:G_OFFS[g] + 1024].rearrange("p (f n) -> p f n", n=128)),
                                 wg_db[par][(3 + g) * 8 + k]))
                    jobs.append((lambda e, g=g, k=k: e.dma_start(out=WG_d[par][:, :, g, k, :].rearrange("f p n -> p f n"),
                                                                 in_=w_br[g][l, k * 128:(k + 1) * 128, :].rearrange("p (f n) -> p f n", n=128)),
                                 wg_db[par][g * 8 + k]))
            return jobs

        def issue_precast(jobs):
            for fn, b in jobs:
                P.wld(fn, [], [b])

        issue_precast(precast_jobs(0))

        def emit_layer(l):
            last = (l == DEPTH - 1)
            x_src = xin if l == 0 else xs
            with Scope(AR) as ly:
                G = ly.sb("G", [128, 2, D], F32)
                gv = ly.sb("gv", [128, 8], F32)
                P.ld(lambda e: e.dma_start(out=gv.t[:], in_=gvec[l]), [], [gv.b])
                s14 = Scope(AR)
                s14.__enter__()
                cqnT = s14.sb("cqnT", [128, 3, T], BF16)
                ckvnT = s14.sb("ckvnT", [128, T], BF16)
                krT = s14.sb("krT", [64, T], BF16)

                with Scope(AR) as s1:
                    AB = s1.sb("AB", [128, 4, D], F32)
                    grep = s1.sb("grep", [128, D], F32)
                    bmrep = s1.sb("bmrep", [128, 3 * D], F32)
                    gq_rep = s1.sb("gq_rep", [128, 512], F32)
                    P.ld(lambda e: e.dma_start(out=grep.t[:], in_=norm_g[l, :].partition_broadcast(128)), [], [grep.b])
                    P.ld(lambda e: e.dma_start(out=bmrep.t[:], in_=b_mod[l, :].partition_broadcast(128)), [], [bmrep.b])
                    P.ld(lambda e: e.dma_start(out=gq_rep.t[:, 0:384], in_=qn_g[l, :].partition_broadcast(128)), [], [gq_rep.b])
                    P.ld(lambda e: e.dma_start(out=gq_rep.t[:, 384:512], in_=kvn_g[l, :].partition_broadcast(128)), [], [gq_rep.b])
                    wm = s1.sb("wm", [128, 2, 8, 512], BF16, nb=2)
                    for n in range(6):
                        P.wld(lambda e, n=n: e.dma_start(out=wm.t[:, n % 2], in_=w_mod[l, :, n * 512:(n + 1) * 512].rearrange("(c p) n -> p c n", p=128)),
                              [], [wm.b[n % 2]])
                        for who in range(2):
                            bk = pb()
                            for k in range(8):
                                P.pe(lambda e, k=k, bk=bk, n=n, who=who: e.matmul(bk.t[:], lhsT=screp.t[:, (8 if who == 0 else 0) + k, :],
                                                                                  rhs=wm.t[:, n % 2, k, :], start=(k == 0), stop=(k == 7)),
                                     [screp.b, wm.b[n % 2]], [bk.b])
                            part, half = n // 2, n % 2
                            cs = slice(half * 512, (half + 1) * 512)
                            bms = bmrep.t[:, n * 512:(n + 1) * 512]
                            if part == 0:
                                P.dve(lambda e, bk=bk, who=who, cs=cs, bms=bms: e.tensor_tensor(out=AB.t[:, 2 * who + 1, cs], in0=bk.t[:], in1=bms, op=ALU.add),
                                      [bk.b, bmrep.b], [AB.b])
                            elif part == 1:
                                P.dve(lambda e, bk=bk, who=who, cs=cs, bms=bms: e.tensor_tensor(out=AB.t[:, 2 * who, cs], in0=bk.t[:], in1=bms, op=ALU.add),
                                      [bk.b, bmrep.b], [AB.b])
                                P.dve(lambda e, who=who, cs=cs: e.scalar_tensor_tensor(out=AB.t[:, 2 * who, cs], in0=AB.t[:, 2 * who, cs], scalar=1.0,
                                                                                       in1=grep.t[:, cs], op0=ALU.add, op1=ALU.mult),
                                      [AB.b, grep.b], [AB.b])
                            else:
                                P.dve(lambda e, bk=bk, who=who, cs=cs, bms=bms: e.tensor_tensor(out=G.t[:, who, cs], in0=bk.t[:], in1=bms, op=ALU.add),
                                      [bk.b, bmrep.b], [G.b])

                    wlat = s1.sb("wlat", [128, 8, 576], BF16)
                    wkrp = s1.sb("wkrp", [128, 8, 64], BF16)
                    wkvs = s1.sb("wkvs", [128, 2048], BF16)
                    P.wld(lambda e: e.dma_start(out=wlat.t[:], in_=w_in[l, :, O_CQ:O_CQ + 576].rearrange("(c p) n -> p c n", p=128)), [], [wlat.b])
                    P.wld(lambda e: e.dma_start(out=wkrp.t[:], in_=w_krp[l].rearrange("(c p) n -> p c n", p=128)), [], [wkrp.b])
                    P.wld(lambda e: e.dma_start(out=wkvs.t[:], in_=wkv[l]), [], [wkvs.b])

                    xt = s1.sb("xt", [128, 2, D], F32, nb=2)
                    junk = s1.sb("junk", [128, D], BF16)
                    st4 = s1.sb("st4", [128, 2, 8], F32, nb=2)
                    tnrm = s1.sb("tnrm", [128, 2, D], F32, nb=2)
                    hb = s1.sb("hb", [128, 2, D], BF16, nb=2)
                    hTt = s1.sb("hTt", [128, 2, 8, 128], BF16, nb=2)
                    cq = s1.sb("cq", [128, 2, 512], BF16, nb=2)
                    rt = s1.sb("rt", [64, 2, 2, 128], F32, nb=2)
                    sqk = s1.sb("sqk", [128, 2, D], F32, nb=2)
                    ssk = s1.sb("ssk", [128, 2, 8], F32, nb=2)
                    for i in range(NT):
                        p2 = i % 2
                        who = 0 if i < NCTX else 1
                        xb, sb_, tb, hbb, hTb, cqb, rtb, sqb, skb = (xt.b[p2], st4.b[p2], tnrm.b[p2], hb.b[p2], hTt.b[p2], cq.b[p2], rt.b[p2],
                                                                      sqk.b[p2], ssk.b[p2])
                        P.ld(lambda e, i=i, p2=p2: e.dma_start(out=xt.t[:, p2], in_=x_src[i]), [xs_b[i]] if l > 0 else [], [xb])
                        P.act(lambda e, p2=p2: e.activation(out=junk.t[:], in_=xt.t[:, p2], func=AF.Square, scale=1.0 / 32, accum_out=st4.t[:, p2, 0:1]),
                              [xb], [junk.b, sb_])
                        P.act(lambda e, p2=p2: e.activation(out=st4.t[:, p2, 1:2], in_=st4.t[:, p2, 0:1], func=AF.Sqrt, bias=EPS, scale=1.0), [sb_], [sb_])
                        P.dve(lambda e, p2=p2: e.reciprocal(out=st4.t[:, p2, 2:3], in_=st4.t[:, p2, 1:2]), [sb_], [sb_])
                        P.dve(lambda e, p2=p2, who=who: e.scalar_tensor_tensor(out=tnrm.t[:, p2], in0=xt.t[:, p2], scalar=st4.t[:, p2, 2:3],
                                                                                in1=AB.t[:, 2 * who], op0=ALU.mult, op1=ALU.mult),
                              [xb, sb_, AB.b], [tb])
                        P.pool(lambda e, p2=p2, who=who: e.tensor_tensor(out=hb.t[:, p2], in0=tnrm.t[:, p2], in1=AB.t[:, 2 * who + 1], op=ALU.add),
                               [tb, AB.b], [hbb])
                        bk = pb()
                        for c in range(8):
                            P.pe(lambda e, c=c, bk=bk, p2=p2: e.transpose(out=bf(bk)[:, c * 128:(c + 1) * 128], in_=hb.t[:, p2, c * 128:(c + 1) * 128],
                                                                           identity=ident.t[:]), [hbb, ident.b], [bk.b])
                        P.act(lambda e, bk=bk, p2=p2: e.activation(out=hTt.t[:, p2].rearrange("p c n -> p (c n)"), in_=bf(bk), func=AF.Copy),
                              [bk.b], [hTb])
                        P.st(lambda e, i=i, p2=p2: e.dma_start(out=hT_d[i], in_=hTt.t[:, p2].rearrange("p c n -> p (c n)")), [hTb], [hT_db[i]])
                        b1, b2 = pb(), pb()
                        for k in range(8):
                            P.pe(lambda e, k=k, b1=b1, p2=p2: e.matmul(b1.t[:], lhsT=hTt.t[:, p2, k, :], rhs=wlat.t[:, k, 0:512], start=(k == 0), stop=(k == 7)),
                                 [hTb, wlat.b], [b1.b])
                        for k in range(8):
                            P.pe(lambda e, k=k, b2=b2, p2=p2: e.matmul(b2.t[:, 0:64], lhsT=hTt.t[:, p2, k, :], rhs=wlat.t[:, k, 512:576], start=(k == 0), stop=(k == 7)),
                                 [hTb, wlat.b], [b2.b])
                        P.act(lambda e, b1=b1, p2=p2: e.activation(out=junk.t[:, 0:384], in_=b1.t[:, 0:384], func=AF.Square, scale=float(384 ** -0.5),
                                                                    accum_out=st4.t[:, p2, 3:4]), [b1.b], [junk.b, sb_])
                        P.act(lambda e, b1=b1, p2=p2: e.activation(out=junk.t[:, 384:512], in_=b1.t[:, 384:512], func=AF.Square, scale=float(128 ** -0.5),
                                                                    accum_out=st4.t[:, p2, 4:5]), [b1.b], [junk.b, sb_])
                        P.act(lambda e, b2=b2, p2=p2: e.activation(out=junk.t[:, 512:576], in_=b2.t[:, 0:64], func=AF.Square,
                                                                    accum_out=st4.t[:, p2, 7:8]), [b2.b], [junk.b, sb_])
                        P.act(lambda e, p2=p2: e.activation(out=st4.t[:, p2, 5:7], in_=st4.t[:, p2, 3:5], func=AF.Sqrt, bias=EPS, scale=1.0), [sb_], [sb_])
                        P.dve(lambda e, p2=p2: e.reciprocal(out=st4.t[:, p2, 5:7], in_=st4.t[:, p2, 5:7]), [sb_], [sb_])
                        P.dve(lambda e, b1=b1, p2=p2: e.scalar_tensor_tensor(out=cq.t[:, p2, 0:384], in0=b1.t[:, 0:384], scalar=st4.t[:, p2, 5:6],
                                                                              in1=gq_rep.t[:, 0:384], op0=ALU.mult, op1=ALU.mult),
                              [b1.b, sb_, gq_rep.b], [cqb])
                        P.dve(lambda e, b1=b1, p2=p2: e.scalar_tensor_tensor(out=cq.t[:, p2, 384:512], in0=b1.t[:, 384:512], scalar=st4.t[:, p2, 6:7],
                                                                              in1=gq_rep.t[:, 384:512], op0=ALU.mult, op1=ALU.mult),
                              [b1.b, sb_, gq_rep.b], [cqb])
                        b3 = pb()
                        for c in range(4):
                            P.pe(lambda e, c=c, b3=b3, p2=p2: e.transpose(out=bf(b3)[:, c * 128:(c + 1) * 128], in_=cq.t[:, p2, c * 128:(c + 1) * 128],
                                                                           identity=ident.t[:]), [cqb, ident.b], [b3.b])
                        tsl = slice(i * 128, (i + 1) * 128)
                        P.act(lambda e, b3=b3, tsl=tsl: e.activation(out=cqnT.t[:, :, tsl], in_=bf(b3)[:, 0:384].rearrange("p (c n) -> p c n", c=3), func=AF.Copy),
                              [b3.b], [cqnT.b])
                        P.act(lambda e, b3=b3, tsl=tsl: e.activation(out=ckvnT.t[:, tsl], in_=bf(b3)[:, 384:512], func=AF.Copy), [b3.b], [ckvnT.b])
                        b4 = pb()
                        for k in range(8):
                            P.pe(lambda e, k=k, b4=b4, p2=p2: e.matmul(b4.t[0:64, 0:128], lhsT=wlat.t[:, k, 512:576], rhs=hTt.t[:, p2, k, :], start=(k == 0), stop=(k == 7)),
                                 [hTb, wlat.b], [b4.b])
                        for k in range(8):
                            P.pe(lambda e, k=k, b4=b4, p2=p2: e.matmul(b4.t[0:64, 128:256], lhsT=wkrp.t[:, k, :], rhs=hTt.t[:, p2, k, :], start=(k == 0), stop=(k == 7)),
                                 [hTb, wkrp.b], [b4.b])
                        P.dve(lambda e, b4=b4, p2=p2, tsl=tsl: e.scalar_tensor_tensor(out=rt.t[:, p2, 0], in0=b4.t[0:64, 0:128], scalar=gv.t[0:64, 4:5],
                                                                                       in1=cosT.t[:, tsl], op0=ALU.mult, op1=ALU.mult),
                              [b4.b, gv.b, cosT.b], [rtb])
                        P.dve(lambda e, b4=b4, p2=p2, tsl=tsl: e.scalar_tensor_tensor(out=rt.t[:, p2, 1], in0=b4.t[0:64, 128:256], scalar=gv.t[0:64, 5:6],
                                                                                       in1=sinT.t[:, tsl], op0=ALU.mult, op1=ALU.mult),
                              [b4.b, gv.b, sinT.b], [rtb])
                        P.pool(lambda e, p2=p2, tsl=tsl: e.tensor_tensor(out=krT.t[:, tsl], in0=rt.t[:, p2, 0], in1=rt.t[:, p2, 1], op=ALU.add), [rtb], [krT.b])
                        b5, b6 = pb(), pb()
                        for hh, bb in ((0, b5), (1, b6)):
                            P.pe(lambda e, hh=hh, bb=bb, tsl=tsl: e.matmul(bb.t[:], lhsT=ckvnT.t[:, tsl],
                                                                            rhs=wkvs.t[:].rearrange("p (h x) -> p h x", h=8)[:, hh * 4:(hh + 1) * 4, 0:128],
                                                                            start=True, stop=True), [ckvnT.b, wkvs.b], [bb.b])
                            P.act(lambda e, hh=hh, bb=bb, p2=p2: e.activation(out=sqk.t[:, p2, hh * 512:(hh + 1) * 512], in_=bb.t[:], func=AF.Square),
                                  [bb.b], [sqb])
                        P.dve(lambda e, p2=p2: e.tensor_reduce(out=ssk.t[:, p2], in_=sqk.t[:, p2].rearrange("p (h x) -> p h x", h=8), axis=AX.X, op=ALU.add),
                              [sqb], [skb])
                        P.dve(lambda e, p2=p2: e.tensor_scalar(out=ssk.t[:, p2], in0=ssk.t[:, p2], scalar1=st4.t[:, p2, 7:8], scalar2=1.0 / 192,
                                                               op0=ALU.add, op1=ALU.mult), [skb, sb_], [skb])
                        P.act(lambda e, p2=p2: e.activation(out=ssk.t[:, p2], in_=ssk.t[:, p2], func=AF.Sqrt, bias=EPS, scale=1.0), [skb], [skb])
                        P.dve(lambda e, p2=p2: e.reciprocal(out=ssk.t[:, p2], in_=ssk.t[:, p2]), [skb], [skb])
                        P.dve(lambda e, p2=p2, i=i: e.tensor_scalar(out=rk_s.t[:, i, :], in0=ssk.t[:, p2], scalar1=float(192 ** -0.5), scalar2=None, op0=ALU.mult),
                              [skb], [rk_s.b])

                if dbg == 1:
                    d_cq = nc.dram_tensor("d_cq", [128, 3 * T], BF16, kind="ExternalOutput").ap()
                    d_ckv = nc.dram_tensor("d_ckv", [128, T], BF16, kind="ExternalOutput").ap()
                    d_kr = nc.dram_tensor("d_kr", [64, T], BF16, kind="ExternalOutput").ap()
                    d_rk = nc.dram_tensor("d_rk", [128, NT * 8], F32, kind="ExternalOutput").ap()
                    d_G = nc.dram_tensor("d_G", [128, 2 * D], F32, kind="ExternalOutput").ap()
                    out_ops.append(P.ld(lambda e: e.dma_start(out=d_cq[:, :], in_=cqnT.t[:].rearrange("p c n -> p (c n)")), [cqnT.b], []))
                    out_ops.append(P.ld(lambda e: e.dma_start(out=d_ckv[:, :], in_=ckvnT.t[:]), [ckvnT.b], []))
                    out_ops.append(P.ld(lambda e: e.dma_start(out=d_kr[:, :], in_=krT.t[:]), [krT.b], []))
                    out_ops.append(P.ld(lambda e: e.dma_start(out=d_rk[:, :], in_=rk_s.t[:].rearrange("p a b -> p (a b)")), [rk_s.b], []))
                    out_ops.append(P.ld(lambda e: e.dma_start(out=d_G[:, :], in_=G.t[:].rearrange("p a b -> p (a b)")), [G.b], []))
                    out_ops.append(P.ld(lambda e: e.dma_start(out=out[1], in_=xin[3]), hT_db, []))
                    s14.__exit__(None, None, None)
                    return True

                with Scope(AR) as s3:
                    wgT = s3.sb("wgT", [128, 8, 2560], BF16)
                    wgq = s3.sb("wgq", [128, 8, 512], BF16)
                    waf = s3.sb("waf", [128, 8, 32], BF16)
                    waa = s3.sb("waa", [17, 2, 512], BF16)
                    gg_rep = s3.sb("gg_rep", [128, D], F32)
                    P.wld(lambda e: e.dma_start(out=wgT.t[:], in_=w_in[l, :, O_BK:O_BK + 2560].rearrange("(c p) n -> p c n", p=128)), [], [wgT.b])
                    P.wld(lambda e: e.dma_start(out=wgq.t[:], in_=w_in[l, :, O_BQ:O_BQ + 512].rearrange("(c p) n -> p c n", p=128)), [], [wgq.b])
                    P.wld(lambda e: e.dma_start(out=waf.t[:], in_=w_in[l, :, O_AF:O_AF + 32].rearrange("(c p) n -> p c n", p=128)), [], [waf.b])
                    P.wld(lambda e: e.dma_start(out=waa.t[:, 0, :], in_=wa_f[l]), [], [waa.b])
                    P.wld(lambda e: e.dma_start(out=waa.t[:, 1, :], in_=wa_b[l]), [], [waa.b])
                    P.ld(lambda e: e.dma_start(out=gg_rep.t[:], in_=gla_g[l, :].partition_broadcast(128)), [], [gg_rep.b])
                    S = s3.sb("S", [128, 4, 256], F32)
                    Sbf = s3.sb("Sbf", [128, 4, 256], BF16)
                    baf = s3.sb("baf", [17, 128], BF16)
                    P.pool(lambda e: e.memset(baf.t[:], 1.0), [], [baf.b])
                    hTg = s3.sb("hTg", [128, 2, 8, 128], BF16, nb=2)
                    lsp = s3.sb("lsp", [128, 2, 512], F32, nb=2)
                    ec = s3.sb("ec", [128, 2, 512], F32, nb=2)
                    kend = s3.sb("kend", [128, 2, 512], BF16, nb=2)
                    vv = s3.sb("vv", [128, 2, 1024], BF16, nb=2)
                    ebT = s3.sb("ebT", [128, 2, 4, 128], F32, nb=2)
                    enbT = s3.sb("enbT", [128, 2, 4, 128], F32, nb=2)
                    qd = s3.sb("qd", [128, 2, 4, 128], BF16, nb=2)
                    ki = s3.sb("ki", [128, 2, 4, 128], BF16, nb=2)
                    AT = s3.sb("AT", [128, 2, 4, 128], BF16, nb=2)
                    ofs = s3.sb("ofs", [128, 2, 1024], BF16, nb=2)
                    osum = s3.sb("osum", [128, 1024], F32)
                    junk3 = s3.sb("junk3", [128, 256], BF16)
                    stt = s3.sb("stt", [128, 2, 8], F32, nb=2)
                    sz = s3.sb("sz", [128, 1024], F32)
                    ybt = s3.sb("ybt", [128, 1024], BF16)
                    ybTt = s3.sb("ybTt", [128, 2, 1024], BF16, nb=2)
                    orders = (list(range(NT)), [1, 0] + list(range(NT - 1, 1, -1)))
                    for ps_ in (0, 1):
                        gcol = 127 if ps_ == 0 else 0
                        P.pool(lambda e: e.memset(S.t[:], 0.0), [], [S.b])
                        P.pool(lambda e: e.memset(Sbf.t[:], 0.0), [], [Sbf.b])
                        for n, i in enumerate(orders[ps_]):
                            p2 = n % 2
                            need_out = not (last and i < NCTX)
                            hB = hTg.b[p2]
                            P.ld(lambda e, i=i, p2=p2: e.dma_start(out=hTg.t[:, p2].rearrange("p c n -> p (c n)"), in_=hT_d[i]), [hT_db[i]], [hB])
                            bk = pb()
                            for k in range(8):
                                P.pe(lambda e, k=k, bk=bk, p2=p2, ps_=ps_: e.matmul(bk.t[0:16, 0:128], lhsT=waf.t[:, k, ps_ * 16:(ps_ + 1) * 16], rhs=hTg.t[:, p2, k, :],
                                                                                   start=(k == 0), stop=(k == 7)), [hB, waf.b], [bk.b])
                            P.act(lambda e, bk=bk: e.activation(out=baf.t[0:16, :], in_=bk.t[0:16, 0:128], func=AF.Copy), [bk.b], [baf.b])
                            bz = pb()
                            P.pe(lambda e, bz=bz, ps_=ps_: e.matmul(bz.t[:], lhsT=baf.t[:, :], rhs=waa.t[:, ps_, :], start=True, stop=True), [baf.b, waa.b], [bz.b])
                            P.act(lambda e, bz=bz, p2=p2: e.activation(out=lsp.t[:, p2], in_=bz.t[:], func=AF.Exp, scale=-1.0), [bz.b], [lsp.b[p2]])
                            P.act(lambda e, p2=p2: e.activation(out=lsp.t[:, p2], in_=lsp.t[:, p2], func=AF.Ln, bias=1.0, scale=1.0), [lsp.b[p2]], [lsp.b[p2]])
                            bc = pb()
                            P.pe(lambda e, bc=bc, p2=p2, ps_=ps_: e.matmul(bc.t[:], lhsT=tri.t[:, 2 * ps_ + 1, :], rhs=lsp.t[:, p2, :], start=True, stop=True),
                                 [tri.b, lsp.b[p2]], [bc.b])
                            bb = pb()
                            for h in range(4):
                                P.pe(lambda e, bb=bb, h=h, p2=p2, ps_=ps_: e.matmul(bb.t[:, h * 128:(h + 1) * 128], lhsT=lsp.t[:, p2, h * 128:(h + 1) * 128],
                                                                                   rhs=tri.t[:, 2 * ps_, :], start=True, stop=True), [tri.b, lsp.b[p2]], [bb.b])
                            P.act(lambda e, bc=bc, p2=p2: e.activation(out=ec.t[:, p2], in_=bc.t[:], func=AF.Exp), [bc.b], [ec.b[p2]])
                            P.act(lambda e, bb=bb, p2=p2: e.activation(out=ebT.t[:, p2].rearrange("p h n -> p (h n)"), in_=bb.t[:], func=AF.Exp), [bb.b], [ebT.b[p2]])
                            P.act(lambda e, bb=bb, p2=p2: e.activation(out=enbT.t[:, p2].rearrange("p h n -> p (h n)"), in_=bb.t[:], func=AF.Exp, scale=-1.0),
                                  [bb.b], [enbT.b[p2]])
                            bkk = pb()
                            for k in range(8):
                                P.pe(lambda e, k=k, bkk=bkk, p2=p2: e.matmul(bkk.t[:], lhsT=hTg.t[:, p2, k, :], rhs=wgT.t[:, k, 0:512], start=(k == 0), stop=(k == 7)),
                                     [hB, wgT.b], [bkk.b])
                            P.dve(lambda e, bkk=bkk, p2=p2: e.tensor_tensor(out=kend.t[:, p2], in0=bkk.t[:], in1=ec.t[:, p2], op=ALU.mult), [bkk.b, ec.b[p2]], [kend.b[p2]])
                            for half in range(2):
                                bv = pb()
                                for k in range(8):
                                    P.pe(lambda e, k=k, bv=bv, p2=p2, half=half: e.matmul(bv.t[:], lhsT=hTg.t[:, p2, k, :], rhs=wgT.t[:, k, 512 + half * 512:1024 + half * 512],
                                                                                         start=(k == 0), stop=(k == 7)), [hB, wgT.b], [bv.b])
                                P.act(lambda e, bv=bv, p2=p2, half=half: e.activation(out=vv.t[:, p2, half * 512:(half + 1) * 512], in_=bv.t[:], func=AF.Copy),
                                      [bv.b], [vv.b[p2]])
                            bq = pb()
                            for h in range(4):
                                for k in range(8):
                                    P.pe(lambda e, k=k, h=h, bq=bq, p2=p2: e.matmul(bq.t[:, h * 128:(h + 1) * 128], lhsT=wgq.t[:, k, h * 128:(h + 1) * 128], rhs=hTg.t[:, p2, k, :],
                                                                                   start=(k == 0), stop=(k == 7)), [hB, wgq.b], [bq.b])
                            P.dve(lambda e, bq=bq, p2=p2: e.scalar_tensor_tensor(out=qd.t[:, p2].rearrange("p h n -> p (h n)"), in0=bq.t[:], scalar=float(128 ** -0.5),
                                                                                  in1=ebT.t[:, p2].rearrange("p h n -> p (h n)"), op0=ALU.mult, op1=ALU.mult),
                                  [bq.b, ebT.b[p2]], [qd.b[p2]])
                            bkT = pb()
                            for h in range(4):
                                for k in range(8):
                                    P.pe(lambda e, k=k, h=h, bkT=bkT, p2=p2: e.matmul(bkT.t[:, h * 128:(h + 1) * 128], lhsT=wgT.t[:, k, h * 128:(h + 1) * 128], rhs=hTg.t[:, p2, k, :],
                                                                                     start=(k == 0), stop=(k == 7)), [hB, wgT.b], [bkT.b])
                            P.dve(lambda e, bkT=bkT, p2=p2: e.tensor_tensor(out=ki.t[:, p2].rearrange("p h n -> p (h n)"), in0=bkT.t[:],
                                                                             in1=enbT.t[:, p2].rearrange("p h n -> p (h n)"), op=ALU.mult), [bkT.b, enbT.b[p2]], [ki.b[p2]])
                            if need_out:
                                ba = pb()
                                for h in range(4):
                                    P.pe(lambda e, h=h, ba=ba, p2=p2: e.matmul(ba.t[:, h * 128:(h + 1) * 128], lhsT=ki.t[:, p2, h, :], rhs=qd.t[:, p2, h, :], start=True, stop=True),
                                         [ki.b[p2], qd.b[p2]], [ba.b])
                                P.dve(lambda e, ba=ba, p2=p2, ps_=ps_: e.tensor_tensor(out=AT.t[:, p2], in0=ba.t[:].rearrange("p (h n) -> p h n", h=4),
                                                                                      in1=mask.t[:, ps_, :].unsqueeze(1).to_broadcast([128, 4, 128]), op=ALU.mult),
                                      [ba.b, mask.b], [AT.b[p2]])
                                bo = (pb(), pb())
                                for h in range(4):
                                    b_ = bo[h // 2]
                                    cs = (h % 2) * 256
                                    P.pe(lambda e, h=h, b_=b_, cs=cs, p2=p2: e.matmul(b_.t[:, cs:cs + 256], lhsT=AT.t[:, p2, h, :], rhs=vv.t[:, p2, h * 256:(h + 1) * 256],
                                                                                     start=True, stop=False), [AT.b[p2], vv.b[p2]], [b_.b])
                                    P.pe(lambda e, h=h, b_=b_, cs=cs, p2=p2: e.matmul(b_.t[:, cs:cs + 256], lhsT=qd.t[:, p2, h, :], rhs=Sbf.t[:, h, :],
                                                                                     start=False, stop=True), [qd.b[p2], Sbf.b], [b_.b])
                                if ps_ == 0:
                                    for half in range(2):
                                        P.act(lambda e, half=half, p2=p2, b_=bo[half]: e.activation(out=ofs.t[:, p2, half * 512:(half + 1) * 512], in_=b_.t[:], func=AF.Copy),
                                              [bo[half].b], [ofs.b[p2]])
                                    P.st(lambda e, i=i, p2=p2: e.dma_start(out=of_d[i], in_=ofs.t[:, p2]), [ofs.b[p2]], [of_db[i]])
                                else:
                                    P.ld(lambda e, i=i, p2=p2: e.dma_start(out=ofs.t[:, p2], in_=of_d[i]), [of_db[i]], [ofs.b[p2]])
                                    for half in range(2):
                                        P.dve(lambda e, half=half, p2=p2, b_=bo[half]: e.tensor_tensor(out=osum.t[:, half * 512:(half + 1) * 512], in0=b_.t[:],
                                                                                                         in1=ofs.t[:, p2, half * 512:(half + 1) * 512], op=ALU.add),
                                              [bo[half].b, ofs.b[p2]], [osum.b])
                                    for h in range(4):
                                        P.act(lambda e, h=h, p2=p2: e.activation(out=junk3.t[:], in_=osum.t[:, h * 256:(h + 1) * 256], func=AF.Square, scale=1.0 / 16,
                                                                                 accum_out=stt.t[:, p2, h:h + 1]), [osum.b], [junk3.b, stt.b[p2]])
                                    P.act(lambda e, p2=p2: e.activation(out=stt.t[:, p2, 4:8], in_=stt.t[:, p2, 0:4], func=AF.Sqrt, bias=EPS, scale=1.0), [stt.b[p2]], [stt.b[p2]])
                                    P.dve(lambda e, p2=p2: e.reciprocal(out=stt.t[:, p2, 4:8], in_=stt.t[:, p2, 4:8]), [stt.b[p2]], [stt.b[p2]])
                                    for half in range(2):
                                        bzz = pb()
                                        for k in range(8):
                                            P.pe(lambda e, k=k, bzz=bzz, p2=p2, half=half: e.matmul(bzz.t[:], lhsT=hTg.t[:, p2, k, :], rhs=wgT.t[:, k, 1536 + half * 512:2048 + half * 512],
                                                                                                   start=(k == 0), stop=(k == 7)), [hB, wgT.b], [bzz.b])
                                        P.act(lambda e, bzz=bzz, half=half: e.activation(out=sz.t[:, half * 512:(half + 1) * 512], in_=bzz.t[:], func=AF.Silu), [bzz.b], [sz.b])
                                    P.pool(lambda e: e.tensor_tensor(out=sz.t[:], in0=sz.t[:], in1=gg_rep.t[:], op=ALU.mult), [sz.b, gg_rep.b], [sz.b])
                                    for h in range(4):
                                        P.dve(lambda e, h=h, p2=p2: e.scalar_tensor_tensor(out=ybt.t[:, h * 256:(h + 1) * 256], in0=osum.t[:, h * 256:(h + 1) * 256],
                                                                                           scalar=stt.t[:, p2, 4 + h:5 + h], in1=sz.t[:, h * 256:(h + 1) * 256],
                                                                                           op0=ALU.mult, op1=ALU.mult), [osum.b, stt.b[p2], sz.b], [ybt.b])
                                    bt = pb()
                                    for c in range(8):
                                        P.pe(lambda e, c=c, bt=bt: e.transpose(out=bf(bt)[:, c * 128:(c + 1) * 128], in_=ybt.t[:, c * 128:(c + 1) * 128], identity=ident.t[:]),
                                             [ybt.b, ident.b], [bt.b])
                                    P.act(lambda e, bt=bt, p2=p2: e.activation(out=ybTt.t[:, p2], in_=bf(bt), func=AF.Copy), [bt.b], [ybTt.b[p2]])
                                    P.st(lambda e, i=i, p2=p2: e.dma_start(out=ybT_d[i], in_=ybTt.t[:, p2]), [ybTt.b[p2]], [ybT_db[i]])
                            bu = (pb(), pb())
                            for h in range(4):
                                b_ = bu[h // 2]
                                cs = (h % 2) * 256
                                P.pe(lambda e, h=h, b_=b_, cs=cs, p2=p2: e.matmul(b_.t[:, cs:cs + 256], lhsT=kend.t[:, p2, h * 128:(h + 1) * 128], rhs=vv.t[:, p2, h * 256:(h + 1) * 256],
                                                                                 start=True, stop=True), [kend.b[p2], vv.b[p2]], [b_.b])
                            for h in range(4):
                                b_ = bu[h // 2]
                                cs = (h % 2) * 256
                                P.dve(lambda e, h=h, b_=b_, cs=cs, p2=p2, gcol=gcol: e.scalar_tensor_tensor(out=S.t[:, h, :], in0=S.t[:, h, :], scalar=ebT.t[:, p2, h, gcol:gcol + 1],
                                                                                                           in1=b_.t[:, cs:cs + 256], op0=ALU.mult, op1=ALU.add),
                                      [S.b, ebT.b[p2], b_.b], [S.b])
                            P.pool(lambda e: e.tensor_copy(out=Sbf.t[:], in_=S.t[:]), [S.b], [Sbf.b])
                if dbg == 3:
                    out_ops.append(P.ld(lambda e: e.dma_start(out=out[1], in_=xin[3]), hT_db + of_db + ybT_db, []))
                    s14.__exit__(None, None, None)
                    return True

                with Scope(AR) as s4:
                    wkvs4 = s4.sb("wkvs4", [128, 2048], BF16)
                    wqs = s4.sb("wqs", [128, 3, 2048], BF16)
                    P.wld(lambda e: e.dma_start(out=wkvs4.t[:], in_=wkv[l]), [], [wkvs4.b])
                    P.wld(lambda e: e.dma_start(out=wqs.t[:], in_=wq_aug[l].rearrange("(c p) n -> p c n", p=128)), [], [wqs.b])
                    KhT = s4.sb("KhT", [128, T], BF16)
                    Vh = s4.sb("Vh", [128, NT, 128], BF16)
                    QnT = s4.sb("QnT", [128, 2, 512], BF16, nb=2)
                    QrT = s4.sb("QrT", [64, 2, 512], BF16, nb=2)
                    sq1 = s4.sb("sq1", [128, 512], BF16)
                    sq2 = s4.sb("sq2", [64, 512], BF16)
                    rq = s4.sb("rq", [128, 2, 512], F32, nb=2)
                    t12 = s4.sb("t12", [64, 2, 512], F32)
                    PT = s4.sb("PT", [128, 4, 512], BF16, nb=4)
                    rs = s4.sb("rs", [128, 512], F32)
                    oc = s4.sb("oc", [128, 2, 512], BF16, nb=2)
                    accS = s4.sb("accS", [128, 2, 2, 512], F32, nb=4)
                    acc_banks = [banks[0], banks[1]]
                    reserved.update({0, 1})
                    qctr = 0
                    qtiles = ([] if last else [(0, 0, 256, [0, 1])]) + [(1 + q, 256 + q * 512, 512, list(range(NT))) for q in range(8)]
                    nxt_jobs = precast_jobs(l + 1) if l + 1 < L else []
                    for h in range(8):
                        issue_precast(nxt_jobs[h * 11:(h + 1) * 11])
                        for kt in range(9):
                            n = 512 if kt < 8 else 256
                            cols = slice(kt * 512, kt * 512 + n)
                            bk = pb()
                            P.pe(lambda e, bk=bk, n=n, cols=cols, h=h: e.matmul(bk.t[:, 0:n], lhsT=wkvs4.t[:, h * 256:h * 256 + 128], rhs=ckvnT.t[:, cols], start=True, stop=True),
                                 [wkvs4.b, ckvnT.b], [bk.b])
                            P.act(lambda e, bk=bk, n=n, cols=cols: e.activation(out=KhT.t[:, cols], in_=bk.t[:, 0:n], func=AF.Identity, scale=gv.t[:, 1:2]), [bk.b, gv.b], [KhT.b])
                        for g in range(9):
                            tl_ = list(range(4 * g, min(4 * g + 4, NT)))
                            bk = pb()
                            for jj, j in enumerate(tl_):
                                P.pe(lambda e, bk=bk, jj=jj, j=j, h=h: e.matmul(bk.t[:, jj * 128:(jj + 1) * 128], lhsT=ckvnT.t[:, j * 128:(j + 1) * 128],
                                                                               rhs=wkvs4.t[:, h * 256 + 128:h * 256 + 256], start=True, stop=True), [wkvs4.b, ckvnT.b], [bk.b])
                            nn = len(tl_)
                            P.act(lambda e, bk=bk, g=g, nn=nn: e.activation(out=Vh.t[:, 4 * g:4 * g + nn, :], in_=bk.t[:, 0:nn * 128].rearrange("p (j n) -> p j n", j=nn), func=AF.Copy),
                                  [bk.b], [Vh.b])
                        for (tg, t0, n, keys) in qtiles:
                            qp = qctr % 2
                            qctr += 1
                            bO = acc_banks[qp]
                            qs = slice(t0, t0 + n)
                            b1, b2, b3 = pb(), pb(), pb()
                            for (bq_, c0, m_) in ((b1, 0, 128), (b2, 128, 64), (b3, 192, 64)):
                                for c in range(3):
                                    P.pe(lambda e, bq_=bq_, c0=c0, m_=m_, c=c, n=n, qs=qs, h=h: e.matmul(bq_.t[0:m_, 0:n], lhsT=wqs.t[:, c, h * 256 + c0:h * 256 + c0 + m_], rhs=cqnT.t[:, c, qs],
                                                                                                        start=(c == 0), stop=(c == 2)), [wqs.b, cqnT.b], [bq_.b])
                            P.act(lambda e, b1=b1, n=n: e.activation(out=sq1.t[:, 0:n], in_=b1.t[:, 0:n], func=AF.Square), [b1.b], [sq1.b])
                            P.act(lambda e, b2=b2, n=n: e.activation(out=sq2.t[:, 0:n], in_=b2.t[0:64, 0:n], func=AF.Square), [b2.b], [sq2.b])
                            b4 = pb()
                            P.pe(lambda e, b4=b4, n=n: e.matmul(b4.t[:, 0:n], lhsT=ones.t[:, :], rhs=sq1.t[:, 0:n], start=True, stop=False), [ones.b, sq1.b], [b4.b])
                            P.pe(lambda e, b4=b4, n=n: e.matmul(b4.t[:, 0:n], lhsT=ones.t[0:64, :], rhs=sq2.t[:, 0:n], start=False, stop=True), [ones.b, sq2.b], [b4.b])
                            P.act(lambda e, b4=b4, n=n, qp=qp: e.activation(out=rq.t[:, qp, 0:n], in_=b4.t[:, 0:n], func=AF.Sqrt, bias=EPS, scale=1.0 / 192), [b4.b], [rq.b[qp]])
                            P.dve(lambda e, n=n, qp=qp: e.reciprocal(out=rq.t[:, qp, 0:n], in_=rq.t[:, qp, 0:n]), [rq.b[qp]], [rq.b[qp]])
                            P.dve(lambda e, b1=b1, n=n, qp=qp: e.scalar_tensor_tensor(out=QnT.t[:, qp, 0:n], in0=b1.t[:, 0:n], scalar=gv.t[:, 0:1], in1=rq.t[:, qp, 0:n],
                                                                                      op0=ALU.mult, op1=ALU.mult), [b1.b, gv.b, rq.b[qp]], [QnT.b[qp]])
                            P.dve(lambda e, b2=b2, n=n, qs=qs: e.scalar_tensor_tensor(out=t12.t[:, 0, 0:n], in0=b2.t[0:64, 0:n], scalar=gv.t[0:64, 2:3], in1=cosT.t[:, qs],
                                                                                      op0=ALU.mult, op1=ALU.mult), [b2.b, gv.b, cosT.b], [t12.b])
                            P.dve(lambda e, b3=b3, n=n, qs=qs: e.scalar_tensor_tensor(out=t12.t[:, 1, 0:n], in0=b3.t[0:64, 0:n], scalar=gv.t[0:64, 3:4], in1=sinT.t[:, qs],
                                                                                      op0=ALU.mult, op1=ALU.mult), [b3.b, gv.b, sinT.b], [t12.b])
                            P.pool(lambda e, n=n: e.tensor_tensor(out=t12.t[:, 0, 0:n], in0=t12.t[:, 0, 0:n], in1=t12.t[:, 1, 0:n], op=ALU.add), [t12.b], [t12.b])
                            P.pool(lambda e, n=n, qp=qp: e.tensor_tensor(out=QrT.t[:, qp, 0:n], in0=t12.t[:, 0, 0:n], in1=rq.t[0:64, qp, 0:n], op=ALU.mult),
                                   [t12.b, rq.b[qp]], [QrT.b[qp]])
                            nk = len(keys)
                            sbanks = {}
                            for jj in range(nk + 1):
                                if jj < nk:
                                    j = keys[jj]
                                    bs = pb()
                                    sbanks[jj] = bs
                                    ks = slice(j * 128, (j + 1) * 128)
                                    P.pe(lambda e, bs=bs, n=n, ks=ks, qp=qp: e.matmul(bs.t[:, 0:n], lhsT=KhT.t[:, ks], rhs=QnT.t[:, qp, 0:n], start=True, stop=False),
                                         [KhT.b, QnT.b[qp]], [bs.b])
                                    P.pe(lambda e, bs=bs, n=n, ks=ks, qp=qp: e.matmul(bs.t[:, 0:n], lhsT=krT.t[:, ks], rhs=QrT.t[:, qp, 0:n], start=False, stop=True),
                                         [krT.b, QrT.b[qp]], [bs.b])
                                    sl = jj % 4
                                    P.act(lambda e, bs=bs, n=n, sl=sl, j=j, h=h: e.activation(out=PT.t[:, sl, 0:n], in_=bs.t[:, 0:n], func=AF.Exp, scale=rk_s.t[:, j, h:h + 1]),
                                          [bs.b, rk_s.b], [PT.b[sl]])
                                if jj >= 1:
                                    j = keys[jj - 1]
                                    sl = (jj - 1) % 4
                                    P.pe(lambda e, bO=bO, n=n, sl=sl, j=j, jj=jj, nk=nk: e.matmul(bO.t[:, 0:n], lhsT=Vh.t[:, j, :], rhs=PT.t[:, sl, 0:n], start=(jj == 1), stop=(jj == nk)),
                                         [Vh.b, PT.b[sl]], [bO.b])
                                    who_ = (jj - 1) % 2
                                    ab_ = accS.b[qp * 2 + who_]
                                    adder = P.dve if who_ == 0 else P.pool
                                    if jj - 1 < 2:
                                        adder(lambda e, n=n, sl=sl, qp=qp, who_=who_: e.tensor_copy(out=accS.t[:, qp, who_, 0:n], in_=PT.t[:, sl, 0:n]), [PT.b[sl]], [ab_])
                                    else:
                                        adder(lambda e, n=n, sl=sl, qp=qp, who_=who_: e.tensor_tensor(out=accS.t[:, qp, who_, 0:n], in0=accS.t[:, qp, who_, 0:n], in1=PT.t[:, sl, 0:n], op=ALU.add),
                                              [PT.b[sl], ab_], [ab_])
                            bS = pb()
                            P.pe(lambda e, bS=bS, n=n, qp=qp: e.matmul(bS.t[:, 0:n], lhsT=ones32.t[:, :], rhs=accS.t[:, qp, 0, 0:n], start=True, stop=False), [ones32.b, accS.b[qp * 2]], [bS.b])
                            P.pe(lambda e, bS=bS, n=n, qp=qp: e.matmul(bS.t[:, 0:n], lhsT=ones32.t[:, :], rhs=accS.t[:, qp, 1, 0:n], start=False, stop=True), [ones32.b, accS.b[qp * 2 + 1]], [bS.b])
                            P.dve(lambda e, bS=bS, n=n: e.reciprocal(out=rs.t[:, 0:n], in_=bS.t[:, 0:n]), [bS.b], [rs.b])
                            P.dve(lambda e, bO=bO, n=n, qp=qp: e.tensor_tensor(out=oc.t[:, qp, 0:n], in0=bO.t[:, 0:n], in1=rs.t[:, 0:n], op=ALU.mult), [bO.b, rs.b], [oc.b[qp]])
                            P.st(lambda e, tg=tg, h=h, n=n, qp=qp: e.dma_start(out=ocT_d[tg][:, h * 512:h * 512 + n], in_=oc.t[:, qp, 0:n]), [oc.b[qp]], [ocT_db[tg]])
                    reserved.clear()
                s14.__exit__(None, None, None)
                if dbg == 4:
                    out_ops.append(P.ld(lambda e: e.dma_start(out=out[1], in_=xin[3]), hT_db + of_db + ybT_db + ocT_db, []))
                    return True

                with Scope(AR) as s5:
                    wout = s5.sb("wout", [128, 8, 1024], BF16)
                    cw = s5.sb("cw", [128, 24], F32)
                    P.wld(lambda e: e.dma_start(out=wout.t[:], in_=w_out[l].rearrange("(c p) n -> p c n", p=128)), [], [wout.b])
                    P.ld(lambda e: e.dma_start(out=cw.t[:], in_=conv_wT[l]), [], [cw.b])
                    hg = s5.sb("hg", [128, 8, 768], BF16, nb=6)
                    ya = s5.sb("ya", [128, 8, 512], BF16)
                    yb5 = s5.sb("yb5", [128, 8, 512], BF16, nb=4)
                    oc5 = s5.sb("oc5", [128, 8, 512], BF16)
                    yc = s5.sb("yc", [128, 8, 512], BF16)
                    mm_ = s5.sb("mm_", [128, 8, 512], BF16)
                    wa4 = s5.sb("wa4", [128, 2, 4, 8, 128], BF16, nb=2)
                    wcz = s5.sb("wcz", [128, 2, 8, 128], BF16, nb=2)
                    wbg = s5.sb("wbg", [128, 2, 6, 8, 128], BF16, nb=2)
                    pext = s5.sb("pext", [128, 2, 514], F32, nb=2)
                    av = s5.sb("av", [128, 2, 512], F32, nb=2)
                    avh = s5.sb("avh", [128, 2, 2], F32, nb=2)
                    szz = s5.sb("szz", [128, 2, 512], F32, nb=2)
                    c1 = s5.sb("c1", [128, 2, 512], F32, nb=2)
                    c2 = s5.sb("c2", [128, 2, 512], F32, nb=2)
                    sig = s5.sb("sig", [128, 2, 512], F32, nb=2)
                    acc = s5.sb("acc", [128, 2, 512], F32, nb=2)
                    tt = s5.sb("tt", [128, 2, 512], F32, nb=2)
                    xt5 = s5.sb("xt5", [128, 2, 1024], F32, nb=2)
                    xo = s5.sb("xo", [128, 2, 1024], F32, nb=2)
                    fctr = hctr = octr = xctr = 0
                    tgs = ([] if last else [(0, [0, 1])]) + [(1 + q, list(range(2 + 4 * q, 6 + 4 * q))) for q in range(8)]
                    a_offs = (O_AV, O_AB, O_AC, O_AZ)
                    g_offs = (O_GA, O_GB, O_GC)
                    for (tg, tl_) in tgs:
                        n = 128 * len(tl_)
                        who = 0 if tg == 0 else 1
                        for jj, i in enumerate(tl_):
                            P.ld(lambda e, i=i, jj=jj: e.dma_start(out=hg.t[:, :, 128 + jj * 128:256 + jj * 128], in_=hT_d[i].rearrange("p (c n) -> p c n", c=8)), [hT_db[i]], [hg.b[1 + jj]])
                        has_l = tl_[0] not in (0, NCTX)
                        has_r = tl_[-1] not in (NCTX - 1, NT - 1)
                        if has_l:
                            P.ld(lambda e, i=tl_[0] - 1: e.dma_start(out=hg.t[:, :, 0:128], in_=hT_d[i].rearrange("p (c n) -> p c n", c=8)), [hT_db[tl_[0] - 1]], [hg.b[0]])
                        if has_r:
                            P.ld(lambda e, i=tl_[-1] + 1, n=n: e.dma_start(out=hg.t[:, :, 128 + n:256 + n], in_=hT_d[i].rearrange("p (c n) -> p c n", c=8)), [hT_db[tl_[-1] + 1]], [hg.b[1 + len(tl_)]])
                        for jj, i in enumerate(tl_):
                            P.ld(lambda e, i=i, jj=jj: e.dma_start(out=yb5.t[:, :, jj * 128:(jj + 1) * 128], in_=ybT_d[i].rearrange("p (c n) -> p c n", c=8)), [ybT_db[i]], [yb5.b[jj]])
                        P.ld(lambda e, tg=tg, n=n: e.dma_start(out=oc5.t[:, :, 0:n], in_=ocT_d[tg].rearrange("p (c n) -> p c n", c=8)[:, :, 0:n]), [ocT_db[tg]], [oc5.b])
                        hmain = lambda k, n=n: hg.t[:, k, 128:128 + n]
                        for f in range(8):
                            fp = fctr % 2
                            fctr += 1
                            P.ld(lambda e, f=f, fp=fp: e.dma_start(out=wa4.t[:, fp], in_=WA_d[l % 2][f]), wa_db[l % 2], [wa4.b[fp]])
                            bC, bV, bH = pb(), pb(), pb()
                            for (bk, w_) in ((bC, 2), (bV, 0)):
                                for k in range(8):
                                    P.pe(lambda e, bk=bk, w_=w_, k=k, n=n, fp=fp: e.matmul(bk.t[:, 0:n], lhsT=wa4.t[:, fp, w_, k, :], rhs=hg.t[:, k, 128:128 + n], start=(k == 0), stop=(k == 7)),
                                         [wa4.b[fp]] + hg.b, [bk.b])
                            for (c0, w_) in ((0, 2), (2, 0)):
                                for k in range(8):
                                    P.pe(lambda e, c0=c0, w_=w_, k=k, n=n, fp=fp, bH=bH: e.matmul(bH.t[:, c0:c0 + 2], lhsT=wa4.t[:, fp, w_, k, :], rhs=hg.t[:, k, 127:127 + n + 2:n + 1],
                                                                                                 start=(k == 0), stop=(k == 7)), [wa4.b[fp]] + hg.b, [bH.b])
                            P.act(lambda e, fp=fp, bV=bV, n=n: e.activation(out=av.t[:, fp, 0:n], in_=bV.t[:, 0:n], func=AF.Copy), [bV.b], [av.b[fp]])
                            P.act(lambda e, fp=fp, bH=bH: e.activation(out=avh.t[:, fp, 0:2], in_=bH.t[:, 2:4], func=AF.Copy), [bH.b], [avh.b[fp]])
                            P.dve(lambda e, fp=fp, bC=bC, n=n: e.tensor_tensor(out=pext.t[:, fp, 1:1 + n], in0=bC.t[:, 0:n], in1=av.t[:, fp, 0:n], op=ALU.mult), [bC.b, av.b[fp]], [pext.b[fp]])
                            P.dve(lambda e, fp=fp, bH=bH, n=n: e.tensor_tensor(out=pext.t[:, fp, 0:n + 2:n + 1], in0=bH.t[:, 0:2], in1=avh.t[:, fp, 0:2], op=ALU.mult), [bH.b, avh.b[fp]], [pext.b[fp]])
                            if not has_l:
                                P.pool(lambda e, fp=fp: e.memset(pext.t[:, fp, 0:1], 0.0), [], [pext.b[fp]])
                            if not has_r:
                                P.pool(lambda e, fp=fp, n=n: e.memset(pext.t[:, fp, n + 1:n + 2], 0.0), [], [pext.b[fp]])
                            bB, bZ = pb(), pb()
                            for (bk, w_) in ((bB, 1), (bZ, 3)):
                                for k in range(8):
                                    P.pe(lambda e, bk=bk, w_=w_, k=k, n=n, fp=fp: e.matmul(bk.t[:, 0:n], lhsT=wa4.t[:, fp, w_, k, :], rhs=hg.t[:, k, 128:128 + n], start=(k == 0), stop=(k == 7)),
                                         [wa4.b[fp]] + hg.b, [bk.b])
                            P.act(lambda e, fp=fp, bZ=bZ, n=n: e.activation(out=szz.t[:, fp, 0:n], in_=bZ.t[:, 0:n], func=AF.Silu), [bZ.b], [szz.b[fp]])
                            P.dve(lambda e, fp=fp, n=n, f=f: e.tensor_scalar(out=c1.t[:, fp, 0:n], in0=pext.t[:, fp, 1:1 + n], scalar1=cw.t[:, f * 3 + 1:f * 3 + 2], scalar2=None, op0=ALU.mult),
                                  [pext.b[fp], cw.b], [c1.b[fp]])
                            P.dve(lambda e, fp=fp, n=n, f=f: e.scalar_tensor_tensor(out=c1.t[:, fp, 0:n], in0=pext.t[:, fp, 0:n], scalar=cw.t[:, f * 3:f * 3 + 1], in1=c1.t[:, fp, 0:n], op0=ALU.mult, op1=ALU.add),
                                  [pext.b[fp], cw.b, c1.b[fp]], [c1.b[fp]])
                            P.dve(lambda e, fp=fp, n=n, f=f: e.scalar_tensor_tensor(out=c1.t[:, fp, 0:n], in0=pext.t[:, fp, 2:2 + n], scalar=cw.t[:, f * 3 + 2:f * 3 + 3], in1=c1.t[:, fp, 0:n], op0=ALU.mult, op1=ALU.add),
                                  [pext.b[fp], cw.b, c1.b[fp]], [c1.b[fp]])
                            P.dve(lambda e, fp=fp, bB=bB, n=n: e.tensor_tensor(out=c2.t[:, fp, 0:n], in0=bB.t[:, 0:n], in1=c1.t[:, fp, 0:n], op=ALU.mult), [bB.b, c1.b[fp]], [c2.b[fp]])
                            P.pool(lambda e, fp=fp, n=n, f=f: e.tensor_tensor(out=ya.t[:, f, 0:n], in0=c2.t[:, fp, 0:n], in1=szz.t[:, fp, 0:n], op=ALU.mult), [c2.b[fp], szz.b[fp]], [ya.b])
                        for h in range(8):
                            hp = hctr % 2
                            hctr += 1
                            P.ld(lambda e, h=h, hp=hp: e.dma_start(out=wcz.t[:, hp], in_=WC_d[l % 2][h]), wc_db[l % 2], [wcz.b[hp]])
                            bz = pb()
                            for k in range(8):
                                P.pe(lambda e, bz=bz, k=k, n=n, hp=hp: e.matmul(bz.t[:, 0:n], lhsT=wcz.t[:, hp, k, :], rhs=hg.t[:, k, 128:128 + n], start=(k == 0), stop=(k == 7)),
                                     [wcz.b[hp]] + hg.b, [bz.b])
                            P.act(lambda e, hp=hp, bz=bz, n=n: e.activation(out=szz.t[:, hp, 0:n], in_=bz.t[:, 0:n], func=AF.Silu), [bz.b], [szz.b[hp]])
                            P.pool(lambda e, hp=hp, n=n, h=h: e.tensor_tensor(out=yc.t[:, h, 0:n], in0=oc5.t[:, h, 0:n], in1=szz.t[:, hp, 0:n], op=ALU.mult), [oc5.b, szz.b[hp]], [yc.b])
                        for o in range(8):
                            op_ = octr % 2
                            octr += 1
                            P.ld(lambda e, o=o, op_=op_: e.dma_start(out=wbg.t[:, op_], in_=WG_d[l % 2][o]), wg_db[l % 2], [wbg.b[op_]])
                            for br in range(3):
                                src = (ya, yb5, yc)[br]
                                bg, bb_ = pb(), pb()
                                for k in range(8):
                                    P.pe(lambda e, bg=bg, k=k, n=n, op_=op_, br=br: e.matmul(bg.t[:, 0:n], lhsT=wbg.t[:, op_, 3 + br, k, :], rhs=hg.t[:, k, 128:128 + n], start=(k == 0), stop=(k == 7)),
                                         [wbg.b[op_]] + hg.b, [bg.b])
                                P.act(lambda e, op_=op_, bg=bg, n=n, br=br: e.activation(out=sig.t[:, br % 2, 0:n], in_=bg.t[:, 0:n], func=AF.Sigmoid), [bg.b], [sig.b[br % 2]])
                                for k in range(8):
                                    P.pe(lambda e, bb_=bb_, k=k, n=n, op_=op_, br=br, src=src: e.matmul(bb_.t[:, 0:n], lhsT=wbg.t[:, op_, br, k, :], rhs=src.t[:, k, 0:n], start=(k == 0), stop=(k == 7)),
                                         [wbg.b[op_]] + (src.b if isinstance(src.b, list) else [src.b]), [bb_.b])
                                if br == 0:
                                    P.dve(lambda e, op_=op_, bb_=bb_, n=n, br=br: e.tensor_tensor(out=acc.t[:, op_, 0:n], in0=bb_.t[:, 0:n], in1=sig.t[:, br % 2, 0:n], op=ALU.mult), [bb_.b, sig.b[br % 2]], [acc.b[op_]])
                                else:
                                    P.dve(lambda e, op_=op_, bb_=bb_, n=n, br=br: e.tensor_tensor(out=tt.t[:, op_, 0:n], in0=bb_.t[:, 0:n], in1=sig.t[:, br % 2, 0:n], op=ALU.mult), [bb_.b, sig.b[br % 2]], [tt.b[op_]])
                                    if br == 1:
                                        P.pool(lambda e, op_=op_, n=n: e.tensor_tensor(out=acc.t[:, op_, 0:n], in0=acc.t[:, op_, 0:n], in1=tt.t[:, op_, 0:n], op=ALU.add), [acc.b[op_], tt.b[op_]], [acc.b[op_]])
                                    else:
                                        P.pool(lambda e, op_=op_, n=n, o=o: e.tensor_tensor(out=mm_.t[:, o, 0:n], in0=acc.t[:, op_, 0:n], in1=tt.t[:, op_, 0:n], op=ALU.add), [acc.b[op_], tt.b[op_]], [mm_.b])
                        for jj, i in enumerate(tl_):
                            xp = xctr % 2
                            xctr += 1
                            P.ld(lambda e, i=i, xp=xp: e.dma_start(out=xt5.t[:, xp], in_=x_src[i]), [xs_b[i]] if l > 0 else [], [xt5.b[xp]])
                            for half in range(2):
                                bo = pb()
                                hs = slice(half * 512, (half + 1) * 512)
                                for o in range(8):
                                    P.pe(lambda e, bo=bo, o=o, jj=jj, hs=hs: e.matmul(bo.t[:], lhsT=mm_.t[:, o, jj * 128:(jj + 1) * 128], rhs=wout.t[:, o, hs], start=(o == 0), stop=(o == 7)),
                                         [mm_.b, wout.b], [bo.b])
                                P.dve(lambda e, bo=bo, hs=hs, xp=xp, who=who: e.tensor_tensor(out=xo.t[:, xp, hs], in0=bo.t[:], in1=G.t[:, who, hs], op=ALU.mult), [bo.b, G.b], [xo.b[xp]])
                                P.pool(lambda e, hs=hs, xp=xp: e.tensor_tensor(out=xo.t[:, xp, hs], in0=xo.t[:, xp, hs], in1=xt5.t[:, xp, hs], op=ALU.add), [xo.b[xp], xt5.b[xp]], [xo.b[xp]])
                            if l == L - 1 and i >= NCTX:
                                out_ops.append(P.st(lambda e, i=i, xp=xp: e.dma_start(out=out[i - NCTX], in_=xo.t[:, xp]), [xo.b[xp]], []))
                            else:
                                P.st(lambda e, i=i, xp=xp: e.dma_start(out=xs[i], in_=xo.t[:, xp]), [xo.b[xp]], [xs_b[i]])
            return False

        for l_ in range(L):
            if emit_layer(l_):
                break
        if dbg in (1, 3, 4):
            dummy = P.ld(lambda e: e.dma_start(out=out[0], in_=xin[2]), [], [])
            out_ops.append(dummy)
        P.emit(out_ops)
    return nc


_ROPE_PERM = np.concatenate([np.arange(16, 32), np.arange(0, 16), np.arange(48, 64), np.arange(32, 48)])
_ROPE_SIGN = np.concatenate([-np.ones(16), np.ones(16), -np.ones(16), np.ones(16)]).astype(np.float32)


def _rope_tables():
    rows = 4096 // 64
    row = np.repeat(np.arange(rows, dtype=np.float32), 64)
    col = np.tile(np.arange(64, dtype=np.float32), rows)
    n_freq = 16
    freqs = (np.float32(10000.0) ** (-np.arange(n_freq, dtype=np.float32) / n_freq)).astype(np.float32)
    ang_r = row[:, None] * freqs[None, :]
    ang_c = col[:, None] * freqs[None, :]
    ang = np.concatenate([ang_r, ang_r, ang_c, ang_c], axis=-1)
    cos = np.ones((T, 64), np.float32)
    sin = np.zeros((T, 64), np.float32)
    cos[NCTX * 128:] = np.cos(ang)
    sin[NCTX * 128:] = np.sin(ang) * _ROPE_SIGN[None, :]
    return np.ascontiguousarray(cos.T), np.ascontiguousarray(sin.T)


def _consts():
    j = np.arange(128)[:, None]
    i = np.arange(128)[None, :]
    s = np.float32(-1.0 / 16.0)
    tri = np.stack([(j <= i), (j > i), (j >= i), (j < i)]).astype(np.float32) * s
    mask = np.stack([(j <= i), (j >= i)]).astype(np.float32)
    cosT, sinT = _rope_tables()
    return dict(ident=np.eye(128, dtype=np.float32), tri=tri, mask=mask, cosT=cosT, sinST=sinT)


def prep_inputs(inp, LW=DEPTH):
    f = lambda a: np.ascontiguousarray(np.asarray(a, dtype=np.float32))
    w_in = f(inp["w_in"])
    shared = dict(
        w_mod=f(inp["w_mod"]), b_mod=f(inp["b_mod"]), norm_g=f(inp["norm_g"]), w_in=w_in,
        w_krp=np.ascontiguousarray(w_in[:, :, O_CKR:O_CKR + 64][:, :, _ROPE_PERM]),
        conv_wT=np.ascontiguousarray(f(inp["conv_w"]).reshape(DEPTH, 3, 8, 128).transpose(0, 3, 2, 1).reshape(DEPTH, 128, 24)),
        wa_f=np.ascontiguousarray(np.concatenate([f(inp["gla_wa_up_f"]), f(inp["gla_ba_f"])[:, None, :]], axis=1)),
        wa_b=np.ascontiguousarray(np.concatenate([f(inp["gla_wa_up_b"]), f(inp["gla_ba_b"])[:, None, :]], axis=1)),
        gla_norm_g=f(inp["gla_norm_g"]), mla_q_norm_g=f(inp["mla_q_norm_g"]), mla_kv_norm_g=f(inp["mla_kv_norm_g"]),
        wkv=f(inp["mla_wkv_up"]),
        w_br_a=f(inp["w_br_a"]), w_br_b=f(inp["w_br_b"]), w_br_c=f(inp["w_br_c"]), w_out=f(inp["w_out"]),
    )
    wq = f(inp["mla_wq_up"]).reshape(DEPTH, 384, 8, 192)
    shared["wq_aug"] = np.ascontiguousarray(np.concatenate([wq, wq[..., 128:192][..., _ROPE_PERM]], axis=-1).reshape(DEPTH, 384, 2048))
    qg, kg = f(inp["mla_qn_g"]), f(inp["mla_kn_g"])
    gv = np.zeros((DEPTH, 128, 8), np.float32)
    gv[:, :, 0] = qg[:, 0:128]
    gv[:, :, 1] = kg[:, 0:128]
    gv[:, 0:64, 2] = qg[:, 128:192]
    gv[:, 0:64, 3] = qg[:, 128:192][:, _ROPE_PERM]
    gv[:, 0:64, 4] = kg[:, 128:192]
    gv[:, 0:64, 5] = kg[:, 128:192][:, _ROPE_PERM]
    shared["gvec"] = gv
    shared = {k: np.ascontiguousarray(v[:LW]) for k, v in shared.items()}
    shared.update(_consts())
    x, ctx, c, c_ctx = f(inp["x"]), f(inp["ctx"]), f(inp["c"]), f(inp["c_ctx"])
    consts = _consts()
    idle = {k: np.zeros_like(v) for k, v in shared.items()}
    idle.update(consts)
    idle["xin"] = np.zeros((NT, 128, D), np.float32)
    idle["cvec"] = np.zeros((128, 16), np.float32)
    maps = []
    for core in range(8):
        if core not in REAL_CORES:
            maps.append(idle)
            continue
        b = REAL_CORES.index(core)
        m = dict(shared)
        m["xin"] = np.ascontiguousarray(np.concatenate([ctx[b], x[b]], axis=0).reshape(NT, 128, D))
        cv = np.zeros((128, 16), np.float32)
        cv[:, 0:8] = c[b].reshape(8, 128).T
        cv[:, 8:16] = c_ctx.reshape(8, 128).T
        m["cvec"] = cv
        maps.append(m)
    return maps


REAL_CORES = [0, 1, 4, 5]
_NC_CACHE = {}


def kernel(**inputs):
    if "nc" not in _NC_CACHE:
        _NC_CACHE["nc"] = build_nc()
    nc = _NC_CACHE["nc"]
    maps = prep_inputs(inputs)
    res = run_bass_kernel_spmd(nc, maps, core_ids=list(range(8)))
    outs = [np.asarray(res.results[REAL_CORES[b]]["out"], dtype=np.float32).reshape(NXT * 128, D) for b in range(4)]
    return np.stack(outs, axis=0)
```

```python
import contextlib
import numpy as np
import concourse.bass as bass
import concourse.mybir as mybir
from concourse.bass_utils import run_bass_kernel_spmd

F32 = mybir.dt.float32
BF16 = mybir.dt.bfloat16
ALU = mybir.AluOpType
AF = mybir.ActivationFunctionType
AX = mybir.AxisListType

D = 1024
DEPTH = 4
NCTX = 2
NXT = 32
NT = NCTX + NXT
T = NT * 128
EPS = 1e-6
IN_DIM = 11872
O_AV, O_AB, O_AC, O_AZ = 0, 1024, 2048, 3072
O_BQ, O_BK, O_BV, O_BZ, O_AF, O_ABW = 4096, 4608, 5120, 6144, 7168, 7184
O_CQ, O_CKV, O_CKR, O_CZ = 7200, 7584, 7712, 7776
O_GA, O_GB, O_GC = 8800, 9824, 10848

SEM_PERIOD = 30000
NDMA_SLOTS = 8


class Buf:
    __slots__ = ("name", "excl", "last_w", "readers")

    def __init__(self, name, excl=False):
        self.name = name
        self.excl = excl
        self.last_w = None
        self.readers = []


class Op:
    __slots__ = ("eng", "fn", "reads", "writes", "dma", "deps", "signal", "tick", "idx")


class Prog:
    ENGS = ("pe", "act", "dve", "pool", "sp")

    def __init__(self, nc):
        self.nc = nc
        self.ops = []

    def add(self, eng, fn, reads=(), writes=(), dma=False):
        op = Op()
        op.eng, op.fn, op.dma = eng, fn, dma
        op.reads, op.writes = list(reads), list(writes)
        op.deps, op.signal, op.tick = [], False, None
        op.idx = len(self.ops)
        deps = {}
        for b in op.reads:
            if not b.excl and b.last_w is not None:
                deps[b.last_w.idx] = b.last_w
        wr = list(op.writes) + [b for b in op.reads if b.excl]
        for b in wr:
            if b.last_w is not None:
                deps[b.last_w.idx] = b.last_w
            for r in b.readers:
                deps[r.idx] = r
        for b in op.reads:
            if not b.excl:
                if not op.dma:
                    b.readers = [r for r in b.readers if r.dma or r.eng != op.eng]
                b.readers.append(op)
        for b in wr:
            b.last_w = op
            b.readers = []
        for d in deps.values():
            if d is op:
                continue
            if d.eng == "pe" and op.eng == "pe":
                continue
            op.deps.append(d)
            d.signal = True
        self.ops.append(op)
        return op

    def pe(self, fn, r=(), w=()):
        return self.add("pe", fn, r, w)

    def act(self, fn, r=(), w=()):
        return self.add("act", fn, r, w)

    def dve(self, fn, r=(), w=()):
        return self.add("dve", fn, r, w)

    def pool(self, fn, r=(), w=()):
        return self.add("pool", fn, r, w)

    def ld(self, fn, r=(), w=()):
        return self.add("sp", fn, r, w, dma=True)

    def wld(self, fn, r=(), w=()):
        return self.add("pool", fn, r, w, dma=True)

    def st(self, fn, r=(), w=()):
        return self.add("act", fn, r, w, dma=True)

    def emit(self, final_ops=()):
        nc = self.nc
        for o in final_ops:
            o.signal = True
        n_ticks = {e: 0 for e in self.ENGS}
        for op in self.ops:
            if op.signal and not op.dma:
                n_ticks[op.eng] += 1
        stack = contextlib.ExitStack()
        eng_sems, dma_sems = {}, {}
        for e in self.ENGS:
            n = max(1, (n_ticks[e] + SEM_PERIOD - 1) // SEM_PERIOD)
            eng_sems[e] = [stack.enter_context(nc.semaphore(f"s_{e}_{i}")) for i in range(n)]
            if any(op.dma and op.eng == e for op in self.ops):
                dma_sems[e] = [stack.enter_context(nc.semaphore(f"d_{e}_{i}")) for i in range(NDMA_SLOTS)]
        cnt = {e: 0 for e in self.ENGS}
        dcnt = {e: 0 for e in self.ENGS}
        for op in self.ops:
            if op.dma:
                k = dcnt[op.eng]
                dcnt[op.eng] += 1
                slot = k % NDMA_SLOTS
                op.tick = (dma_sems[op.eng][slot], 16 * (k // NDMA_SLOTS + 1), ("d", op.eng, slot))
            elif op.signal:
                k = cnt[op.eng]
                cnt[op.eng] += 1
                si = k // SEM_PERIOD
                op.tick = (eng_sems[op.eng][si], k % SEM_PERIOD + 1, ("e", op.eng, si))
        by_eng = {e: [op for op in self.ops if op.eng == e] for e in self.ENGS}
        final = list(final_ops)

        def run_engine(ename, eng):
            waited = {}
            for op in by_eng[ename]:
                needs = {}
                for d in op.deps:
                    sem, val, key = d.tick
                    if waited.get(key, 0) >= val:
                        continue
                    if key not in needs or needs[key][1] < val:
                        needs[key] = (sem, val)
                if op.dma:
                    sem, val, key = op.tick
                    if val > 16 and waited.get(key, 0) < val - 16:
                        if key not in needs or needs[key][1] < val - 16:
                            needs[key] = (sem, val - 16)
                for key, (sem, val) in needs.items():
                    eng.wait_ge(sem, val)
                    waited[key] = val
                ins = op.fn(eng)
                if op.dma:
                    ins.then_inc(op.tick[0], 16)
                elif op.signal:
                    ins.then_inc(op.tick[0], 1)
            if ename == "sp":
                for o in final:
                    sem, val, key = o.tick
                    eng.wait_ge(sem, val)

        with stack:
            with nc.Block() as block:
                @block.tensor
                def _(e):
                    run_engine("pe", e)

                @block.scalar
                def _(e):
                    run_engine("act", e)

                @block.vector
                def _(e):
                    run_engine("dve", e)

                @block.gpsimd
                def _(e):
                    run_engine("pool", e)

                @block.sync
                def _(e):
                    run_engine("sp", e)


class Tl:
    __slots__ = ("t", "b")

    def __init__(self, t, b):
        self.t, self.b = t, b


class Arena:
    def __init__(self, nc, stack, nbytes):
        self.a = stack.enter_context(nc.sbuf_tensor("arena", [128, nbytes // 2], BF16))
        self.nbytes = nbytes
        self.top = 0
        self.live = []
        self.dead = []

    def alloc(self, name, shape, dt, nb=0):
        esz = 4 if dt == F32 else 2
        n = 1
        for d_ in shape[1:]:
            n *= d_
        nbytes = (n * esz + 63) // 64 * 64
        off = self.top
        self.top += nbytes
        assert self.top <= self.nbytes, f"SBUF arena overflow at {name}: {self.top}"
        v = self.a[0:shape[0], off // 2:off // 2 + n * esz // 2]
        if dt == F32:
            v = v.bitcast(F32)
        if len(shape) > 2:
            names = "abcdef"[:len(shape) - 1]
            kw = {names[i]: shape[i + 1] for i in range(len(shape) - 2)}
            v = v.rearrange(f"p ({' '.join(names)}) -> p {' '.join(names)}", **kw)
        inherit = {}
        for (o0, o1, ops) in self.dead:
            if o0 < off + nbytes and off < o1:
                for op in ops:
                    inherit[op.idx] = op
        bufs = [Buf(f"{name}{i}") for i in range(nb)] if nb else [Buf(name)]
        for b in bufs:
            b.readers = list(inherit.values())
        self.live.append((off, off + nbytes, bufs))
        return Tl(v, bufs if nb else bufs[0])

    def release(self, mark):
        keep = []
        for (o0, o1, bufs) in self.live:
            if o0 >= mark:
                ops = {}
                for b in bufs:
                    if b.last_w is not None:
                        ops[b.last_w.idx] = b.last_w
                    for r in b.readers:
                        ops[r.idx] = r
                self.dead.append((o0, o1, list(ops.values())))
            else:
                keep.append((o0, o1, bufs))
        self.live = keep
        self.top = mark


class Scope:
    def __init__(self, ar):
        self.ar = ar

    def __enter__(self):
        self.mark = self.ar.top
        return self

    def __exit__(self, *a):
        self.ar.release(self.mark)
        return False

    def sb(self, name, shape, dt, nb=0):
        return self.ar.alloc(name, shape, dt, nb)


def build_nc(L=DEPTH, dbg=False):
    LW = L
    nc = bass.Bass("TRN2", target_bir_lowering=False)
    P = Prog(nc)

    def din(name, shape, dt=F32):
        return nc.dram_tensor(name, shape, dt, kind="ExternalInput").ap()

    xin = din("xin", [NT, 128, D])
    cvec = din("cvec", [128, 16])
    w_mod = din("w_mod", [LW, D, 3 * D])
    b_mod = din("b_mod", [LW, 3 * D])
    norm_g = din("norm_g", [LW, D])
    w_in = din("w_in", [LW, D, IN_DIM])
    w_krp = din("w_krp", [LW, D, 64])
    conv_wT = din("conv_wT", [LW, 128, 24])
    wa_f = din("wa_f", [LW, 17, 512])
    wa_b = din("wa_b", [LW, 17, 512])
    gla_g = din("gla_norm_g", [LW, D])
    qn_g = din("mla_q_norm_g", [LW, 384])
    kvn_g = din("mla_kv_norm_g", [LW, 128])
    wq_aug = din("wq_aug", [LW, 384, 2048])
    wkv = din("wkv", [LW, 128, 2048])
    gvec = din("gvec", [LW, 128, 8])
    w_br = [din(f"w_br_{s}", [LW, D, D]) for s in "abc"]
    w_out = din("w_out", [LW, D, D])
    ident_d = din("ident", [128, 128])
    tri_d = din("tri", [4, 128, 128])
    mask_d = din("mask", [2, 128, 128])
    cos_d = din("cosT", [64, T])
    sin_d = din("sinST", [64, T])
    out = nc.dram_tensor("out", [NXT, 128, D], F32, kind="ExternalOutput").ap()

    skind = "ExternalOutput" if dbg else "Internal"
    xs = nc.dram_tensor("xs", [NT, 128, D], F32, kind=skind).ap()
    hT_d = nc.dram_tensor("hT_d", [NT, 128, D], BF16, kind=skind).ap()
    of_d = nc.dram_tensor("of_d", [NT, 128, D], BF16, kind=skind).ap()
    ybT_d = nc.dram_tensor("ybT_d", [NT, 128, D], BF16, kind=skind).ap()
    ocT_d = nc.dram_tensor("ocT_d", [9, 128, 8 * 512], BF16, kind=skind).ap()
    WA_d = nc.dram_tensor("WA_d", [2, 8, 128, 4, 8, 128], BF16, kind="Internal").ap()
    WC_d = nc.dram_tensor("WC_d", [2, 8, 128, 8, 128], BF16, kind="Internal").ap()
    WG_d = nc.dram_tensor("WG_d", [2, 8, 128, 6, 8, 128], BF16, kind="Internal").ap()
    wa_db = [[Buf(f"wad{p}_{i}") for i in range(32)] for p in range(2)]
    wc_db = [[Buf(f"wcd{p}_{i}") for i in range(8)] for p in range(2)]
    wg_db = [[Buf(f"wgd{p}_{i}") for i in range(48)] for p in range(2)]
    xs_b = [Buf(f"xs{i}") for i in range(NT)]
    hT_db = [Buf(f"hTd{i}") for i in range(NT)]
    of_db = [Buf(f"ofd{i}") for i in range(NT)]
    ybT_db = [Buf(f"ybTd{i}") for i in range(NT)]
    ocT_db = [Buf(f"ocTd{i}") for i in range(9)]
    out_ops = []

    gstack = contextlib.ExitStack()
    with gstack:
        AR = Arena(nc, gstack, 200 * 1024)
        top = Scope(AR)
        top.__enter__()
        banks = []
        for i in range(8):
            t = gstack.enter_context(nc.psum_tensor(f"bank{i}", [128, 512], F32))
            banks.append(Tl(t, Buf(f"bank{i}", excl=True)))
        bank_ctr = [0]
        reserved = set()

        def pb():
            while True:
                k = bank_ctr[0] % 8
                bank_ctr[0] += 1
                if k not in reserved:
                    return banks[k]

        def bf(bank):
            return bank.t[:].bitcast(BF16)

        ident = top.sb("ident", [128, 128], BF16)
        ones = top.sb("ones", [128, 128], BF16)
        zeros = top.sb("zeros", [128, 128], F32)
        ones32 = top.sb("ones32", [128, 128], F32)
        tri = top.sb("tri", [128, 4, 128], F32)
        mask = top.sb("mask", [128, 2, 128], F32)
        cosT = top.sb("cosT", [64, T], BF16)
        sinT = top.sb("sinT", [64, T], BF16)
        cv = top.sb("cv", [128, 16], F32)
        screp = top.sb("screp", [128, 16, 128], BF16)
        rk_s = top.sb("rk_s", [128, NT, 8], F32)
        P.wld(lambda e: e.dma_start(out=ident.t[:], in_=ident_d[:, :]), [], [ident.b])
        P.ld(lambda e: e.dma_start(out=tri.t[:], in_=tri_d.rearrange("a p n -> p a n")), [], [tri.b])
        P.ld(lambda e: e.dma_start(out=mask.t[:], in_=mask_d.rearrange("a p n -> p a n")), [], [mask.b])
        P.wld(lambda e: e.dma_start(out=cosT.t[:], in_=cos_d[:, :]), [], [cosT.b])
        P.wld(lambda e: e.dma_start(out=sinT.t[:], in_=sin_d[:, :]), [], [sinT.b])
        P.ld(lambda e: e.dma_start(out=cv.t[:], in_=cvec[:, :]), [], [cv.b])
        P.pool(lambda e: e.memset(ones.t[:], 1.0), [], [ones.b])
        P.pool(lambda e: e.memset(zeros.t[:], 0.0), [], [zeros.b])
        P.pool(lambda e: e.memset(ones32.t[:], 1.0), [], [ones32.b])
        P.act(lambda e: e.activation(out=cv.t[:], in_=cv.t[:], func=AF.Silu), [cv.b], [cv.b])
        for c in range(16):
            P.act(lambda e, c=c: e.activation(out=screp.t[:, c, :], in_=zeros.t[:], func=AF.Identity, bias=cv.t[:, c:c + 1]),
                  [cv.b, zeros.b], [screp.b])


        A_OFFS = (O_AV, O_AB, O_AC, O_AZ)
        G_OFFS = (O_GA, O_GB, O_GC)

        def precast_jobs(l):
            par = l % 2
            jobs = []
            for a in range(4):
                for k in range(8):
                    jobs.append((lambda e, a=a, k=k: e.dma_start(out=WA_d[par][:, :, a, k, :].rearrange("f p n -> p f n"),
                                                                 in_=w_in[l, k * 128:(k + 1) * 128, A_OFFS[a]:A_OFFS[a] + 1024].rearrange("p (f n) -> p f n", n=128)),
                                 wa_db[par][a * 8 + k]))
            for k in range(8):
                jobs.append((lambda e, k=k: e.dma_start(out=WC_d[par][:, :, k, :].rearrange("f p n -> p f n"),
                                                        in_=w_in[l, k * 128:(k + 1) * 128, O_CZ:O_CZ + 1024].rearrange("p (f n) -> p f n", n=128)),
                             wc_db[par][k]))
            for g in range(3):
                for k in range(8):
                    jobs.append((lambda e, g=g, k=k: e.dma_start(out=WG_d[par][:, :, 3 + g, k, :].rearrange("f p n -> p f n"),
                                                                 in_=w_in[l, k * 128:(k + 1) * 128, G_OFFS[g]:G_OFFS[g] + 1024].rearrange("p (f n) -> p f n", n=128)),
                                 wg_db[par][(3 + g) * 8 + k]))
                    jobs.append((lambda e, g=g, k=k: e.dma_start(out=WG_d[par][:, :, g, k, :].rearrange("f p n -> p f n"),
                                                                 in_=w_br[g][l, k * 128:(k + 1) * 128, :].rearrange("p (f n) -> p f n", n=128)),
                                 wg_db[par][g * 8 + k]))
            return jobs

        def issue_precast(jobs):
            for fn, b in jobs:
                P.wld(fn, [], [b])

        issue_precast(precast_jobs(0))

        def emit_layer(l):
            last = (l == DEPTH - 1)
            x_src = xin if l == 0 else xs
            with Scope(AR) as ly:
                G = ly.sb("G", [128, 2, D], F32)
                gv = ly.sb("gv", [128, 8], F32)
                P.ld(lambda e: e.dma_start(out=gv.t[:], in_=gvec[l]), [], [gv.b])
                s14 = Scope(AR)
                s14.__enter__()
                cqnT = s14.sb("cqnT", [128, 3, T], BF16)
                ckvnT = s14.sb("ckvnT", [128, T], BF16)
                krT = s14.sb("krT", [128, T], BF16)
                P.pool(lambda e: e.memset(krT.t[64:128, :], 0.0), [], [krT.b])

                with Scope(AR) as s1:
                    AB = s1.sb("AB", [128, 4, D], F32)
                    grep = s1.sb("grep", [128, D], F32)
                    bmrep = s1.sb("bmrep", [128, 3 * D], F32)
                    gq_rep = s1.sb("gq_rep", [128, 512], F32)
                    P.ld(lambda e: e.dma_start(out=grep.t[:], in_=norm_g[l, :].partition_broadcast(128)), [], [grep.b])
                    P.ld(lambda e: e.dma_start(out=bmrep.t[:], in_=b_mod[l, :].partition_broadcast(128)), [], [bmrep.b])
                    P.ld(lambda e: e.dma_start(out=gq_rep.t[:, 0:384], in_=qn_g[l, :].partition_broadcast(128)), [], [gq_rep.b])
                    P.ld(lambda e: e.dma_start(out=gq_rep.t[:, 384:512], in_=kvn_g[l, :].partition_broadcast(128)), [], [gq_rep.b])
                    wm = s1.sb("wm", [128, 2, 8, 512], BF16, nb=2)
                    for n in range(6):
                        P.wld(lambda e, n=n: e.dma_start(out=wm.t[:, n % 2], in_=w_mod[l, :, n * 512:(n + 1) * 512].rearrange("(c p) n -> p c n", p=128)),
                              [], [wm.b[n % 2]])
                        for who in range(2):
                            bk = pb()
                            for k in range(8):
                                P.pe(lambda e, k=k, bk=bk, n=n, who=who: e.matmul(bk.t[:], lhsT=screp.t[:, (8 if who == 0 else 0) + k, :],
                                                                                  rhs=wm.t[:, n % 2, k, :], start=(k == 0), stop=(k == 7)),
                                     [screp.b, wm.b[n % 2]], [bk.b])
                            part, half = n // 2, n % 2
                            cs = slice(half * 512, (half + 1) * 512)
                            bms = bmrep.t[:, n * 512:(n + 1) * 512]
                            if part == 0:
                                P.dve(lambda e, bk=bk, who=who, cs=cs, bms=bms: e.tensor_tensor(out=AB.t[:, 2 * who + 1, cs], in0=bk.t[:], in1=bms, op=ALU.add),
                                      [bk.b, bmrep.b], [AB.b])
                            elif part == 1:
                                P.dve(lambda e, bk=bk, who=who, cs=cs, bms=bms: e.tensor_tensor(out=AB.t[:, 2 * who, cs], in0=bk.t[:], in1=bms, op=ALU.add),
                                      [bk.b, bmrep.b], [AB.b])
                                P.dve(lambda e, who=who, cs=cs: e.scalar_tensor_tensor(out=AB.t[:, 2 * who, cs], in0=AB.t[:, 2 * who, cs], scalar=1.0,
                                                                                       in1=grep.t[:, cs], op0=ALU.add, op1=ALU.mult),
                                      [AB.b, grep.b], [AB.b])
                            else:
                                P.dve(lambda e, bk=bk, who=who, cs=cs, bms=bms: e.tensor_tensor(out=G.t[:, who, cs], in0=bk.t[:], in1=bms, op=ALU.add),
                                      [bk.b, bmrep.b], [G.b])

                    wlat = s1.sb("wlat", [128, 8, 576], BF16)
                    wkrp = s1.sb("wkrp", [128, 8, 64], BF16)
                    wkvs = s1.sb("wkvs", [128, 2048], BF16)
                    P.wld(lambda e: e.dma_start(out=wlat.t[:], in_=w_in[l, :, O_CQ:O_CQ + 576].rearrange("(c p) n -> p c n", p=128)), [], [wlat.b])
                    P.wld(lambda e: e.dma_start(out=wkrp.t[:], in_=w_krp[l].rearrange("(c p) n -> p c n", p=128)), [], [wkrp.b])
                    P.wld(lambda e: e.dma_start(out=wkvs.t[:], in_=wkv[l]), [], [wkvs.b])

                    xt = s1.sb("xt", [128, 2, D], F32, nb=2)
                    junk = s1.sb("junk", [128, D], BF16)
                    st4 = s1.sb("st4", [128, 2, 8], F32, nb=2)
                    tnrm = s1.sb("tnrm", [128, 2, D], F32, nb=2)
                    hb = s1.sb("hb", [128, 2, D], BF16, nb=2)
                    hTt = s1.sb("hTt", [128, 2, 8, 128], BF16, nb=2)
                    cq = s1.sb("cq", [128, 2, 512], BF16, nb=2)
                    rt = s1.sb("rt", [64, 2, 2, 128], F32, nb=2)
                    sqk = s1.sb("sqk", [128, 2, D], F32, nb=2)
                    ssk = s1.sb("ssk", [128, 2, 8], F32, nb=2)
                    for i in range(NT):
                        p2 = i % 2
                        who = 0 if i < NCTX else 1
                        xb, sb_, tb, hbb, hTb, cqb, rtb, sqb, skb = (xt.b[p2], st4.b[p2], tnrm.b[p2], hb.b[p2], hTt.b[p2], cq.b[p2], rt.b[p2],
                                                                      sqk.b[p2], ssk.b[p2])
                        P.ld(lambda e, i=i, p2=p2: e.dma_start(out=xt.t[:, p2], in_=x_src[i]), [xs_b[i]] if l > 0 else [], [xb])
                        P.act(lambda e, p2=p2: e.activation(out=junk.t[:], in_=xt.t[:, p2], func=AF.Square, scale=1.0 / 32, accum_out=st4.t[:, p2, 0:1]),
                              [xb], [junk.b, sb_])
                        P.act(lambda e, p2=p2: e.activation(out=st4.t[:, p2, 1:2], in_=st4.t[:, p2, 0:1], func=AF.Sqrt, bias=EPS, scale=1.0), [sb_], [sb_])
                        P.dve(lambda e, p2=p2: e.reciprocal(out=st4.t[:, p2, 2:3], in_=st4.t[:, p2, 1:2]), [sb_], [sb_])
                        P.dve(lambda e, p2=p2, who=who: e.scalar_tensor_tensor(out=tnrm.t[:, p2], in0=xt.t[:, p2], scalar=st4.t[:, p2, 2:3],
                                                                                in1=AB.t[:, 2 * who], op0=ALU.mult, op1=ALU.mult),
                              [xb, sb_, AB.b], [tb])
                        P.pool(lambda e, p2=p2, who=who: e.tensor_tensor(out=hb.t[:, p2], in0=tnrm.t[:, p2], in1=AB.t[:, 2 * who + 1], op=ALU.add),
                               [tb, AB.b], [hbb])
                        bk = pb()
                        for c in range(8):
                            P.pe(lambda e, c=c, bk=bk, p2=p2: e.transpose(out=bf(bk)[:, c * 128:(c + 1) * 128], in_=hb.t[:, p2, c * 128:(c + 1) * 128],
                                                                           identity=ident.t[:]), [hbb, ident.b], [bk.b])
                        P.act(lambda e, bk=bk, p2=p2: e.activation(out=hTt.t[:, p2].rearrange("p c n -> p (c n)"), in_=bf(bk), func=AF.Copy),
                              [bk.b], [hTb])
                        P.st(lambda e, i=i, p2=p2: e.dma_start(out=hT_d[i], in_=hTt.t[:, p2].rearrange("p c n -> p (c n)")), [hTb], [hT_db[i]])
                        b1, b2 = pb(), pb()
                        for k in range(8):
                            P.pe(lambda e, k=k, b1=b1, p2=p2: e.matmul(b1.t[:], lhsT=hTt.t[:, p2, k, :], rhs=wlat.t[:, k, 0:512], start=(k == 0), stop=(k == 7)),
                                 [hTb, wlat.b], [b1.b])
                        for k in range(8):
                            P.pe(lambda e, k=k, b2=b2, p2=p2: e.matmul(b2.t[:, 0:64], lhsT=hTt.t[:, p2, k, :], rhs=wlat.t[:, k, 512:576], start=(k == 0), stop=(k == 7)),
                                 [hTb, wlat.b], [b2.b])
                        P.act(lambda e, b1=b1, p2=p2: e.activation(out=junk.t[:, 0:384], in_=b1.t[:, 0:384], func=AF.Square, scale=float(384 ** -0.5),
                                                                    accum_out=st4.t[:, p2, 3:4]), [b1.b], [junk.b, sb_])
                        P.act(lambda e, b1=b1, p2=p2: e.activation(out=junk.t[:, 384:512], in_=b1.t[:, 384:512], func=AF.Square, scale=float(128 ** -0.5),
                                                                    accum_out=st4.t[:, p2, 4:5]), [b1.b], [junk.b, sb_])
                        P.act(lambda e, b2=b2, p2=p2: e.activation(out=junk.t[:, 512:576], in_=b2.t[:, 0:64], func=AF.Square,
                                                                    accum_out=st4.t[:, p2, 7:8]), [b2.b], [junk.b, sb_])
                        P.act(lambda e, p2=p2: e.activation(out=st4.t[:, p2, 5:7], in_=st4.t[:, p2, 3:5], func=AF.Sqrt, bias=EPS, scale=1.0), [sb_], [sb_])
                        P.dve(lambda e, p2=p2: e.reciprocal(out=st4.t[:, p2, 5:7], in_=st4.t[:, p2, 5:7]), [sb_], [sb_])
                        P.dve(lambda e, b1=b1, p2=p2: e.scalar_tensor_tensor(out=cq.t[:, p2, 0:384], in0=b1.t[:, 0:384], scalar=st4.t[:, p2, 5:6],
                                                                              in1=gq_rep.t[:, 0:384], op0=ALU.mult, op1=ALU.mult),
                              [b1.b, sb_, gq_rep.b], [cqb])
                        P.dve(lambda e, b1=b1, p2=p2: e.scalar_tensor_tensor(out=cq.t[:, p2, 384:512], in0=b1.t[:, 384:512], scalar=st4.t[:, p2, 6:7],
                                                                              in1=gq_rep.t[:, 384:512], op0=ALU.mult, op1=ALU.mult),
                              [b1.b, sb_, gq_rep.b], [cqb])
                        b3 = pb()
                        for c in range(4):
                            P.pe(lambda e, c=c, b3=b3, p2=p2: e.transpose(out=bf(b3)[:, c * 128:(c + 1) * 128], in_=cq.t[:, p2, c * 128:(c + 1) * 128],
                                                                           identity=ident.t[:]), [cqb, ident.b], [b3.b])
                        tsl = slice(i * 128, (i + 1) * 128)
                        P.act(lambda e, b3=b3, tsl=tsl: e.activation(out=cqnT.t[:, :, tsl], in_=bf(b3)[:, 0:384].rearrange("p (c n) -> p c n", c=3), func=AF.Copy),
                              [b3.b], [cqnT.b])
                        P.act(lambda e, b3=b3, tsl=tsl: e.activation(out=ckvnT.t[:, tsl], in_=bf(b3)[:, 384:512], func=AF.Copy), [b3.b], [ckvnT.b])
                        b4 = pb()
                        for k in range(8):
                            P.pe(lambda e, k=k, b4=b4, p2=p2: e.matmul(b4.t[0:64, 0:128], lhsT=wlat.t[:, k, 512:576], rhs=hTt.t[:, p2, k, :], start=(k == 0), stop=(k == 7)),
                                 [hTb, wlat.b], [b4.b])
                        for k in range(8):
                            P.pe(lambda e, k=k, b4=b4, p2=p2: e.matmul(b4.t[0:64, 128:256], lhsT=wkrp.t[:, k, :], rhs=hTt.t[:, p2, k, :], start=(k == 0), stop=(k == 7)),
                                 [hTb, wkrp.b], [b4.b])
                        P.dve(lambda e, b4=b4, p2=p2, tsl=tsl: e.scalar_tensor_tensor(out=rt.t[:, p2, 0], in0=b4.t[0:64, 0:128], scalar=gv.t[0:64, 4:5],
                                                                                       in1=cosT.t[:, tsl], op0=ALU.mult, op1=ALU.mult),
                              [b4.b, gv.b, cosT.b], [rtb])
                        P.dve(lambda e, b4=b4, p2=p2, tsl=tsl: e.scalar_tensor_tensor(out=rt.t[:, p2, 1], in0=b4.t[0:64, 128:256], scalar=gv.t[0:64, 5:6],
                                                                                       in1=sinT.t[:, tsl], op0=ALU.mult, op1=ALU.mult),
                              [b4.b, gv.b, sinT.b], [rtb])
                        P.pool(lambda e, p2=p2, tsl=tsl: e.tensor_tensor(out=krT.t[0:64, tsl], in0=rt.t[:, p2, 0], in1=rt.t[:, p2, 1], op=ALU.add), [rtb], [krT.b])
                        b5, b6 = pb(), pb()
                        for hh, bb in ((0, b5), (1, b6)):
                            P.pe(lambda e, hh=hh, bb=bb, tsl=tsl: e.matmul(bb.t[:], lhsT=ckvnT.t[:, tsl],
                                                                            rhs=wkvs.t[:].rearrange("p (h x) -> p h x", h=8)[:, hh * 4:(hh + 1) * 4, 0:128],
                                                                            start=True, stop=True), [ckvnT.b, wkvs.b], [bb.b])
                            P.act(lambda e, hh=hh, bb=bb, p2=p2: e.activation(out=sqk.t[:, p2, hh * 512:(hh + 1) * 512], in_=bb.t[:], func=AF.Square),
                                  [bb.b], [sqb])
                        P.dve(lambda e, p2=p2: e.tensor_reduce(out=ssk.t[:, p2], in_=sqk.t[:, p2].rearrange("p (h x) -> p h x", h=8), axis=AX.X, op=ALU.add),
                              [sqb], [skb])
                        P.dve(lambda e, p2=p2: e.tensor_scalar(out=ssk.t[:, p2], in0=ssk.t[:, p2], scalar1=st4.t[:, p2, 7:8], scalar2=1.0 / 192,
                                                               op0=ALU.add, op1=ALU.mult), [skb, sb_], [skb])
                        P.act(lambda e, p2=p2: e.activation(out=ssk.t[:, p2], in_=ssk.t[:, p2], func=AF.Sqrt, bias=EPS, scale=1.0), [skb], [skb])
                        P.dve(lambda e, p2=p2: e.reciprocal(out=ssk.t[:, p2], in_=ssk.t[:, p2]), [skb], [skb])
                        P.dve(lambda e, p2=p2, i=i: e.tensor_scalar(out=rk_s.t[:, i, :], in0=ssk.t[:, p2], scalar1=float(192 ** -0.5), scalar2=None, op0=ALU.mult),
                              [skb], [rk_s.b])

                if dbg == 1:
                    d_cq = nc.dram_tensor("d_cq", [128, 3 * T], BF16, kind="ExternalOutput").ap()
                    d_ckv = nc.dram_tensor("d_ckv", [128, T], BF16, kind="ExternalOutput").ap()
                    d_kr = nc.dram_tensor("d_kr", [64, T], BF16, kind="ExternalOutput").ap()
                    d_rk = nc.dram_tensor("d_rk", [128, NT * 8], F32, kind="ExternalOutput").ap()
                    d_G = nc.dram_tensor("d_G", [128, 2 * D], F32, kind="ExternalOutput").ap()
                    out_ops.append(P.ld(lambda e: e.dma_start(out=d_cq[:, :], in_=cqnT.t[:].rearrange("p c n -> p (c n)")), [cqnT.b], []))
                    out_ops.append(P.ld(lambda e: e.dma_start(out=d_ckv[:, :], in_=ckvnT.t[:]), [ckvnT.b], []))
                    out_ops.append(P.ld(lambda e: e.dma_start(out=d_kr[:, :], in_=krT.t[0:64, :]), [krT.b], []))
                    out_ops.append(P.ld(lambda e: e.dma_start(out=d_rk[:, :], in_=rk_s.t[:].rearrange("p a b -> p (a b)")), [rk_s.b], []))
                    out_ops.append(P.ld(lambda e: e.dma_start(out=d_G[:, :], in_=G.t[:].rearrange("p a b -> p (a b)")), [G.b], []))
                    out_ops.append(P.ld(lambda e: e.dma_start(out=out[1], in_=xin[3]), hT_db, []))
                    s14.__exit__(None, None, None)
                    return True

                with Scope(AR) as s3:
                    wgT = s3.sb("wgT", [128, 8, 2560], BF16)
                    wgq = s3.sb("wgq", [128, 8, 512], BF16)
                    waf = s3.sb("waf", [128, 8, 32], BF16)
                    waa = s3.sb("waa", [17, 2, 512], BF16)
                    gg_rep = s3.sb("gg_rep", [128, D], F32)
                    P.wld(lambda e: e.dma_start(out=wgT.t[:], in_=w_in[l, :, O_BK:O_BK + 2560].rearrange("(c p) n -> p c n", p=128)), [], [wgT.b])
                    P.wld(lambda e: e.dma_start(out=wgq.t[:], in_=w_in[l, :, O_BQ:O_BQ + 512].rearrange("(c p) n -> p c n", p=128)), [], [wgq.b])
                    P.wld(lambda e: e.dma_start(out=waf.t[:], in_=w_in[l, :, O_AF:O_AF + 32].rearrange("(c p) n -> p c n", p=128)), [], [waf.b])
                    P.wld(lambda e: e.dma_start(out=waa.t[:, 0, :], in_=wa_f[l]), [], [waa.b])
                    P.wld(lambda e: e.dma_start(out=waa.t[:, 1, :], in_=wa_b[l]), [], [waa.b])
                    P.ld(lambda e: e.dma_start(out=gg_rep.t[:], in_=gla_g[l, :].partition_broadcast(128)), [], [gg_rep.b])
                    S = s3.sb("S", [128, 4, 256], F32)
                    Sbf = s3.sb("Sbf", [128, 4, 256], BF16)
                    baf = s3.sb("baf", [17, 128], BF16)
                    P.pool(lambda e: e.memset(baf.t[:], 1.0), [], [baf.b])
                    hTg = s3.sb("hTg", [128, 2, 8, 128], BF16, nb=2)
                    lsp = s3.sb("lsp", [128, 2, 512], F32, nb=2)
                    ec = s3.sb("ec", [128, 2, 512], F32, nb=2)
                    kend = s3.sb("kend", [128, 2, 512], BF16, nb=2)
                    vv = s3.sb("vv", [128, 2, 1024], BF16, nb=2)
                    ebT = s3.sb("ebT", [128, 2, 4, 128], F32, nb=2)
                    enbT = s3.sb("enbT", [128, 2, 4, 128], F32, nb=2)
                    qd = s3.sb("qd", [128, 2, 4, 128], BF16, nb=2)
                    ki = s3.sb("ki", [128, 2, 4, 128], BF16, nb=2)
                    AT = s3.sb("AT", [128, 2, 4, 128], BF16, nb=2)
                    ofs = s3.sb("ofs", [128, 2, 1024], BF16, nb=2)
                    osum = s3.sb("osum", [128, 1024], F32)
                    junk3 = s3.sb("junk3", [128, 256], BF16)
                    stt = s3.sb("stt", [128, 2, 8], F32, nb=2)
                    sz = s3.sb("sz", [128, 1024], F32)
                    ybt = s3.sb("ybt", [128, 1024], BF16)
                    ybTt = s3.sb("ybTt", [128, 2, 1024], BF16, nb=2)
                    orders = (list(range(NT)), [1, 0] + list(range(NT - 1, 1, -1)))
                    for ps_ in (0, 1):
                        gcol = 127 if ps_ == 0 else 0
                        P.pool(lambda e: e.memset(S.t[:], 0.0), [], [S.b])
                        P.pool(lambda e: e.memset(Sbf.t[:], 0.0), [], [Sbf.b])
                        for n, i in enumerate(orders[ps_]):
                            p2 = n % 2
                            need_out = not (last and i < NCTX)
                            hB = hTg.b[p2]
                            P.ld(lambda e, i=i, p2=p2: e.dma_start(out=hTg.t[:, p2].rearrange("p c n -> p (c n)"), in_=hT_d[i]), [hT_db[i]], [hB])
                            bk = pb()
                            for k in range(8):
                                P.pe(lambda e, k=k, bk=bk, p2=p2, ps_=ps_: e.matmul(bk.t[0:16, 0:128], lhsT=waf.t[:, k, ps_ * 16:(ps_ + 1) * 16], rhs=hTg.t[:, p2, k, :],
                                                                                   start=(k == 0), stop=(k == 7)), [hB, waf.b], [bk.b])
                            P.act(lambda e, bk=bk: e.activation(out=baf.t[0:16, :], in_=bk.t[0:16, 0:128], func=AF.Copy), [bk.b], [baf.b])
                            bz = pb()
                            P.pe(lambda e, bz=bz, ps_=ps_: e.matmul(bz.t[:], lhsT=baf.t[:, :], rhs=waa.t[:, ps_, :], start=True, stop=True), [baf.b, waa.b], [bz.b])
                            P.act(lambda e, bz=bz, p2=p2: e.activation(out=lsp.t[:, p2], in_=bz.t[:], func=AF.Exp, scale=-1.0), [bz.b], [lsp.b[p2]])
                            P.act(lambda e, p2=p2: e.activation(out=lsp.t[:, p2], in_=lsp.t[:, p2], func=AF.Ln, bias=1.0, scale=1.0), [lsp.b[p2]], [lsp.b[p2]])
                            bc = pb()
                            P.pe(lambda e, bc=bc, p2=p2, ps_=ps_: e.matmul(bc.t[:], lhsT=tri.t[:, 2 * ps_ + 1, :], rhs=lsp.t[:, p2, :], start=True, stop=True),
                                 [tri.b, lsp.b[p2]], [bc.b])
                            bb = pb()
                            for h in range(4):
                                P.pe(lambda e, bb=bb, h=h, p2=p2, ps_=ps_: e.matmul(bb.t[:, h * 128:(h + 1) * 128], lhsT=lsp.t[:, p2, h * 128:(h + 1) * 128],
                                                                                   rhs=tri.t[:, 2 * ps_, :], start=True, stop=True), [tri.b, lsp.b[p2]], [bb.b])
                            P.act(lambda e, bc=bc, p2=p2: e.activation(out=ec.t[:, p2], in_=bc.t[:], func=AF.Exp), [bc.b], [ec.b[p2]])
                            P.act(lambda e, bb=bb, p2=p2: e.activation(out=ebT.t[:, p2].rearrange("p h n -> p (h n)"), in_=bb.t[:], func=AF.Exp), [bb.b], [ebT.b[p2]])
                            P.act(lambda e, bb=bb, p2=p2: e.activation(out=enbT.t[:, p2].rearrange("p h n -> p (h n)"), in_=bb.t[:], func=AF.Exp, scale=-1.0),
                                  [bb.b], [enbT.b[p2]])
                            bkk = pb()
                            for k in range(8):
                                P.pe(lambda e, k=k, bkk=bkk, p2=p2: e.matmul(bkk.t[:], lhsT=hTg.t[:, p2, k, :], rhs=wgT.t[:, k, 0:512], start=(k == 0), stop=(k == 7)),
                                     [hB, wgT.b], [bkk.b])
                            P.dve(lambda e, bkk=bkk, p2=p2: e.tensor_tensor(out=kend.t[:, p2], in0=bkk.t[:], in1=ec.t[:, p2], op=ALU.mult), [bkk.b, ec.b[p2]], [kend.b[p2]])
                            for half in range(2):
                                bv = pb()
                                for k in range(8):
                                    P.pe(lambda e, k=k, bv=bv, p2=p2, half=half: e.matmul(bv.t[:], lhsT=hTg.t[:, p2, k, :], rhs=wgT.t[:, k, 512 + half * 512:1024 + half * 512],
                                                                                         start=(k == 0), stop=(k == 7)), [hB, wgT.b], [bv.b])
                                P.act(lambda e, bv=bv, p2=p2, half=half: e.activation(out=vv.t[:, p2, half * 512:(half + 1) * 512], in_=bv.t[:], func=AF.Copy),
                                      [bv.b], [vv.b[p2]])
                            bq = pb()
                            for h in range(4):
                                for k in range(8):
                                    P.pe(lambda e, k=k, h=h, bq=bq, p2=p2: e.matmul(bq.t[:, h * 128:(h + 1) * 128], lhsT=wgq.t[:, k, h * 128:(h + 1) * 128], rhs=hTg.t[:, p2, k, :],
                                                                                   start=(k == 0), stop=(k == 7)), [hB, wgq.b], [bq.b])
                            P.dve(lambda e, bq=bq, p2=p2: e.scalar_tensor_tensor(out=qd.t[:, p2].rearrange("p h n -> p (h n)"), in0=bq.t[:], scalar=float(128 ** -0.5),
                                                                                  in1=ebT.t[:, p2].rearrange("p h n -> p (h n)"), op0=ALU.mult, op1=ALU.mult),
                                  [bq.b, ebT.b[p2]], [qd.b[p2]])
                            bkT = pb()
                            for h in range(4):
                                for k in range(8):
                                    P.pe(lambda e, k=k, h=h, bkT=bkT, p2=p2: e.matmul(bkT.t[:, h * 128:(h + 1) * 128], lhsT=wgT.t[:, k, h * 128:(h + 1) * 128], rhs=hTg.t[:, p2, k, :],
                                                                                     start=(k == 0), stop=(k == 7)), [hB, wgT.b], [bkT.b])
                            P.dve(lambda e, bkT=bkT, p2=p2: e.tensor_tensor(out=ki.t[:, p2].rearrange("p h n -> p (h n)"), in0=bkT.t[:],
                                                                             in1=enbT.t[:, p2].rearrange("p h n -> p (h n)"), op=ALU.mult), [bkT.b, enbT.b[p2]], [ki.b[p2]])
                            if need_out:
                                ba = pb()
                                for h in range(4):
                                    P.pe(lambda e, h=h, ba=ba, p2=p2: e.matmul(ba.t[:, h * 128:(h + 1) * 128], lhsT=ki.t[:, p2, h, :], rhs=qd.t[:, p2, h, :], start=True, stop=True),
                                         [ki.b[p2], qd.b[p2]], [ba.b])
                                P.dve(lambda e, ba=ba, p2=p2, ps_=ps_: e.tensor_tensor(out=AT.t[:, p2], in0=ba.t[:].rearrange("p (h n) -> p h n", h=4),
                                                                                      in1=mask.t[:, ps_, :].unsqueeze(1).to_broadcast([128, 4, 128]), op=ALU.mult),
                                      [ba.b, mask.b], [AT.b[p2]])
                                bo = (pb(), pb())
                                for h in range(4):
                                    b_ = bo[h // 2]
                                    cs = (h % 2) * 256
                                    P.pe(lambda e, h=h, b_=b_, cs=cs, p2=p2: e.matmul(b_.t[:, cs:cs + 256], lhsT=AT.t[:, p2, h, :], rhs=vv.t[:, p2, h * 256:(h + 1) * 256],
                                                                                     start=True, stop=False), [AT.b[p2], vv.b[p2]], [b_.b])
                                    P.pe(lambda e, h=h, b_=b_, cs=cs, p2=p2: e.matmul(b_.t[:, cs:cs + 256], lhsT=qd.t[:, p2, h, :], rhs=Sbf.t[:, h, :],
                                                                                     start=False, stop=True), [qd.b[p2], Sbf.b], [b_.b])
                                if ps_ == 0:
                                    for half in range(2):
                                        P.act(lambda e, half=half, p2=p2, b_=bo[half]: e.activation(out=ofs.t[:, p2, half * 512:(half + 1) * 512], in_=b_.t[:], func=AF.Copy),
                                              [bo[half].b], [ofs.b[p2]])
                                    P.st(lambda e, i=i, p2=p2: e.dma_start(out=of_d[i], in_=ofs.t[:, p2]), [ofs.b[p2]], [of_db[i]])
                                else:
                                    P.ld(lambda e, i=i, p2=p2: e.dma_start(out=ofs.t[:, p2], in_=of_d[i]), [of_db[i]], [ofs.b[p2]])
                                    for half in range(2):
                                        P.dve(lambda e, half=half, p2=p2, b_=bo[half]: e.tensor_tensor(out=osum.t[:, half * 512:(half + 1) * 512], in0=b_.t[:],
                                                                                                         in1=ofs.t[:, p2, half * 512:(half + 1) * 512], op=ALU.add),
                                              [bo[half].b, ofs.b[p2]], [osum.b])
                                    for h in range(4):
                                        P.act(lambda e, h=h, p2=p2: e.activation(out=junk3.t[:], in_=osum.t[:, h * 256:(h + 1) * 256], func=AF.Square, scale=1.0 / 16,
                                                                                 accum_out=stt.t[:, p2, h:h + 1]), [osum.b], [junk3.b, stt.b[p2]])
                                    P.act(lambda e, p2=p2: e.activation(out=stt.t[:, p2, 4:8], in_=stt.t[:, p2, 0:4], func=AF.Sqrt, bias=EPS, scale=1.0), [stt.b[p2]], [stt.b[p2]])
                                    P.dve(lambda e, p2=p2: e.reciprocal(out=stt.t[:, p2, 4:8], in_=stt.t[:, p2, 4:8]), [stt.b[p2]], [stt.b[p2]])
                                    for half in range(2):
                                        bzz = pb()
                                        for k in range(8):
                                            P.pe(lambda e, k=k, bzz=bzz, p2=p2, half=half: e.matmul(bzz.t[:], lhsT=hTg.t[:, p2, k, :], rhs=wgT.t[:, k, 1536 + half * 512:2048 + half * 512],
                                                                                                   start=(k == 0), stop=(k == 7)), [hB, wgT.b], [bzz.b])
                                        P.act(lambda e, bzz=bzz, half=half: e.activation(out=sz.t[:, half * 512:(half + 1) * 512], in_=bzz.t[:], func=AF.Silu), [bzz.b], [sz.b])
                                    P.pool(lambda e: e.tensor_tensor(out=sz.t[:], in0=sz.t[:], in1=gg_rep.t[:], op=ALU.mult), [sz.b, gg_rep.b], [sz.b])
                                    for h in range(4):
                                        P.dve(lambda e, h=h, p2=p2: e.scalar_tensor_tensor(out=ybt.t[:, h * 256:(h + 1) * 256], in0=osum.t[:, h * 256:(h + 1) * 256],
                                                                                           scalar=stt.t[:, p2, 4 + h:5 + h], in1=sz.t[:, h * 256:(h + 1) * 256],
                                                                                           op0=ALU.mult, op1=ALU.mult), [osum.b, stt.b[p2], sz.b], [ybt.b])
                                    bt = pb()
                                    for c in range(8):
                                        P.pe(lambda e, c=c, bt=bt: e.transpose(out=bf(bt)[:, c * 128:(c + 1) * 128], in_=ybt.t[:, c * 128:(c + 1) * 128], identity=ident.t[:]),
                                             [ybt.b, ident.b], [bt.b])
                                    P.act(lambda e, bt=bt, p2=p2: e.activation(out=ybTt.t[:, p2], in_=bf(bt), func=AF.Copy), [bt.b], [ybTt.b[p2]])
                                    P.st(lambda e, i=i, p2=p2: e.dma_start(out=ybT_d[i], in_=ybTt.t[:, p2]), [ybTt.b[p2]], [ybT_db[i]])
                            bu = (pb(), pb())
                            for h in range(4):
                                b_ = bu[h // 2]
                                cs = (h % 2) * 256
                                P.pe(lambda e, h=h, b_=b_, cs=cs, p2=p2: e.matmul(b_.t[:, cs:cs + 256], lhsT=kend.t[:, p2, h * 128:(h + 1) * 128], rhs=vv.t[:, p2, h * 256:(h + 1) * 256],
                                                                                 start=True, stop=True), [kend.b[p2], vv.b[p2]], [b_.b])
                            for h in range(4):
                                b_ = bu[h // 2]
                                cs = (h % 2) * 256
                                P.dve(lambda e, h=h, b_=b_, cs=cs, p2=p2, gcol=gcol: e.scalar_tensor_tensor(out=S.t[:, h, :], in0=S.t[:, h, :], scalar=ebT.t[:, p2, h, gcol:gcol + 1],
                                                                                                           in1=b_.t[:, cs:cs + 256], op0=ALU.mult, op1=ALU.add),
                                      [S.b, ebT.b[p2], b_.b], [S.b])
                            P.pool(lambda e: e.tensor_copy(out=Sbf.t[:], in_=S.t[:]), [S.b], [Sbf.b])
                if dbg == 3:
                    out_ops.append(P.ld(lambda e: e.dma_start(out=out[1], in_=xin[3]), hT_db + of_db + ybT_db, []))
                    s14.__exit__(None, None, None)
                    return True

                with Scope(AR) as s4:
                    wkvs4 = s4.sb("wkvs4", [128, 2048], BF16)
                    wqs = s4.sb("wqs", [128, 3, 2048], BF16)
                    P.wld(lambda e: e.dma_start(out=wkvs4.t[:], in_=wkv[l]), [], [wkvs4.b])
                    P.wld(lambda e: e.dma_start(out=wqs.t[:], in_=wq_aug[l].rearrange("(c p) n -> p c n", p=128)), [], [wqs.b])
                    KhT = s4.sb("KhT", [128, T], BF16)
                    Vh = s4.sb("Vh", [128, NT, 128], BF16)
                    QnT = s4.sb("QnT", [128, 2, 512], BF16, nb=2)
                    QrT = s4.sb("QrT", [128, 2, 512], BF16, nb=2)
                    P.pool(lambda e: e.memset(QrT.t[64:128], 0.0), [], [QrT.b[0], QrT.b[1]])
                    sq1 = s4.sb("sq1", [128, 512], BF16)
                    sq2 = s4.sb("sq2", [64, 512], BF16)
                    rq = s4.sb("rq", [128, 2, 512], F32, nb=2)
                    t12 = s4.sb("t12", [64, 2, 512], F32)
                    PT = s4.sb("PT", [128, 4, 512], BF16, nb=4)
                    rs = s4.sb("rs", [128, 512], F32)
                    oc = s4.sb("oc", [128, 2, 512], BF16, nb=2)
                    accS = s4.sb("accS", [128, 2, 2, 512], F32, nb=4)
                    acc_banks = [banks[0], banks[1]]
                    reserved.update({0, 1})
                    qctr = 0
                    qtiles = ([] if last else [(0, 0, 256, [0, 1])]) + [(1 + q, 256 + q * 512, 512, list(range(NT))) for q in range(8)]
                    nxt_jobs = precast_jobs(l + 1) if l + 1 < L else []
                    for h in range(8):
                        issue_precast(nxt_jobs[h * 11:(h + 1) * 11])
                        for kt in range(9):
                            n = 512 if kt < 8 else 256
                            cols = slice(kt * 512, kt * 512 + n)
                            bk = pb()
                            P.pe(lambda e, bk=bk, n=n, cols=cols, h=h: e.matmul(bk.t[:, 0:n], lhsT=wkvs4.t[:, h * 256:h * 256 + 128], rhs=ckvnT.t[:, cols], start=True, stop=True),
                                 [wkvs4.b, ckvnT.b], [bk.b])
                            P.act(lambda e, bk=bk, n=n, cols=cols: e.activation(out=KhT.t[:, cols], in_=bk.t[:, 0:n], func=AF.Identity, scale=gv.t[:, 1:2]), [bk.b, gv.b], [KhT.b])
                        for g in range(9):
                            tl_ = list(range(4 * g, min(4 * g + 4, NT)))
                            bk = pb()
                            for jj, j in enumerate(tl_):
                                P.pe(lambda e, bk=bk, jj=jj, j=j, h=h: e.matmul(bk.t[:, jj * 128:(jj + 1) * 128], lhsT=ckvnT.t[:, j * 128:(j + 1) * 128],
                                                                               rhs=wkvs4.t[:, h * 256 + 128:h * 256 + 256], start=True, stop=True), [wkvs4.b, ckvnT.b], [bk.b])
                            nn = len(tl_)
                            P.act(lambda e, bk=bk, g=g, nn=nn: e.activation(out=Vh.t[:, 4 * g:4 * g + nn, :], in_=bk.t[:, 0:nn * 128].rearrange("p (j n) -> p j n", j=nn), func=AF.Copy),
                                  [bk.b], [Vh.b])
                        for (tg, t0, n, keys) in qtiles:
                            qp = qctr % 2
                            qctr += 1
                            bO = acc_banks[qp]
                            qs = slice(t0, t0 + n)
                            b1, b2, b3 = pb(), pb(), pb()
                            for (bq_, c0, m_) in ((b1, 0, 128), (b2, 128, 64), (b3, 192, 64)):
                                for c in range(3):
                                    P.pe(lambda e, bq_=bq_, c0=c0, m_=m_, c=c, n=n, qs=qs, h=h: e.matmul(bq_.t[0:m_, 0:n], lhsT=wqs.t[:, c, h * 256 + c0:h * 256 + c0 + m_], rhs=cqnT.t[:, c, qs],
                                                                                                        start=(c == 0), stop=(c == 2)), [wqs.b, cqnT.b], [bq_.b])
                            P.act(lambda e, b1=b1, n=n: e.activation(out=sq1.t[:, 0:n], in_=b1.t[:, 0:n], func=AF.Square), [b1.b], [sq1.b])
                            P.act(lambda e, b2=b2, n=n: e.activation(out=sq2.t[:, 0:n], in_=b2.t[0:64, 0:n], func=AF.Square), [b2.b], [sq2.b])
                            b4 = pb()
                            P.pe(lambda e, b4=b4, n=n: e.matmul(b4.t[:, 0:n], lhsT=ones.t[:, :], rhs=sq1.t[:, 0:n], start=True, stop=False), [ones.b, sq1.b], [b4.b])
                            P.pe(lambda e, b4=b4, n=n: e.matmul(b4.t[:, 0:n], lhsT=ones.t[0:64, :], rhs=sq2.t[:, 0:n], start=False, stop=True), [ones.b, sq2.b], [b4.b])
                            P.act(lambda e, b4=b4, n=n, qp=qp: e.activation(out=rq.t[:, qp, 0:n], in_=b4.t[:, 0:n], func=AF.Sqrt, bias=EPS, scale=1.0 / 192), [b4.b], [rq.b[qp]])
                            P.dve(lambda e, n=n, qp=qp: e.reciprocal(out=rq.t[:, qp, 0:n], in_=rq.t[:, qp, 0:n]), [rq.b[qp]], [rq.b[qp]])
                            P.dve(lambda e, b1=b1, n=n, qp=qp: e.scalar_tensor_tensor(out=QnT.t[:, qp, 0:n], in0=b1.t[:, 0:n], scalar=gv.t[:, 0:1], in1=rq.t[:, qp, 0:n],
                                                                                      op0=ALU.mult, op1=ALU.mult), [b1.b, gv.b, rq.b[qp]], [QnT.b[qp]])
                            P.dve(lambda e, b2=b2, n=n, qs=qs: e.scalar_tensor_tensor(out=t12.t[:, 0, 0:n], in0=b2.t[0:64, 0:n], scalar=gv.t[0:64, 2:3], in1=cosT.t[:, qs],
                                                                                      op0=ALU.mult, op1=ALU.mult), [b2.b, gv.b, cosT.b], [t12.b])
                            P.dve(lambda e, b3=b3, n=n, qs=qs: e.scalar_tensor_tensor(out=t12.t[:, 1, 0:n], in0=b3.t[0:64, 0:n], scalar=gv.t[0:64, 3:4], in1=sinT.t[:, qs],
                                                                                      op0=ALU.mult, op1=ALU.mult), [b3.b, gv.b, sinT.b], [t12.b])
                            P.pool(lambda e, n=n: e.tensor_tensor(out=t12.t[:, 0, 0:n], in0=t12.t[:, 0, 0:n], in1=t12.t[:, 1, 0:n], op=ALU.add), [t12.b], [t12.b])
                            P.pool(lambda e, n=n, qp=qp: e.tensor_tensor(out=QrT.t[0:64, qp, 0:n], in0=t12.t[:, 0, 0:n], in1=rq.t[0:64, qp, 0:n], op=ALU.mult),
                                   [t12.b, rq.b[qp]], [QrT.b[qp]])
                            nk = len(keys)
                            sbanks = {}
                            DEP = 2
                            for jj in range(nk + DEP):
                                if jj < nk:
                                    j = keys[jj]
                                    bs = pb()
                                    sbanks[jj] = bs
                                    ks = slice(j * 128, (j + 1) * 128)
                                    P.pe(lambda e, bs=bs, n=n, ks=ks, qp=qp: e.matmul(bs.t[:, 0:n], lhsT=KhT.t[:, ks], rhs=QnT.t[:, qp, 0:n], start=True, stop=False),
                                         [KhT.b, QnT.b[qp]], [bs.b])
                                    P.pe(lambda e, bs=bs, n=n, ks=ks, qp=qp: e.matmul(bs.t[:, 0:n], lhsT=krT.t[:, ks], rhs=QrT.t[:, qp, 0:n], start=False, stop=True),
                                         [krT.b, QrT.b[qp]], [bs.b])
                                    sl = jj % 4
                                    P.act(lambda e, bs=bs, n=n, sl=sl, j=j, h=h: e.activation(out=PT.t[:, sl, 0:n], in_=bs.t[:, 0:n], func=AF.Exp, scale=rk_s.t[:, j, h:h + 1]),
                                          [bs.b, rk_s.b], [PT.b[sl]])
                                if jj >= DEP:
                                    jv = jj - DEP
                                    j = keys[jv]
                                    sl = jv % 4
                                    P.pe(lambda e, bO=bO, n=n, sl=sl, j=j, jv=jv, nk=nk: e.matmul(bO.t[:, 0:n], lhsT=Vh.t[:, j, :], rhs=PT.t[:, sl, 0:n], start=(jv == 0), stop=(jv == nk - 1)),
                                         [Vh.b, PT.b[sl]], [bO.b])
                                    who_ = jv % 2
                                    ab_ = accS.b[qp * 2 + who_]
                                    adder = P.dve if who_ == 0 else P.pool
                                    if jv < 2:
                                        adder(lambda e, n=n, sl=sl, qp=qp, who_=who_: e.tensor_copy(out=accS.t[:, qp, who_, 0:n], in_=PT.t[:, sl, 0:n]), [PT.b[sl]], [ab_])
                                    else:
                                        adder(lambda e, n=n, sl=sl, qp=qp, who_=who_: e.tensor_tensor(out=accS.t[:, qp, who_, 0:n], in0=accS.t[:, qp, who_, 0:n], in1=PT.t[:, sl, 0:n], op=ALU.add),
                                              [PT.b[sl], ab_], [ab_])
                            bS = pb()
                            P.pe(lambda e, bS=bS, n=n, qp=qp: e.matmul(bS.t[:, 0:n], lhsT=ones32.t[:, :], rhs=accS.t[:, qp, 0, 0:n], start=True, stop=False), [ones32.b, accS.b[qp * 2]], [bS.b])
                            P.pe(lambda e, bS=bS, n=n, qp=qp: e.matmul(bS.t[:, 0:n], lhsT=ones32.t[:, :], rhs=accS.t[:, qp, 1, 0:n], start=False, stop=True), [ones32.b, accS.b[qp * 2 + 1]], [bS.b])
                            P.dve(lambda e, bS=bS, n=n: e.reciprocal(out=rs.t[:, 0:n], in_=bS.t[:, 0:n]), [bS.b], [rs.b])
                            P.dve(lambda e, bO=bO, n=n, qp=qp: e.tensor_tensor(out=oc.t[:, qp, 0:n], in0=bO.t[:, 0:n], in1=rs.t[:, 0:n], op=ALU.mult), [bO.b, rs.b], [oc.b[qp]])
                            P.st(lambda e, tg=tg, h=h, n=n, qp=qp: e.dma_start(out=ocT_d[tg][:, h * 512:h * 512 + n], in_=oc.t[:, qp, 0:n]), [oc.b[qp]], [ocT_db[tg]])
                    reserved.clear()
                s14.__exit__(None, None, None)
                if dbg == 4:
                    out_ops.append(P.ld(lambda e: e.dma_start(out=out[1], in_=xin[3]), hT_db + of_db + ybT_db + ocT_db, []))
                    return True

                with Scope(AR) as s5:
                    wout = s5.sb("wout", [128, 8, 1024], BF16)
                    cw = s5.sb("cw", [128, 24], F32)
                    P.wld(lambda e: e.dma_start(out=wout.t[:], in_=w_out[l].rearrange("(c p) n -> p c n", p=128)), [], [wout.b])
                    P.ld(lambda e: e.dma_start(out=cw.t[:], in_=conv_wT[l]), [], [cw.b])
                    hg = s5.sb("hg", [128, 8, 768], BF16, nb=6)
                    ya = s5.sb("ya", [128, 8, 512], BF16)
                    yb5 = s5.sb("yb5", [128, 8, 512], BF16, nb=4)
                    oc5 = s5.sb("oc5", [128, 8, 512], BF16)
                    yc = s5.sb("yc", [128, 8, 512], BF16)
                    mm_ = s5.sb("mm_", [128, 8, 512], BF16)
                    wa4 = s5.sb("wa4", [128, 2, 4, 8, 128], BF16, nb=2)
                    wcz = s5.sb("wcz", [128, 2, 8, 128], BF16, nb=2)
                    wbg = s5.sb("wbg", [128, 2, 6, 8, 128], BF16, nb=2)
                    pext = s5.sb("pext", [128, 2, 514], F32, nb=2)
                    av = s5.sb("av", [128, 2, 512], F32, nb=2)
                    avh = s5.sb("avh", [128, 2, 2], F32, nb=2)
                    szz = s5.sb("szz", [128, 2, 512], F32, nb=2)
                    c1 = s5.sb("c1", [128, 2, 512], F32, nb=2)
                    c2 = s5.sb("c2", [128, 2, 512], F32, nb=2)
                    sig = s5.sb("sig", [128, 2, 512], F32, nb=2)
                    acc = s5.sb("acc", [128, 2, 512], F32, nb=2)
                    tt = s5.sb("tt", [128, 2, 512], F32, nb=2)
                    xt5 = s5.sb("xt5", [128, 2, 1024], F32, nb=2)
                    xo = s5.sb("xo", [128, 2, 1024], F32, nb=2)
                    fctr = hctr = octr = xctr = 0
                    tgs = ([] if last else [(0, [0, 1])]) + [(1 + q, list(range(2 + 4 * q, 6 + 4 * q))) for q in range(8)]
                    a_offs = (O_AV, O_AB, O_AC, O_AZ)
                    g_offs = (O_GA, O_GB, O_GC)
                    for (tg, tl_) in tgs:
                        n = 128 * len(tl_)
                        who = 0 if tg == 0 else 1
                        for jj, i in enumerate(tl_):
                            P.ld(lambda e, i=i, jj=jj: e.dma_start(out=hg.t[:, :, 128 + jj * 128:256 + jj * 128], in_=hT_d[i].rearrange("p (c n) -> p c n", c=8)), [hT_db[i]], [hg.b[1 + jj]])
                        has_l = tl_[0] not in (0, NCTX)
                        has_r = tl_[-1] not in (NCTX - 1, NT - 1)
                        if has_l:
                            P.ld(lambda e, i=tl_[0] - 1: e.dma_start(out=hg.t[:, :, 0:128], in_=hT_d[i].rearrange("p (c n) -> p c n", c=8)), [hT_db[tl_[0] - 1]], [hg.b[0]])
                        if has_r:
                            P.ld(lambda e, i=tl_[-1] + 1, n=n: e.dma_start(out=hg.t[:, :, 128 + n:256 + n], in_=hT_d[i].rearrange("p (c n) -> p c n", c=8)), [hT_db[tl_[-1] + 1]], [hg.b[1 + len(tl_)]])
                        for jj, i in enumerate(tl_):
                            P.ld(lambda e, i=i, jj=jj: e.dma_start(out=yb5.t[:, :, jj * 128:(jj + 1) * 128], in_=ybT_d[i].rearrange("p (c n) -> p c n", c=8)), [ybT_db[i]], [yb5.b[jj]])
                        P.ld(lambda e, tg=tg, n=n: e.dma_start(out=oc5.t[:, :, 0:n], in_=ocT_d[tg].rearrange("p (c n) -> p c n", c=8)[:, :, 0:n]), [ocT_db[tg]], [oc5.b])
                        hmain = lambda k, n=n: hg.t[:, k, 128:128 + n]
                        for f in range(8):
                            fp = fctr % 2
                            fctr += 1
                            P.ld(lambda e, f=f, fp=fp: e.dma_start(out=wa4.t[:, fp], in_=WA_d[l % 2][f]), wa_db[l % 2], [wa4.b[fp]])
                            bC, bV, bH = pb(), pb(), pb()
                            for (bk, w_) in ((bC, 2), (bV, 0)):
                                for k in range(8):
                                    P.pe(lambda e, bk=bk, w_=w_, k=k, n=n, fp=fp: e.matmul(bk.t[:, 0:n], lhsT=wa4.t[:, fp, w_, k, :], rhs=hg.t[:, k, 128:128 + n], start=(k == 0), stop=(k == 7)),
                                         [wa4.b[fp]] + hg.b, [bk.b])
                            for (c0, w_) in ((0, 2), (2, 0)):
                                for k in range(8):
                                    P.pe(lambda e, c0=c0, w_=w_, k=k, n=n, fp=fp, bH=bH: e.matmul(bH.t[:, c0:c0 + 2], lhsT=wa4.t[:, fp, w_, k, :], rhs=hg.t[:, k, 127:127 + n + 2:n + 1],
                                                                                                 start=(k == 0), stop=(k == 7)), [wa4.b[fp]] + hg.b, [bH.b])
                            P.act(lambda e, fp=fp, bV=bV, n=n: e.activation(out=av.t[:, fp, 0:n], in_=bV.t[:, 0:n], func=AF.Copy), [bV.b], [av.b[fp]])
                            P.act(lambda e, fp=fp, bH=bH: e.activation(out=avh.t[:, fp, 0:2], in_=bH.t[:, 2:4], func=AF.Copy), [bH.b], [avh.b[fp]])
                            P.dve(lambda e, fp=fp, bC=bC, n=n: e.tensor_tensor(out=pext.t[:, fp, 1:1 + n], in0=bC.t[:, 0:n], in1=av.t[:, fp, 0:n], op=ALU.mult), [bC.b, av.b[fp]], [pext.b[fp]])
                            P.dve(lambda e, fp=fp, bH=bH, n=n: e.tensor_tensor(out=pext.t[:, fp, 0:n + 2:n + 1], in0=bH.t[:, 0:2], in1=avh.t[:, fp, 0:2], op=ALU.mult), [bH.b, avh.b[fp]], [pext.b[fp]])
                            if not has_l:
                                P.pool(lambda e, fp=fp: e.memset(pext.t[:, fp, 0:1], 0.0), [], [pext.b[fp]])
                            if not has_r:
                                P.pool(lambda e, fp=fp, n=n: e.memset(pext.t[:, fp, n + 1:n + 2], 0.0), [], [pext.b[fp]])
                            bB, bZ = pb(), pb()
                            for (bk, w_) in ((bB, 1), (bZ, 3)):
                                for k in range(8):
                                    P.pe(lambda e, bk=bk, w_=w_, k=k, n=n, fp=fp: e.matmul(bk.t[:, 0:n], lhsT=wa4.t[:, fp, w_, k, :], rhs=hg.t[:, k, 128:128 + n], start=(k == 0), stop=(k == 7)),
                                         [wa4.b[fp]] + hg.b, [bk.b])
                            P.act(lambda e, fp=fp, bZ=bZ, n=n: e.activation(out=szz.t[:, fp, 0:n], in_=bZ.t[:, 0:n], func=AF.Silu), [bZ.b], [szz.b[fp]])
                            P.dve(lambda e, fp=fp, n=n, f=f: e.tensor_scalar(out=c1.t[:, fp, 0:n], in0=pext.t[:, fp, 1:1 + n], scalar1=cw.t[:, f * 3 + 1:f * 3 + 2], scalar2=None, op0=ALU.mult),
                                  [pext.b[fp], cw.b], [c1.b[fp]])
                            P.dve(lambda e, fp=fp, n=n, f=f: e.scalar_tensor_tensor(out=c1.t[:, fp, 0:n], in0=pext.t[:, fp, 0:n], scalar=cw.t[:, f * 3:f * 3 + 1], in1=c1.t[:, fp, 0:n], op0=ALU.mult, op1=ALU.add),
                                  [pext.b[fp], cw.b, c1.b[fp]], [c1.b[fp]])
                            P.dve(lambda e, fp=fp, n=n, f=f: e.scalar_tensor_tensor(out=c1.t[:, fp, 0:n], in0=pext.t[:, fp, 2:2 + n], scalar=cw.t[:, f * 3 + 2:f * 3 + 3], in1=c1.t[:, fp, 0:n], op0=ALU.mult, op1=ALU.add),
                                  [pext.b[fp], cw.b, c1.b[fp]], [c1.b[fp]])
                            P.dve(lambda e, fp=fp, bB=bB, n=n: e.tensor_tensor(out=c2.t[:, fp, 0:n], in0=bB.t[:, 0:n], in1=c1.t[:, fp, 0:n], op=ALU.mult), [bB.b, c1.b[fp]], [c2.b[fp]])
                            P.pool(lambda e, fp=fp, n=n, f=f: e.tensor_tensor(out=ya.t[:, f, 0:n], in0=c2.t[:, fp, 0:n], in1=szz.t[:, fp, 0:n], op=ALU.mult), [c2.b[fp], szz.b[fp]], [ya.b])
                        for h in range(8):
                            hp = hctr % 2
                            hctr += 1
                            P.ld(lambda e, h=h, hp=hp: e.dma_start(out=wcz.t[:, hp], in_=WC_d[l % 2][h]), wc_db[l % 2], [wcz.b[hp]])
                            bz = pb()
                            for k in range(8):
                                P.pe(lambda e, bz=bz, k=k, n=n, hp=hp: e.matmul(bz.t[:, 0:n], lhsT=wcz.t[:, hp, k, :], rhs=hg.t[:, k, 128:128 + n], start=(k == 0), stop=(k == 7)),
                                     [wcz.b[hp]] + hg.b, [bz.b])
                            P.act(lambda e, hp=hp, bz=bz, n=n: e.activation(out=szz.t[:, hp, 0:n], in_=bz.t[:, 0:n], func=AF.Silu), [bz.b], [szz.b[hp]])
                            P.pool(lambda e, hp=hp, n=n, h=h: e.tensor_tensor(out=yc.t[:, h, 0:n], in0=oc5.t[:, h, 0:n], in1=szz.t[:, hp, 0:n], op=ALU.mult), [oc5.b, szz.b[hp]], [yc.b])
                        for o in range(8):
                            op_ = octr % 2
                            octr += 1
                            P.ld(lambda e, o=o, op_=op_: e.dma_start(out=wbg.t[:, op_], in_=WG_d[l % 2][o]), wg_db[l % 2], [wbg.b[op_]])
                            for br in range(3):
                                src = (ya, yb5, yc)[br]
                                bg, bb_ = pb(), pb()
                                for k in range(8):
                                    P.pe(lambda e, bg=bg, k=k, n=n, op_=op_, br=br: e.matmul(bg.t[:, 0:n], lhsT=wbg.t[:, op_, 3 + br, k, :], rhs=hg.t[:, k, 128:128 + n], start=(k == 0), stop=(k == 7)),
                                         [wbg.b[op_]] + hg.b, [bg.b])
                                P.act(lambda e, op_=op_, bg=bg, n=n, br=br: e.activation(out=sig.t[:, br % 2, 0:n], in_=bg.t[:, 0:n], func=AF.Sigmoid), [bg.b], [sig.b[br % 2]])
                                for k in range(8):
                                    P.pe(lambda e, bb_=bb_, k=k, n=n, op_=op_, br=br, src=src: e.matmul(bb_.t[:, 0:n], lhsT=wbg.t[:, op_, br, k, :], rhs=src.t[:, k, 0:n], start=(k == 0), stop=(k == 7)),
                                         [wbg.b[op_]] + (src.b if isinstance(src.b, list) else [src.b]), [bb_.b])
                                if br == 0:
                                    P.dve(lambda e, op_=op_, bb_=bb_, n=n, br=br: e.tensor_tensor(out=acc.t[:, op_, 0:n], in0=bb_.t[:, 0:n], in1=sig.t[:, br % 2, 0:n], op=ALU.mult), [bb_.b, sig.b[br % 2]], [acc.b[op_]])
                                else:
                                    P.dve(lambda e, op_=op_, bb_=bb_, n=n, br=br: e.tensor_tensor(out=tt.t[:, op_, 0:n], in0=bb_.t[:, 0:n], in1=sig.t[:, br % 2, 0:n], op=ALU.mult), [bb_.b, sig.b[br % 2]], [tt.b[op_]])
                                    if br == 1:
                                        P.pool(lambda e, op_=op_, n=n: e.tensor_tensor(out=acc.t[:, op_, 0:n], in0=acc.t[:, op_, 0:n], in1=tt.t[:, op_, 0:n], op=ALU.add), [acc.b[op_], tt.b[op_]], [acc.b[op_]])
                                    else:
                                        P.pool(lambda e, op_=op_, n=n, o=o: e.tensor_tensor(out=mm_.t[:, o, 0:n], in0=acc.t[:, op_, 0:n], in1=tt.t[:, op_, 0:n], op=ALU.add), [acc.b[op_], tt.b[op_]], [mm_.b])
                        for jj, i in enumerate(tl_):
                            xp = xctr % 2
                            xctr += 1
                            P.ld(lambda e, i=i, xp=xp: e.dma_start(out=xt5.t[:, xp], in_=x_src[i]), [xs_b[i]] if l > 0 else [], [xt5.b[xp]])
                            for half in range(2):
                                bo = pb()
                                hs = slice(half * 512, (half + 1) * 512)
                                for o in range(8):
                                    P.pe(lambda e, bo=bo, o=o, jj=jj, hs=hs: e.matmul(bo.t[:], lhsT=mm_.t[:, o, jj * 128:(jj + 1) * 128], rhs=wout.t[:, o, hs], start=(o == 0), stop=(o == 7)),
                                         [mm_.b, wout.b], [bo.b])
                                P.dve(lambda e, bo=bo, hs=hs, xp=xp, who=who: e.tensor_tensor(out=xo.t[:, xp, hs], in0=bo.t[:], in1=G.t[:, who, hs], op=ALU.mult), [bo.b, G.b], [xo.b[xp]])
                                P.pool(lambda e, hs=hs, xp=xp: e.tensor_tensor(out=xo.t[:, xp, hs], in0=xo.t[:, xp, hs], in1=xt5.t[:, xp, hs], op=ALU.add), [xo.b[xp], xt5.b[xp]], [xo.b[xp]])
                            if l == L - 1 and i >= NCTX:
                                out_ops.append(P.st(lambda e, i=i, xp=xp: e.dma_start(out=out[i - NCTX], in_=xo.t[:, xp]), [xo.b[xp]], []))
                            else:
                                P.st(lambda e, i=i, xp=xp: e.dma_start(out=xs[i], in_=xo.t[:, xp]), [xo.b[xp]], [xs_b[i]])
            return False

        for l_ in range(L):
            if emit_layer(l_):
                break
        if dbg in (1, 3, 4):
            dummy = P.ld(lambda e: e.dma_start(out=out[0], in_=xin[2]), [], [])
            out_ops.append(dummy)
        P.emit(out_ops)
    return nc


_ROPE_PERM = np.concatenate([np.arange(16, 32), np.arange(0, 16), np.arange(48, 64), np.arange(32, 48)])
_ROPE_SIGN = np.concatenate([-np.ones(16), np.ones(16), -np.ones(16), np.ones(16)]).astype(np.float32)


def _rope_tables():
    rows = 4096 // 64
    row = np.repeat(np.arange(rows, dtype=np.float32), 64)
    col = np.tile(np.arange(64, dtype=np.float32), rows)
    n_freq = 16
    freqs = (np.float32(10000.0) ** (-np.arange(n_freq, dtype=np.float32) / n_freq)).astype(np.float32)
    ang_r = row[:, None] * freqs[None, :]
    ang_c = col[:, None] * freqs[None, :]
    ang = np.concatenate([ang_r, ang_r, ang_c, ang_c], axis=-1)
    cos = np.ones((T, 64), np.float32)
    sin = np.zeros((T, 64), np.float32)
    cos[NCTX * 128:] = np.cos(ang)
    sin[NCTX * 128:] = np.sin(ang) * _ROPE_SIGN[None, :]
    return np.ascontiguousarray(cos.T), np.ascontiguousarray(sin.T)


def _consts():
    j = np.arange(128)[:, None]
    i = np.arange(128)[None, :]
    s = np.float32(-1.0 / 16.0)
    tri = np.stack([(j <= i), (j > i), (j >= i), (j < i)]).astype(np.float32) * s
    mask = np.stack([(j <= i), (j >= i)]).astype(np.float32)
    cosT, sinT = _rope_tables()
    return dict(ident=np.eye(128, dtype=np.float32), tri=tri, mask=mask, cosT=cosT, sinST=sinT)


def prep_inputs(inp, LW=DEPTH):
    f = lambda a: np.ascontiguousarray(np.asarray(a, dtype=np.float32))
    w_in = f(inp["w_in"])
    shared = dict(
        w_mod=f(inp["w_mod"]), b_mod=f(inp["b_mod"]), norm_g=f(inp["norm_g"]), w_in=w_in,
        w_krp=np.ascontiguousarray(w_in[:, :, O_CKR:O_CKR + 64][:, :, _ROPE_PERM]),
        conv_wT=np.ascontiguousarray(f(inp["conv_w"]).reshape(DEPTH, 3, 8, 128).transpose(0, 3, 2, 1).reshape(DEPTH, 128, 24)),
        wa_f=np.ascontiguousarray(np.concatenate([f(inp["gla_wa_up_f"]), f(inp["gla_ba_f"])[:, None, :]], axis=1)),
        wa_b=np.ascontiguousarray(np.concatenate([f(inp["gla_wa_up_b"]), f(inp["gla_ba_b"])[:, None, :]], axis=1)),
        gla_norm_g=f(inp["gla_norm_g"]), mla_q_norm_g=f(inp["mla_q_norm_g"]), mla_kv_norm_g=f(inp["mla_kv_norm_g"]),
        wkv=f(inp["mla_wkv_up"]),
        w_br_a=f(inp["w_br_a"]), w_br_b=f(inp["w_br_b"]), w_br_c=f(inp["w_br_c"]), w_out=f(inp["w_out"]),
    )
    wq = f(inp["mla_wq_up"]).reshape(DEPTH, 384, 8, 192)
    shared["wq_aug"] = np.ascontiguousarray(np.concatenate([wq, wq[..., 128:192][..., _ROPE_PERM]], axis=-1).reshape(DEPTH, 384, 2048))
    qg, kg = f(inp["mla_qn_g"]), f(inp["mla_kn_g"])
    gv = np.zeros((DEPTH, 128, 8), np.float32)
    gv[:, :, 0] = qg[:, 0:128]
    gv[:, :, 1] = kg[:, 0:128]
    gv[:, 0:64, 2] = qg[:, 128:192]
    gv[:, 0:64, 3] = qg[:, 128:192][:, _ROPE_PERM]
    gv[:, 0:64, 4] = kg[:, 128:192]
    gv[:, 0:64, 5] = kg[:, 128:192][:, _ROPE_PERM]
    shared["gvec"] = gv
    shared = {k: np.ascontiguousarray(v[:LW]) for k, v in shared.items()}
    shared.update(_consts())
    x, ctx, c, c_ctx = f(inp["x"]), f(inp["ctx"]), f(inp["c"]), f(inp["c_ctx"])
    consts = _consts()
    idle = {k: np.zeros_like(v) for k, v in shared.items()}
    idle.update(consts)
    idle["xin"] = np.zeros((NT, 128, D), np.float32)
    idle["cvec"] = np.zeros((128, 16), np.float32)
    maps = []
    for core in range(8):
        if core not in REAL_CORES:
            maps.append(idle)
            continue
        b = REAL_CORES.index(core)
        m = dict(shared)
        m["xin"] = np.ascontiguousarray(np.concatenate([ctx[b], x[b]], axis=0).reshape(NT, 128, D))
        cv = np.zeros((128, 16), np.float32)
        cv[:, 0:8] = c[b].reshape(8, 128).T
        cv[:, 8:16] = c_ctx.reshape(8, 128).T
        m["cvec"] = cv
        maps.append(m)
    return maps


REAL_CORES = [0, 1, 4, 5]
_NC_CACHE = {}


def kernel(**inputs):
    if "nc" not in _NC_CACHE:
        _NC_CACHE["nc"] = build_nc()
    nc = _NC_CACHE["nc"]
    maps = prep_inputs(inputs)
    res = run_bass_kernel_spmd(nc, maps, core_ids=list(range(8)))
    outs = [np.asarray(res.results[REAL_CORES[b]]["out"], dtype=np.float32).reshape(NXT * 128, D) for b in range(4)]
    return np.stack(outs, axis=0)
```
